# Optimizing a Trainium2 kernel written in Bass

```python
import math
import jax, jax.numpy as jnp
from jax import lax
import numpy as np

D_MODEL = 1024
BATCH = 4
SEQ = 4096
DEPTH = 2
DEC_BATCH = 32
DEC_SEQ = 8
PAST_LEN = 8192
PAGE_SIZE = 128

CONV_C = D_MODEL // 4
CONV_W = 31
NSA_H = 8
NSA_DH = D_MODEL // 16
NSA_G = 2
NSA_HPG = NSA_H // NSA_G
CMP_STRIDE = 16
CMP_LEN = 2 * CMP_STRIDE
CMP_HID = 2 * NSA_DH
SEL_BLK = 64
SEL_TOPN = 16
WINDOW = 512
FORCE_SCORE = 1e9
DIFF_H = 4
DIFF_DV = D_MODEL // 16
DIFF_DQK = DIFF_DV // 2
D_FF = -(-8 * D_MODEL // (3 * 256)) * 256
ROPE_THETA = 10000.0
QBLK = 128
LN_EPS = 1e-5
DN_ALPHA = (2 * DEPTH) ** 0.25
DN_BETA = (8 * DEPTH) ** -0.25

IN_SPLITS = (CONV_C, CONV_C, NSA_H * NSA_DH,
             NSA_G * NSA_DH, NSA_G * NSA_DH, NSA_G * NSA_DH, NSA_G * NSA_DH, NSA_G * NSA_DH, NSA_G * NSA_DH,
             3 * NSA_H, DIFF_H * 2 * DIFF_DQK, DIFF_H * 2 * DIFF_DQK, DIFF_H * DIFF_DV)
IN_VALUE_LIKE = (True, True, False, False, True, False, True, False, True, False, False, False, True)
N_IN = sum(IN_SPLITS)
MIX_W = CONV_C + NSA_H * NSA_DH + DIFF_H * DIFF_DV

kernel_name = "hymba_conv_nsa_diff_decoder_step"


def _split_points():
    return [int(v) for v in np.cumsum(IN_SPLITS)[:-1]]


def _in_col_scale():
    cols = [np.full((n,), DN_BETA if v else 1.0, np.float32) for n, v in zip(IN_SPLITS, IN_VALUE_LIKE)]
    return np.concatenate(cols) * np.float32(D_MODEL ** -0.5)


def layer_norm(x, g, b):
    xf = x.astype(jnp.float32)
    mu = jnp.mean(xf, -1, keepdims=True)
    var = jnp.mean(jnp.square(xf - mu), -1, keepdims=True)
    return ((xf - mu) * lax.rsqrt(var + LN_EPS) * g.astype(jnp.float32) + b.astype(jnp.float32)).astype(x.dtype)


def rms_norm(x, g):
    xf = x.astype(jnp.float32)
    return (xf * lax.rsqrt(jnp.mean(jnp.square(xf), -1, keepdims=True) + LN_EPS) * g.astype(jnp.float32)).astype(x.dtype)


def rope(x, pos):
    half = x.shape[-1] // 2
    inv = ROPE_THETA ** (-jnp.arange(half, dtype=jnp.float32) / half)
    ang = pos.astype(jnp.float32)[:, None] * inv[None, :]
    cos = jnp.cos(ang)[None, :, None, :]
    sin = jnp.sin(ang)[None, :, None, :]
    x1 = x[..., :half].astype(jnp.float32)
    x2 = x[..., half:].astype(jnp.float32)
    return jnp.concatenate([x1 * cos - x2 * sin, x2 * cos + x1 * sin], -1).astype(x.dtype)


def masked_softmax(s, mask):
    s = jnp.where(mask, s.astype(jnp.float32), -jnp.inf)
    m = jnp.max(s, -1, keepdims=True)
    m = jnp.where(jnp.isfinite(m), m, 0.0)
    p = jnp.exp(s - m)
    return p / jnp.maximum(jnp.sum(p, -1, keepdims=True), 1e-30)


def sweep_queries(fn, q_args, q_pos):
    T = q_pos.shape[0]
    blk = QBLK if T % QBLK == 0 else T
    n = T // blk
    xs = tuple(jnp.moveaxis(a.reshape(a.shape[0], n, blk, *a.shape[2:]), 1, 0) for a in q_args)
    out = lax.map(lambda z: fn(*z[0], z[1]), (xs, q_pos.reshape(n, blk)))
    out = jnp.moveaxis(out, 0, 1)
    return out.reshape(out.shape[0], T, *out.shape[3:])


def gather_pages(cache_l, page_table):
    g = cache_l[page_table]
    return g.reshape(page_table.shape[0], page_table.shape[1] * cache_l.shape[1], *cache_l.shape[2:])


def conv_mixer(a, g, buf, dw_w, dw_b, ln_g, ln_b):
    u = a * jax.nn.sigmoid(g)
    ext = jnp.concatenate([buf.astype(u.dtype), u], axis=1)
    y = lax.conv_general_dilated(ext, dw_w[:, None, :].astype(u.dtype), window_strides=(1,), padding='VALID',
                                 dimension_numbers=('NWC', 'WIO', 'NWC'), feature_group_count=CONV_C) + dw_b
    y = jax.nn.silu(layer_norm(y, ln_g, ln_b))
    return y, ext[:, -(CONV_W - 1):]


def nsa_compress(k, pe, w1, b1, w2):
    B, L, G, D = k.shape
    n16 = L // CMP_STRIDE
    k16 = k[:, :n16 * CMP_STRIDE].reshape(B, n16, CMP_STRIDE, G, D)
    blk = jnp.concatenate([k16[:, :-1], k16[:, 1:]], axis=2) + pe[None, None, :, None, :]
    blk = blk.transpose(0, 1, 3, 2, 4).reshape(B, n16 - 1, G, CMP_LEN * D)
    return jax.nn.gelu(blk @ w1 + b1) @ w2


def nsa_cmp_sel(qg, qg_r, kc, vc, ks, vs, q_pos, w):
    B, T, G, HPG, D = qg.shape
    L = kc.shape[1]
    scale = D ** -0.5
    kcc = nsa_compress(kc, w['cmp_pe_k'], w['cmp_w1_k'], w['cmp_b1_k'], w['cmp_w2_k'])
    vcc = nsa_compress(vc, w['cmp_pe_v'], w['cmp_w1_v'], w['cmp_b1_v'], w['cmp_w2_v'])
    nc = kcc.shape[1]
    c_end = CMP_STRIDE * jnp.arange(nc, dtype=jnp.int32) + CMP_LEN - 1
    s = jnp.einsum('btghd,bcgd->btghc', qg, kcc) * scale
    pc = masked_softmax(s, (c_end[None, :] <= q_pos[:, None])[None, :, None, None, :])
    o_cmp = jnp.einsum('btghc,bcgd->btghd', pc.astype(vcc.dtype), vcc)
    imp = jnp.sum(pc, axis=3)
    n_sel = -(-L // SEL_BLK)
    r = SEL_BLK // CMP_STRIDE
    p_pad = jnp.pad(imp, ((0, 0), (0, 0), (0, 0), (1, max(0, r * n_sel - nc))))
    idx_np = r * np.arange(n_sel)[:, None] + np.arange(r + 1)[None, :]
    imp_sel = jnp.sum(jnp.take(p_pad, idx_np, axis=-1), -1)
    j = jnp.arange(n_sel, dtype=jnp.int32)[None, :]
    cur = (q_pos // SEL_BLK)[:, None]
    forced = (j == 0) | (j == cur) | (j == cur - 1)
    score = jnp.where(forced[None, :, None, :], FORCE_SCORE, imp_sel)
    score = jnp.where((j <= cur)[None, :, None, :], score, -jnp.inf)
    _, sel_idx = lax.top_k(score, min(SEL_TOPN, n_sel))
    pad = n_sel * SEL_BLK - L
    ksb = jnp.pad(ks, ((0, 0), (0, pad), (0, 0), (0, 0))).reshape(B, n_sel, SEL_BLK, G, D).transpose(0, 3, 1, 2, 4)
    vsb = jnp.pad(vs, ((0, 0), (0, pad), (0, 0), (0, 0))).reshape(B, n_sel, SEL_BLK, G, D).transpose(0, 3, 1, 2, 4)
    b_ix = jnp.arange(B)[:, None, None, None]
    g_ix = jnp.arange(G)[None, None, :, None]

    def sel_block(qb, ib, tb):
        kg = ksb[b_ix, g_ix, ib]
        vg = vsb[b_ix, g_ix, ib]
        kpos = ib[..., None] * SEL_BLK + jnp.arange(SEL_BLK, dtype=jnp.int32)
        sb = jnp.einsum('bcghd,bcgkpd->bcghkp', qb, kg) * scale
        mask = (kpos <= tb[None, :, None, None, None])[:, :, :, None]
        pb = masked_softmax(sb.reshape(*sb.shape[:4], -1), mask.reshape(*mask.shape[:4], -1)).reshape(sb.shape)
        return jnp.einsum('bcghkp,bcgkpd->bcghd', pb.astype(vg.dtype), vg)

    o_sel = sweep_queries(sel_block, (qg_r, sel_idx), q_pos)
    return o_cmp, o_sel


def window_banded(q, k, v):
    B, T, G, HPG, D = q.shape
    nb = T // QBLK
    nw = WINDOW // QBLK
    kp = jnp.pad(k, ((0, 0), (WINDOW, 0), (0, 0), (0, 0))).reshape(B, nb + nw, QBLK, G, D)
    vp = jnp.pad(v, ((0, 0), (WINDOW, 0), (0, 0), (0, 0))).reshape(B, nb + nw, QBLK, G, D)
    kband = jnp.concatenate([kp[:, o:o + nb] for o in range(nw + 1)], axis=2)
    vband = jnp.concatenate([vp[:, o:o + nb] for o in range(nw + 1)], axis=2)
    qb = q.reshape(B, nb, QBLK, G, HPG, D)
    s = jnp.einsum('bnqghd,bnkgd->bnqghk', qb, kband) * (D ** -0.5)
    start = jnp.arange(nb, dtype=jnp.int32)[:, None] * QBLK
    qpos = start + jnp.arange(QBLK, dtype=jnp.int32)[None, :]
    kpos = start - WINDOW + jnp.arange(WINDOW + QBLK, dtype=jnp.int32)[None, :]
    dlt = qpos[:, :, None] - kpos[:, None, :]
    mask = (dlt >= 0) & (dlt < WINDOW) & (kpos[:, None, :] >= 0)
    p = masked_softmax(s, mask[None, :, :, None, None, :])
    o = jnp.einsum('bnqghk,bnkgd->bnqghd', p.astype(vband.dtype), vband)
    return o.reshape(B, T, G, HPG, D)


def window_dense(q, k, v, q_pos, k_pos):
    s = jnp.einsum('btghd,bsgd->btghs', q, k) * (q.shape[-1] ** -0.5)
    dlt = q_pos[:, None] - k_pos[None, :]
    p = masked_softmax(s, ((dlt >= 0) & (dlt < WINDOW))[None, :, None, None, :])
    return jnp.einsum('btghs,bsgd->btghd', p.astype(v.dtype), v)


def diff_mixer(q, k, v, q_pos, k_pos, lam, sub_g, lam_init):
    scale = DIFF_DQK ** -0.5

    def block(qb, tb):
        s = jnp.einsum('bqhid,bkhid->bhiqk', qb, k) * scale
        p = masked_softmax(s, k_pos[None, :] <= tb[:, None])
        a = p[:, :, 0] - lam * p[:, :, 1]
        return jnp.einsum('bhqk,bkhd->bqhd', a.astype(v.dtype), v)

    o = sweep_queries(block, (q,), q_pos)
    return rms_norm(o, sub_g) * (1.0 - lam_init)


def layer(x, pos, past, w, lam_init, past_len):
    B, T, _ = x.shape
    z = x @ w['w_in']
    ca, cg, nq, ck, cv, sk, sv, wk, wv, gt, dq, dk, dv = jnp.split(z, _split_points(), axis=-1)

    def cat(a, b):
        return jnp.concatenate([a.astype(b.dtype), b], axis=1)

    conv_buf = jnp.zeros((B, CONV_W - 1, CONV_C), x.dtype) if past is None else past['conv']
    y_conv, new_conv = conv_mixer(ca, cg, conv_buf, w['conv_dw_w'], w['conv_dw_b'], w['conv_ln_g'], w['conv_ln_b'])

    q = nq.reshape(B, T, NSA_H, NSA_DH)
    qg = q.reshape(B, T, NSA_G, NSA_HPG, NSA_DH)
    qg_r = rope(q, pos).reshape(B, T, NSA_G, NSA_HPG, NSA_DH)
    ck = ck.reshape(B, T, NSA_G, NSA_DH)
    cv = cv.reshape(B, T, NSA_G, NSA_DH)
    sk = rope(sk.reshape(B, T, NSA_G, NSA_DH), pos)
    sv = sv.reshape(B, T, NSA_G, NSA_DH)
    wk = rope(wk.reshape(B, T, NSA_G, NSA_DH), pos)
    wv = wv.reshape(B, T, NSA_G, NSA_DH)
    if past is None:
        kc_all, vc_all, ks_all, vs_all = ck, cv, sk, sv
    else:
        kc_all, vc_all = cat(past['cmp_k'], ck), cat(past['cmp_v'], cv)
        ks_all, vs_all = cat(past['sel_k'], sk), cat(past['sel_v'], sv)
    o_cmp, o_sel = nsa_cmp_sel(qg, qg_r, kc_all, vc_all, ks_all, vs_all, pos, w)
    if past is None:
        o_win = window_banded(qg_r, wk, wv)
        n_keep = min(WINDOW, T)
        new_wk, new_wv = wk[:, T - n_keep:], wv[:, T - n_keep:]
    else:
        wb = past['win_k'].shape[1]
        wk_all, wv_all = cat(past['win_k'], wk), cat(past['win_v'], wv)
        k_pos = jnp.concatenate([past_len - wb + jnp.arange(wb, dtype=jnp.int32), pos])
        o_win = window_dense(qg_r, wk_all, wv_all, pos, k_pos)
        new_wk, new_wv = wk_all[:, -wb:], wv_all[:, -wb:]
    gate = jax.nn.sigmoid(gt).reshape(B, T, NSA_G, NSA_HPG, 3)
    o_nsa = (gate[..., 0:1] * o_cmp + gate[..., 1:2] * o_sel + gate[..., 2:3] * o_win).reshape(B, T, NSA_H * NSA_DH)

    dq = rope(dq.reshape(B, T, 2 * DIFF_H, DIFF_DQK), pos).reshape(B, T, DIFF_H, 2, DIFF_DQK)
    dk = rope(dk.reshape(B, T, 2 * DIFF_H, DIFF_DQK), pos).reshape(B, T, DIFF_H, 2 * DIFF_DQK)
    dv = dv.reshape(B, T, DIFF_H, DIFF_DV)
    if past is None:
        dk_all, dv_all, dk_pos = dk, dv, pos
    else:
        dk_all, dv_all = cat(past['diff_k'], dk), cat(past['diff_v'], dv)
        dk_pos = jnp.arange(past_len + T, dtype=jnp.int32)
    f32 = jnp.float32
    lam = (jnp.exp(jnp.sum(w['lq1'].astype(f32) * w['lk1'].astype(f32)))
           - jnp.exp(jnp.sum(w['lq2'].astype(f32) * w['lk2'].astype(f32))) + lam_init)
    o_diff = diff_mixer(dq, dk_all.reshape(B, -1, DIFF_H, 2, DIFF_DQK), dv_all, pos, dk_pos, lam,
                        w['diff_subln_g'], lam_init).reshape(B, T, DIFF_H * DIFF_DV)

    mix = jnp.concatenate([y_conv, o_nsa, o_diff], axis=-1) @ w['w_out']
    x = layer_norm(DN_ALPHA * x + mix, w['ln1_g'], w['ln1_b'])
    h = jax.nn.silu(x @ w['ffn_w1']) * (x @ w['ffn_w3'])
    x = layer_norm(DN_ALPHA * x + h @ w['ffn_w2'], w['ln2_g'], w['ln2_b'])
    return x, (ck, cv, sk, sv, dk, dv, new_wk, new_wv, new_conv)


def setup_inputs(seed: int = 0) -> dict:
    key = jax.random.key(seed)
    keys = iter(jax.random.split(key, 64))

    def nrm(shape, scale):
        return jax.random.normal(next(keys), shape, jnp.float32) * scale

    n_pages = PAST_LEN // PAGE_SIZE
    n_used = DEC_BATCH * n_pages
    n_pool = (5 * n_used + 3) // 4
    wb = min(WINDOW, PAST_LEN)
    page_table = jax.random.permutation(next(keys), n_pool)[:n_used].astype(jnp.int32).reshape(DEC_BATCH, n_pages)
    kv_nsa = (DEPTH, n_pool, PAGE_SIZE, NSA_G, NSA_DH)
    return {
        "x_prompt": nrm((BATCH, SEQ, D_MODEL), 1.0),
        "x_sample": nrm((DEC_BATCH, DEC_SEQ, D_MODEL), 1.0),
        "cache_nsa_cmp_k": nrm(kv_nsa, 1.0),
        "cache_nsa_cmp_v": nrm(kv_nsa, DN_BETA),
        "cache_nsa_sel_k": nrm(kv_nsa, 1.0),
        "cache_nsa_sel_v": nrm(kv_nsa, DN_BETA),
        "cache_diff_k": nrm((DEPTH, n_pool, PAGE_SIZE, DIFF_H, 2 * DIFF_DQK), 1.0),
        "cache_diff_v": nrm((DEPTH, n_pool, PAGE_SIZE, DIFF_H, DIFF_DV), DN_BETA),
        "state_nsa_win_k": nrm((DEPTH, DEC_BATCH, wb, NSA_G, NSA_DH), 1.0),
        "state_nsa_win_v": nrm((DEPTH, DEC_BATCH, wb, NSA_G, NSA_DH), DN_BETA),
        "state_conv": nrm((DEPTH, DEC_BATCH, CONV_W - 1, CONV_C), 0.5),
        "page_table": page_table,
        "w_in": nrm((DEPTH, D_MODEL, N_IN), 1.0) * jnp.asarray(_in_col_scale()),
        "conv_dw_w": nrm((DEPTH, CONV_W, CONV_C), CONV_W ** -0.5),
        "conv_dw_b": nrm((DEPTH, CONV_C), 0.02),
        "conv_ln_g": 1.0 + nrm((DEPTH, CONV_C), 0.02),
        "conv_ln_b": nrm((DEPTH, CONV_C), 0.02),
        "cmp_pe_k": nrm((DEPTH, CMP_LEN, NSA_DH), 0.02),
        "cmp_w1_k": nrm((DEPTH, CMP_LEN * NSA_DH, CMP_HID), (CMP_LEN * NSA_DH) ** -0.5),
        "cmp_b1_k": nrm((DEPTH, CMP_HID), 0.02),
        "cmp_w2_k": nrm((DEPTH, CMP_HID, NSA_DH), CMP_HID ** -0.5),
        "cmp_pe_v": nrm((DEPTH, CMP_LEN, NSA_DH), 0.02),
        "cmp_w1_v": nrm((DEPTH, CMP_LEN * NSA_DH, CMP_HID), (CMP_LEN * NSA_DH) ** -0.5),
        "cmp_b1_v": nrm((DEPTH, CMP_HID), 0.02),
        "cmp_w2_v": nrm((DEPTH, CMP_HID, NSA_DH), CMP_HID ** -0.5 * DN_BETA),
        "diff_lq1": nrm((DEPTH, DIFF_DQK), 0.1),
        "diff_lk1": nrm((DEPTH, DIFF_DQK), 0.1),
        "diff_lq2": nrm((DEPTH, DIFF_DQK), 0.1),
        "diff_lk2": nrm((DEPTH, DIFF_DQK), 0.1),
        "diff_subln_g": 1.0 + nrm((DEPTH, DIFF_DV), 0.02),
        "w_out": nrm((DEPTH, MIX_W, D_MODEL), MIX_W ** -0.5 * DN_BETA),
        "ln1_g": 1.0 + nrm((DEPTH, D_MODEL), 0.02),
        "ln1_b": nrm((DEPTH, D_MODEL), 0.02),
        "ln2_g": 1.0 + nrm((DEPTH, D_MODEL), 0.02),
        "ln2_b": nrm((DEPTH, D_MODEL), 0.02),
        "ffn_w1": nrm((DEPTH, D_MODEL, D_FF), D_MODEL ** -0.5 * DN_BETA),
        "ffn_w3": nrm((DEPTH, D_MODEL, D_FF), D_MODEL ** -0.5 * DN_BETA),
        "ffn_w2": nrm((DEPTH, D_FF, D_MODEL), D_FF ** -0.5 * DN_BETA),
    }


def reference(x_prompt, x_sample, cache_nsa_cmp_k, cache_nsa_cmp_v, cache_nsa_sel_k, cache_nsa_sel_v,
              cache_diff_k, cache_diff_v, state_nsa_win_k, state_nsa_win_v, state_conv, page_table,
              w_in, conv_dw_w, conv_dw_b, conv_ln_g, conv_ln_b,
              cmp_pe_k, cmp_w1_k, cmp_b1_k, cmp_w2_k, cmp_pe_v, cmp_w1_v, cmp_b1_v, cmp_w2_v,
              diff_lq1, diff_lk1, diff_lq2, diff_lk2, diff_subln_g, w_out,
              ln1_g, ln1_b, ln2_g, ln2_b, ffn_w1, ffn_w3, ffn_w2):
    past_len = page_table.shape[1] * PAGE_SIZE
    pos_p = jnp.arange(x_prompt.shape[1], dtype=jnp.int32)
    pos_s = past_len + jnp.arange(x_sample.shape[1], dtype=jnp.int32)
    xp, xs = x_prompt, x_sample
    outs_p, outs_s = [], []
    for l in range(DEPTH):
        w = dict(w_in=w_in[l], conv_dw_w=conv_dw_w[l], conv_dw_b=conv_dw_b[l], conv_ln_g=conv_ln_g[l],
                 conv_ln_b=conv_ln_b[l], cmp_pe_k=cmp_pe_k[l], cmp_w1_k=cmp_w1_k[l], cmp_b1_k=cmp_b1_k[l],
                 cmp_w2_k=cmp_w2_k[l], cmp_pe_v=cmp_pe_v[l], cmp_w1_v=cmp_w1_v[l], cmp_b1_v=cmp_b1_v[l],
                 cmp_w2_v=cmp_w2_v[l], lq1=diff_lq1[l], lk1=diff_lk1[l], lq2=diff_lq2[l], lk2=diff_lk2[l],
                 diff_subln_g=diff_subln_g[l], w_out=w_out[l], ln1_g=ln1_g[l], ln1_b=ln1_b[l],
                 ln2_g=ln2_g[l], ln2_b=ln2_b[l], ffn_w1=ffn_w1[l], ffn_w3=ffn_w3[l], ffn_w2=ffn_w2[l])
        lam_init = 0.8 - 0.6 * math.exp(-0.3 * l)
        past = dict(cmp_k=gather_pages(cache_nsa_cmp_k[l], page_table),
                    cmp_v=gather_pages(cache_nsa_cmp_v[l], page_table),
                    sel_k=gather_pages(cache_nsa_sel_k[l], page_table),
                    sel_v=gather_pages(cache_nsa_sel_v[l], page_table),
                    diff_k=gather_pages(cache_diff_k[l], page_table),
                    diff_v=gather_pages(cache_diff_v[l], page_table),
                    win_k=state_nsa_win_k[l], win_v=state_nsa_win_v[l], conv=state_conv[l])
        xp, new_p = layer(xp, pos_p, None, w, lam_init, past_len)
        xs, new_s = layer(xs, pos_s, past, w, lam_init, past_len)
        outs_p.append(new_p)
        outs_s.append(new_s)
    p_cmp_k, p_cmp_v, p_sel_k, p_sel_v, p_diff_k, p_diff_v, p_win_k, p_win_v, p_conv = [
        jnp.stack([o[i] for o in outs_p]) for i in range(9)]
    s_cmp_k, s_cmp_v, s_sel_k, s_sel_v, s_diff_k, s_diff_v, s_win_k, s_win_v, s_conv = [
        jnp.stack([o[i] for o in outs_s]) for i in range(9)]
    return (xp, xs,
            p_cmp_k, p_cmp_v, p_sel_k, p_sel_v, p_diff_k, p_diff_v, p_win_k, p_win_v, p_conv,
            s_cmp_k, s_cmp_v, s_sel_k, s_sel_v, s_diff_k, s_diff_v, s_win_k, s_win_v, s_conv)
```

```python
import math
from contextlib import ExitStack

import numpy as np
import ml_dtypes
import concourse.bass as bass
import concourse.mybir as mybir
from concourse.bass_utils import run_bass_kernel_spmd

F32 = mybir.dt.float32
BF16 = mybir.dt.bfloat16
I32 = mybir.dt.int32
AF = mybir.ActivationFunctionType
ALU = mybir.AluOpType

D = 1024
T = 4096
NT = 32
NSB = 4
NS = 32
XC = T + NS
DEPTH = 2
import os
KDEV = os.environ.get('KDEV', '') == '1'
NPOOL = 256 if KDEV else 2560
PAST = 8192
NPG = 64
NIN = 2584
DFF = 2816
NEG = -30000.0
EPS = 1e-5
DN_ALPHA = (2 * DEPTH) ** 0.25
THETA = 10000.0


class Buf:
    __slots__ = ("w", "r", "name")

    def __init__(self, name=""):
        self.w = None
        self.r = {}
        self.name = name


class Bufs(dict):
    def __init__(self, name):
        super().__init__()
        self.name = name

    def __missing__(self, k):
        b = Buf(f"{self.name}{k}")
        self[k] = b
        return b


class Sched:
    def __init__(self, nc, ndma=8):
        self.nc = nc
        self.eng = {"pe": nc.tensor, "act": nc.scalar, "dve": nc.vector, "pool": nc.gpsimd, "sp": nc.sync}
        self.sem, self.cnt = {}, {}
        self.waited = {k: {} for k in self.eng}
        self._ctx = []
        for k in ("pe", "act", "dve", "pool"):
            cm = nc.semaphore("s_" + k)
            self.sem[k] = cm.__enter__()
            self._ctx.append(cm)
            self.cnt[k] = 0
        self.dq = {}
        for q in ("sp", "pool"):
            ring = []
            for i in range(ndma):
                cm = nc.semaphore(f"d_{q}{i}")
                ring.append(cm.__enter__())
                self._ctx.append(cm)
            self.dq[q] = dict(ring=ring, n=0)

    def close(self):
        for cm in reversed(self._ctx):
            cm.__exit__(None, None, None)

    def _wait(self, e, tok):
        sem, val, key = tok
        w = self.waited[e]
        if w.get(key, 0) >= val:
            return
        self.eng[e].wait_ge(sem, val)
        w[key] = val

    @staticmethod
    def _flat(bs):
        out = []
        for b in bs:
            if isinstance(b, (list, tuple)):
                out.extend(Sched._flat(b))
            else:
                out.append(b)
        return out

    def _deps(self, e, reads, writes):
        reads, writes = self._flat(reads), self._flat(writes)
        for b in reads:
            t = b.w
            if t is not None and not (e == "pe" and t[2] == "pe"):
                self._wait(e, t)
        for b in writes:
            t = b.w
            if t is not None and not (e == "pe" and t[2] == "pe"):
                self._wait(e, t)
            for t in b.r.values():
                if not (e == "pe" and t[2] == "pe"):
                    self._wait(e, t)

    def _commit(self, tok, reads, writes):
        reads, writes = self._flat(reads), self._flat(writes)
        for b in reads:
            o = b.r.get(tok[2])
            if o is None or o[1] < tok[1]:
                b.r[tok[2]] = tok
        for b in writes:
            b.w = tok
            b.r = {}

    def op(self, e, fn, reads=(), writes=(), inc=True):
        self._deps(e, reads, writes)
        ins = fn(self.eng[e])
        if inc:
            self.cnt[e] += 1
            ins.then_inc(self.sem[e], 1)
            tok = (self.sem[e], self.cnt[e], e)
        else:
            tok = (self.sem[e], self.cnt[e] + 1, e)
        self._commit(tok, reads, writes)
        return tok

    def _dma_tok(self, q):
        d = self.dq[q]
        R = len(d["ring"])
        i = d["n"]
        slot = i % R
        sem = d["ring"][slot]
        key = f"{q}{slot}"
        if i >= R:
            self._wait(q, (sem, 16 * (i // R), key))
        return d, sem, (sem, 16 * (i // R + 1), key)

    def dma(self, q, out, in_, reads=(), writes=(), **kw):
        d, sem, tok = self._dma_tok(q)
        self._deps(q, reads, writes)
        ins = self.eng[q].dma_start(out=out, in_=in_, **kw)
        ins.then_inc(sem, 16)
        d["n"] += 1
        self._commit(tok, reads, writes)
        return tok

    def gather(self, out, src2d, idx_col, reads=(), writes=()):
        q = "pool"
        d, sem, tok = self._dma_tok(q)
        self._deps(q, reads, writes)
        ins = self.nc.gpsimd.indirect_dma_start(out=out, out_offset=None, in_=src2d,
                                                 in_offset=bass.IndirectOffsetOnAxis(ap=idx_col, axis=0))
        ins.then_inc(sem, 16)
        d["n"] += 1
        self._commit(tok, reads, writes)
        return tok

    def barrier(self):
        toks = [(self.sem[k], self.cnt[k], k) for k in ("pe", "act", "dve", "pool") if self.cnt[k] > 0]
        for q, d in self.dq.items():
            R = len(d["ring"])
            for slot in range(min(R, d["n"])):
                n_on = (d["n"] - 1 - slot) // R + 1
                toks.append((d["ring"][slot], 16 * n_on, f"{q}{slot}"))
        for e in self.eng:
            for t in toks:
                if t[2] != e:
                    self._wait(e, t)


def _rope_tab(half):
    inv = (np.float32(THETA) ** (-np.arange(half, dtype=np.float32) / np.float32(half))).astype(np.float32)
    pos = np.zeros((128, NT + 1), np.float32)
    for n in range(NT):
        pos[:, n] = 128 * n + np.arange(128)
    pos[:, NT] = PAST + (np.arange(128) % 8)
    ang = (pos[:, :, None] * inv[None, None, :]).astype(np.float32)
    return np.cos(ang).astype(np.float32), np.sin(ang).astype(np.float32)


def _consts():
    c = {}
    c["ident_f"] = np.eye(128, dtype=np.float32)
    c["cosN"], c["sinN"] = _rope_tab(32)
    c["cosD"], c["sinD"] = _rope_tab(16)
    BIG = -NEG
    k = np.arange(128)
    Ep = np.zeros((128, 32, 128), np.float32)
    for kt in range(32):
        for kk in range(128):
            j = 2 * kt + kk // 64
            Ep[j, kt, kk] = BIG
            Ep[64 + j, kt, kk] = BIG
    c["Ep"] = Ep
    Es = np.zeros((128, 64, 128), np.float32)
    for kt in range(64):
        for kk in range(128):
            Es[2 * kt + kk // 64, kt, kk] = BIG
    c["Es"] = Es
    cp = np.arange(503)
    c["M0"] = np.where(16 * (cp[None, :] - 248) + 31 <= k[:, None], 0.0, NEG).astype(np.float32)
    Ft = np.zeros((128, 32, 64), np.float32)
    jj = np.arange(64)
    for n in range(32):
        t = 128 * n + k
        cur = (t // 64)[:, None]
        forced = (jj[None, :] == 0) | (jj[None, :] == cur) | (jj[None, :] == cur - 1)
        Ft[:, n, :] = np.where(jj[None, :] > cur, -2e9, np.where(forced, 1e9, 0.0))
    c["Ftab"] = Ft
    Fs = np.zeros((128, 129), np.float32)
    Fs[:, [0, 127, 128]] = 1e9
    c["Fs"] = Fs
    q = np.arange(128)
    cm = np.where(k[:, None] <= q[None, :], 0.0, NEG).astype(np.float32)
    c["Cm4"] = np.ascontiguousarray(np.broadcast_to(cm[:, None, :], (128, 4, 128)))
    bu = np.where(k[:, None] > q[None, :], 0.0, NEG).astype(np.float32)
    c["Bu4"] = np.ascontiguousarray(np.broadcast_to(bu[:, None, :], (128, 4, 128)))
    t8 = np.arange(8)
    cs = np.where(t8[:, None] <= t8[None, :], 0.0, NEG).astype(np.float32)
    c["CsN"] = np.ascontiguousarray(np.broadcast_to(cs[:, None, :], (8, 4, 8)))
    ws = np.where(k[:, None] > t8[None, :], 0.0, NEG).astype(np.float32)
    c["WsN"] = np.ascontiguousarray(np.broadcast_to(ws[:, None, :], (128, 4, 8)))
    CmD = np.zeros((128, 4, 4, 128), np.float32)
    for r in range(4):
        for j in range(4):
            CmD[:, r, j, :] = np.where(128 * r + k[:, None] <= 128 * j + q[None, :], 0.0, NEG)
    c["CmD"] = CmD
    return c


CONST_SPECS = {
    "ident_f": ([128, 128], F32),
    "Ep": ([128, 32, 128], F32), "Es": ([128, 64, 128], F32), "M0": ([128, 503], F32), "Ftab": ([128, 32, 64], F32), "Fs": ([128, 129], F32),
    "Cm4": ([128, 4, 128], F32), "Bu4": ([128, 4, 128], F32), "CsN": ([8, 4, 8], F32), "WsN": ([128, 4, 8], F32), "CmD": ([128, 4, 4, 128], F32),
    "cosN": ([128, NT + 1, 32], F32), "sinN": ([128, NT + 1, 32], F32),
    "cosD": ([128, NT + 1, 16], F32), "sinD": ([128, NT + 1, 16], F32),
}

C_CA, C_CG, C_NQ, C_CK, C_CV, C_SK, C_SV, C_WK, C_WV, C_GT, C_DQ, C_DK, C_DV = (
    0, 256, 512, 1024, 1152, 1280, 1408, 1536, 1664, 1792, 1816, 2072, 2328)


class Prog:
    def __init__(self):
        self.nc = nc = bass.Bass("TRN2", target_bir_lowering=False)
        self.S = Sched(nc)
        self.din, self.dout = {}, {}
        self.outbufs = []

    def inp(self, name, shape, dt=F32):
        self.din[name] = self.nc.dram_tensor(name, list(shape), dt, kind="ExternalInput").ap()
        return self.din[name]

    def outp(self, name, shape, dt=F32):
        self.dout[name] = self.nc.dram_tensor(name, list(shape), dt, kind="ExternalOutput").ap()
        return self.dout[name]

    def scratch(self, name, shape, dt):
        return self.nc.dram_tensor(name, list(shape), dt, kind="Internal").ap()

    def store(self, out_ap, in_ap, reads, q="sp"):
        b = Buf("o")
        self.S.dma(q, out_ap, in_ap, reads=reads, writes=[b])
        return b


_UC = [0]


def U(name):
    _UC[0] += 1
    return f"{name}_{_UC[0]}"


class Rot:
    def __init__(self, es, nc, name, shape, dt, n):
        self.t = [es.enter_context(nc.sbuf_tensor(U(f"{name}{i}"), list(shape), dt)) for i in range(n)]
        self.b = [Buf(f"{name}{i}") for i in range(n)]
        self.i = 0

    def next(self):
        k = self.i % len(self.t)
        self.i += 1
        return self.t[k], self.b[k]


class PsumRot:
    def __init__(self, es, nc, n=8):
        self.t = [es.enter_context(nc.psum_tensor(f"psb{i}", [128, 512], F32)) for i in range(n)]
        self.b = [Buf(f"psb{i}") for i in range(n)]
        self.free = list(range(n))
        self.i = 0

    def next(self):
        k = self.free[self.i % len(self.free)]
        self.i += 1
        return self.t[k], self.b[k]

    def take(self, n):
        ks = self.free[-n:]
        self.free = self.free[:-n]
        return [(self.t[k], self.b[k]) for k in ks], ks

    def give(self, ks):
        self.free = self.free + list(ks)


def build_program():
    P = Prog()
    nc, S = P.nc, P.S
    xp = P.inp("xp", [T, D])
    xs = P.inp("xs", [NS, D])
    c_cmp_k = P.inp("c_cmp_k", [DEPTH * NPOOL * 128, 128])
    c_cmp_v = P.inp("c_cmp_v", [DEPTH * NPOOL * 128, 128])
    c_sel_k = P.inp("c_sel_k", [DEPTH * NPOOL * 128, 128])
    c_sel_v = P.inp("c_sel_v", [DEPTH * NPOOL * 128, 128])
    c_diff_k = P.inp("c_diff_k", [DEPTH * NPOOL * 128, 256])
    c_diff_v = P.inp("c_diff_v", [DEPTH * NPOOL * 128, 256])
    st_win_k = P.inp("st_win_k", [DEPTH, NSB, 512, 128])
    st_win_v = P.inp("st_win_v", [DEPTH, NSB, 512, 128])
    st_conv = P.inp("st_conv", [DEPTH, NSB, 30, 256])
    pt = P.inp("pt", [NSB, NPG], I32)
    w_in = P.inp("w_in", [DEPTH, D, NIN])

    for nm, shp in (("conv_dw_w", [DEPTH, 31, 256]), ("conv_dw_b", [DEPTH, 256]), ("conv_ln_g", [DEPTH, 256]), ("conv_ln_b", [DEPTH, 256]),
                    ("cmp_pe_k", [DEPTH, 32, 64]), ("cmp_w1_k", [DEPTH, 2048, 128]), ("cmp_b1_k", [DEPTH, 128]), ("cmp_w2_k", [DEPTH, 128, 64]),
                    ("cmp_pe_v", [DEPTH, 32, 64]), ("cmp_w1_v", [DEPTH, 2048, 128]), ("cmp_b1_v", [DEPTH, 128]), ("cmp_w2_v", [DEPTH, 128, 64]),
                    ("diff_lq1", [DEPTH, 32]), ("diff_lk1", [DEPTH, 32]), ("diff_lq2", [DEPTH, 32]), ("diff_lk2", [DEPTH, 32]),
                    ("diff_subln_g", [DEPTH, 64]), ("w_out", [DEPTH, D, D]), ("ln1_g", [DEPTH, D]), ("ln1_b", [DEPTH, D]),
                    ("ln2_g", [DEPTH, D]), ("ln2_b", [DEPTH, D]), ("ffn_w1", [DEPTH, D, DFF]), ("ffn_w3", [DEPTH, D, DFF]), ("ffn_w2", [DEPTH, DFF, D])):
        P.inp(nm, shp)
    for nm, shp in CONST_SPECS.items():
        P.inp(nm, shp[0], shp[1])

    y_p = P.outp("y_p", [T, D])
    y_s = P.outp("y_s", [NS, D])
    o_p = {}
    for nm, w_ in (("cmp_k", 128), ("cmp_v", 128), ("sel_k", 128), ("sel_v", 128), ("diff_k", 256), ("diff_v", 256)):
        o_p[nm] = P.outp("p_" + nm, [DEPTH, T, w_])
    o_p["win_k"] = P.outp("p_win_k", [DEPTH, 512, 128])
    o_p["win_v"] = P.outp("p_win_v", [DEPTH, 512, 128])
    o_p["conv"] = P.outp("p_conv", [DEPTH, 30, 256])
    o_s = {}
    for nm, w_ in (("cmp_k", 128), ("cmp_v", 128), ("sel_k", 128), ("sel_v", 128), ("diff_k", 256), ("diff_v", 256)):
        o_s[nm] = P.outp("s_" + nm, [DEPTH, NSB, 8, w_])
    o_s["win_k"] = P.outp("s_win_k", [DEPTH, NSB, 512, 128])
    o_s["win_v"] = P.outp("s_win_v", [DEPTH, NSB, 512, 128])
    o_s["conv"] = P.outp("s_conv", [DEPTH, NSB, 30, 256])

    xT_d = P.scratch("xT_d", [128, 8, XC], BF16)
    xmid = P.scratch("xmid", [XC, D], F32)
    wk_d = P.scratch("wk_d", [T, 128], F32)
    wv_d = P.scratch("wv_d", [T, 128], F32)
    u_d = P.scratch("u_d", [XC, 256], F32)
    mixT_d = P.scratch("mixT_d", [128, 8, XC], BF16)
    B_kvd = {k: [] for k in range(NT + NSB)}
    B_mix = Bufs("mix")
    B_xT = Bufs("xTd")
    B_xmid = Bufs("xmid")

    with ExitStack() as gs:
        PS = PsumRot(gs, nc)
        ident_f = gs.enter_context(nc.sbuf_tensor(U("sb_ident_f"), [128, 128], F32))
        ident_b = gs.enter_context(nc.sbuf_tensor(U("sb_ident_b"), [128, 128], BF16))
        B_id = Buf("ident")
        S.dma("sp", ident_f[:], P.din["ident_f"], writes=[B_id])
        S.op("dve", lambda e: e.tensor_copy(out=ident_b[:], in_=ident_f[:]), reads=[B_id], writes=[B_id])

        def evac(i, out, in_, reads, writes):
            if i % 2 == 0:
                return S.op("act", lambda e: e.copy(out=out, in_=in_), reads=reads, writes=writes)
            return S.op("dve", lambda e: e.tensor_copy(out=out, in_=in_), reads=reads, writes=writes)

        def phase_xT(l):
            with ExitStack() as es:
                xin = Rot(es, nc, "xin", [128, D], F32, 3)
                xts = Rot(es, nc, "xts", [128, 8, 128], BF16, 3)
                for n in range(NT + 1):
                    rows = 128 if n < NT else NS
                    t0 = n * 128
                    if l == 0:
                        src = xp[t0:t0 + rows, :] if n < NT else xs
                        rd = []
                    else:
                        src = xmid[t0:t0 + rows, :]
                        rd = [B_xmid[n]]
                    xt, bx = xin.next()
                    S.dma("sp", xt[:rows, :], src, reads=rd, writes=[bx])
                    st, bs = xts.next()
                    for hf in range(2):
                        ps, bp = PS.next()
                        for j in range(4):
                            kc = hf * 4 + j
                            S.op("pe", lambda e: e.transpose(ps[:, j * 128:j * 128 + rows], xt[:rows, kc * 128:(kc + 1) * 128],
                                                             ident_f[:rows, :rows]),
                                 reads=[bx, B_id], writes=[bp], inc=(j == 3))
                        evac(hf, st[:, hf * 4:hf * 4 + 4, :rows],
                             ps[:].rearrange("p (j c) -> p j c", j=4)[:, :, :rows], [bp], [bs])
                    S.dma("sp", xT_d[:, :, t0:t0 + rows], st[:, :, :rows], reads=[bs], writes=[B_xT[n]])

        def rope_tm(rows, out4, in4, cos, sin, nh, half, tmp):
            tt, bt = tmp
            n = nh * half
            cb = cos.unsqueeze(1).broadcast_to([rows, nh, half])
            sb_ = sin.unsqueeze(1).broadcast_to([rows, nh, half])
            x1, x2 = in4[:, :, 0, :], in4[:, :, 1, :]
            tv = [tt[:rows, k * n:(k + 1) * n].rearrange("p (h d) -> p h d", h=nh) for k in range(4)]
            return x1, x2, cb, sb_, tv

        def load_w_cols(es, name, l, c0, ncols):
            wt = es.enter_context(nc.sbuf_tensor(U(name), [128, 8, ncols], BF16))
            bw = Buf(name)
            src = w_in[l].rearrange("(kc p) c -> p kc c", p=128)
            for kc in range(8):
                S.dma("pool", wt[:, kc, :], src[:, kc, c0:c0 + ncols], writes=[bw])
            return wt, bw

        def phase_kv_outputs(l):
            with ExitStack() as es:
                wkv, bwkv = load_w_cols(es, "wkv", l, C_CK, C_GT - C_CK)
                wdf, bwdf = load_w_cols(es, "wdf", l, C_DK, 512)
                wcv, bwcv = load_w_cols(es, "wcv", l, C_CA, 512)
                cosN = es.enter_context(nc.sbuf_tensor(U("sb_cosN"), [128, NT + 1, 32], F32))
                sinN = es.enter_context(nc.sbuf_tensor(U("sb_sinN"), [128, NT + 1, 32], F32))
                cosD = es.enter_context(nc.sbuf_tensor(U("sb_cosD"), [128, NT + 1, 16], F32))
                sinD = es.enter_context(nc.sbuf_tensor(U("sb_sinD"), [128, NT + 1, 16], F32))
                B_tab = Buf("tabs")
                for t_, nm in ((cosN, "cosN"), (sinN, "sinN"), (cosD, "cosD"), (sinD, "sinD")):
                    S.dma("sp", t_[:], P.din[nm], writes=[B_tab])
                xtr = Rot(es, nc, "xtl", [128, 8, 128], BF16, 3)
                zs = Rot(es, nc, "zs", [128, 768 + 512 + 512], F32, 3)
                kr = Rot(es, nc, "kr", [128, 2 * 128 + 256], F32, 3)
                tmp = Rot(es, nc, "rtmp", [128, 4 * 256], F32, 2)

                def do_tile(n, rows, c0, seq):
                    xt, bx = xtr.next()
                    S.dma("sp", xt[:, :, :rows], xT_d[:, :, c0:c0 + rows], reads=[B_xT[min(n, NT)]], writes=[bx])
                    z, bz = zs.next()
                    pieces = [(wkv, bwkv, 0, 512, 0), (wkv, bwkv, 512, 256, 512), (wdf, bwdf, 0, 512, 768), (wcv, bwcv, 0, 512, 1280)]
                    for i, (wt, bw, wc0, wn, zc0) in enumerate(pieces):
                        ps, bp = PS.next()
                        for kc in range(8):
                            S.op("pe", lambda e: e.matmul(ps[:rows, :wn], lhsT=xt[:, kc, :rows], rhs=wt[:, kc, wc0:wc0 + wn],
                                                          start=(kc == 0), stop=(kc == 7)),
                                 reads=[bx, bw], writes=[bp], inc=(kc == 7))
                        evac(i, z[:rows, zc0:zc0 + wn], ps[:rows, :wn], [bp], [bz])
                    k, bk = kr.next()
                    tt, bt = tmp.next()
                    ti = NT if seq is not None else n
                    zin = z[:rows, 256:768].rearrange("p (a g h d) -> p a g h d", a=4, g=2, h=2)[:, 0:4:2]
                    kout = k[:rows, 0:256].rearrange("p (a g h d) -> p a g h d", a=2, g=2, h=2)
                    cb = cosN[:rows, ti, :].unsqueeze(1).broadcast_to([rows, 2, 32])
                    sb_ = sinN[:rows, ti, :].unsqueeze(1).broadcast_to([rows, 2, 32])
                    for a in range(2):
                        x1, x2 = zin[:, a, :, 0, :], zin[:, a, :, 1, :]
                        o1, o2 = kout[:, a, :, 0, :], kout[:, a, :, 1, :]
                        t = [tt[:rows, (a * 4 + j) * 64:(a * 4 + j + 1) * 64].rearrange("p (g d) -> p g d", g=2) for j in range(4)]
                        S.op("dve", lambda e: e.tensor_tensor(out=t[0], in0=x1, in1=cb, op=ALU.mult), reads=[bz, B_tab], writes=[bt])
                        S.op("pool", lambda e: e.tensor_tensor(out=t[1], in0=x2, in1=sb_, op=ALU.mult), reads=[bz, B_tab], writes=[bt])
                        S.op("dve", lambda e: e.tensor_tensor(out=t[2], in0=x2, in1=cb, op=ALU.mult), reads=[bz, B_tab], writes=[bt])
                        S.op("pool", lambda e: e.tensor_tensor(out=t[3], in0=x1, in1=sb_, op=ALU.mult), reads=[bz, B_tab], writes=[bt])
                        S.op("dve", lambda e: e.tensor_tensor(out=o1, in0=t[0], in1=t[1], op=ALU.subtract), reads=[bt], writes=[bk])
                        S.op("pool", lambda e: e.tensor_tensor(out=o2, in0=t[2], in1=t[3], op=ALU.add), reads=[bt], writes=[bk])
                    zd = z[:rows, 768:1024].rearrange("p (h s d) -> p h s d", h=8, s=2)
                    kd = k[:rows, 256:512].rearrange("p (h s d) -> p h s d", h=8, s=2)
                    cbd = cosD[:rows, ti, :].unsqueeze(1).broadcast_to([rows, 8, 16])
                    sbd = sinD[:rows, ti, :].unsqueeze(1).broadcast_to([rows, 8, 16])
                    x1, x2 = zd[:, :, 0, :], zd[:, :, 1, :]
                    t = [tt[:rows, 512 + j * 128:512 + (j + 1) * 128].rearrange("p (h d) -> p h d", h=8) for j in range(4)]
                    S.op("dve", lambda e: e.tensor_tensor(out=t[0], in0=x1, in1=cbd, op=ALU.mult), reads=[bz, B_tab], writes=[bt])
                    S.op("pool", lambda e: e.tensor_tensor(out=t[1], in0=x2, in1=sbd, op=ALU.mult), reads=[bz, B_tab], writes=[bt])
                    S.op("dve", lambda e: e.tensor_tensor(out=t[2], in0=x2, in1=cbd, op=ALU.mult), reads=[bz, B_tab], writes=[bt])
                    S.op("pool", lambda e: e.tensor_tensor(out=t[3], in0=x1, in1=sbd, op=ALU.mult), reads=[bz, B_tab], writes=[bt])
                    S.op("dve", lambda e: e.tensor_tensor(out=kd[:, :, 0, :], in0=t[0], in1=t[1], op=ALU.subtract), reads=[bt], writes=[bk])
                    S.op("pool", lambda e: e.tensor_tensor(out=kd[:, :, 1, :], in0=t[2], in1=t[3], op=ALU.add), reads=[bt], writes=[bk])
                    S.op("act", lambda e: e.activation(out=z[:rows, 1536:1792], in_=z[:rows, 1536:1792], func=AF.Sigmoid), reads=[bz], writes=[bz])
                    S.op("pool", lambda e: e.tensor_tensor(out=z[:rows, 1536:1792], in0=z[:rows, 1280:1536], in1=z[:rows, 1536:1792], op=ALU.mult),
                         reads=[bz], writes=[bz])
                    B_kvd[n].clear()

                    def st_(o_, i_, rd_):
                        B_kvd[n].append(P.store(o_, i_, rd_))
                    if seq is None:
                        r0 = n * 128
                        st_(o_p["cmp_k"][l, r0:r0 + 128, :], z[:, 0:128], [bz])
                        st_(o_p["cmp_v"][l, r0:r0 + 128, :], z[:, 128:256], [bz])
                        st_(o_p["sel_k"][l, r0:r0 + 128, :], k[:, 0:128], [bk])
                        st_(o_p["sel_v"][l, r0:r0 + 128, :], z[:, 384:512], [bz])
                        st_(o_p["diff_k"][l, r0:r0 + 128, :], k[:, 256:512], [bk])
                        st_(o_p["diff_v"][l, r0:r0 + 128, :], z[:, 1024:1280], [bz])
                        if n >= NT - 4:
                            w0 = (n - (NT - 4)) * 128
                            st_(o_p["win_k"][l, w0:w0 + 128, :], k[:, 128:256], [bk])
                            st_(o_p["win_v"][l, w0:w0 + 128, :], z[:, 640:768], [bz])
                        if n == NT - 1:
                            st_(o_p["conv"][l, :, :], z[98:128, 1536:1792], [bz])
                        st_(wk_d[r0:r0 + 128, :], k[:, 128:256], [bk])
                        st_(wv_d[r0:r0 + 128, :], z[:, 640:768], [bz])
                        st_(u_d[r0:r0 + 128, :], z[:, 1536:1792], [bz])
                    else:
                        b = seq
                        st_(o_s["cmp_k"][l, b], z[:8, 0:128], [bz])
                        st_(o_s["cmp_v"][l, b], z[:8, 128:256], [bz])
                        st_(o_s["sel_k"][l, b], k[:8, 0:128], [bk])
                        st_(o_s["sel_v"][l, b], z[:8, 384:512], [bz])
                        st_(o_s["diff_k"][l, b], k[:8, 256:512], [bk])
                        st_(o_s["diff_v"][l, b], z[:8, 1024:1280], [bz])
                        st_(o_s["win_k"][l, b, 504:512, :], k[:8, 128:256], [bk])
                        st_(o_s["win_v"][l, b, 504:512, :], z[:8, 640:768], [bz])
                        st_(o_s["conv"][l, b, 22:30, :], z[:8, 1536:1792], [bz])
                        st_(u_d[T + 8 * b:T + 8 * b + 8, :], z[:8, 1536:1792], [bz])
                        st_(o_s["win_k"][l, b, 0:504, :], st_win_k[l, b, 8:512, :], [])
                        st_(o_s["win_v"][l, b, 0:504, :], st_win_v[l, b, 8:512, :], [])
                        st_(o_s["conv"][l, b, 0:22, :], st_conv[l, b, 8:30, :], [])

                for n in range(NT):
                    do_tile(n, 128, n * 128, None)
                for b in range(NSB):
                    do_tile(NT + b, 8, T + 8 * b, b)

        def sbt(es, name, shape, dt):
            return es.enter_context(nc.sbuf_tensor(U(name), list(shape), dt))

        def cload(es, name, shape, q="pool", dt=BF16):
            t_ = sbt(es, "c_" + name, shape, dt)
            b_ = Buf(name)
            S.dma(q if dt != F32 else "sp", t_[:], P.din[name], writes=[b_])
            return t_, b_

        def rope6(rows, x1, x2, o1, o2, cb, sb_, t, rd, bt, bo):
            S.op("dve", lambda e: e.tensor_tensor(out=t[0], in0=x1, in1=cb, op=ALU.mult), reads=rd, writes=[bt])
            S.op("pool", lambda e: e.tensor_tensor(out=t[1], in0=x2, in1=sb_, op=ALU.mult), reads=rd, writes=[bt])
            S.op("dve", lambda e: e.tensor_tensor(out=t[2], in0=x2, in1=cb, op=ALU.mult), reads=rd, writes=[bt])
            S.op("pool", lambda e: e.tensor_tensor(out=t[3], in0=x1, in1=sb_, op=ALU.mult), reads=rd, writes=[bt])
            S.op("dve", lambda e: e.tensor_tensor(out=o1, in0=t[0], in1=t[1], op=ALU.subtract), reads=[bt], writes=[bo])
            S.op("pool", lambda e: e.tensor_tensor(out=o2, in0=t[2], in1=t[3], op=ALU.add), reads=[bt], writes=[bo])

        def attn(units, kts, rows):
            pts = attn.pts
            for ki, kt in enumerate(kts):
                nk = kt["nk"]
                for ui, u in enumerate(units):
                    nco = u["ncols"]
                    ps, bp = PS.next()
                    ms = kt["masks"][ui]
                    S.op("pe", lambda e: e.matmul(ps[:nk, :nco], lhsT=kt["K"][ui], rhs=u["Q"], start=True, stop=(len(ms) == 0)),
                         reads=kt["rd"] + u["rdQ"], writes=[bp], inc=(len(ms) == 0))
                    for mi, (ml, mr) in enumerate(ms):
                        S.op("pe", lambda e: e.matmul(ps[:nk, :nco], lhsT=ml, rhs=mr, start=False, stop=(mi == len(ms) - 1)),
                             reads=kt["rd"] + u["rdQ"], writes=[bp], inc=(mi == len(ms) - 1))
                    pt_, bpt = pts.next()
                    S.op("act", lambda e: e.activation(out=pt_[:nk, :nco], in_=ps[:nk, :nco], func=AF.Exp, scale=u["scale"]),
                         reads=[bp], writes=[bpt])
                    acc, bacc = u["acc"]
                    nb = u["nblk"]
                    for blk in range(nb):
                        first = (ki == 0 and blk == 0)
                        last = (ki == len(kts) - 1)
                        S.op("pe", lambda e: e.matmul(acc[:rows, blk * 65:blk * 65 + 65], lhsT=pt_[:nk, blk * rows:(blk + 1) * rows],
                                                      rhs=kt["V"][ui], start=first, stop=last, skip_group_check=True),
                             reads=[bpt] + kt["rd"], writes=[bacc], inc=(blk == nb - 1))

        def phase_conv(l):
            with ExitStack() as es:
                uT = sbt(es, "uT", [128, 2, 30 + T], BF16)
                uTs = sbt(es, "uTs", [128, 2, NSB, 38], BF16)
                B_u = Buf("uT")
                Dg = sbt(es, "Dg", [128, 2, 31, 128], BF16)
                B_Dg = Buf("Dg")
                dwT = sbt(es, "dwT", [128, 2, 32], F32)
                dws = sbt(es, "dws", [32, 256], F32)
                prm = sbt(es, "cprm", [128, 3, 2], F32)
                B_prm = Buf("cprm")
                onesF = sbt(es, "onesF", [128, 128], F32)
                B_ones = Buf("ones")
                S.op("pool", lambda e: e.memset(onesF[:], 1.0 / 256.0), writes=[B_ones])
                S.op("pool", lambda e: e.memset(uT[:, :, 0:30], 0.0), writes=[B_u])
                for i, src in enumerate((P.din["conv_dw_b"], P.din["conv_ln_g"], P.din["conv_ln_b"])):
                    S.dma("sp", prm[:, i, :], src[l].rearrange("(h p) -> p h", p=128), writes=[B_prm], allow_slow_non_contiguous=True)
                B_dws = Buf("dws")
                S.dma("sp", dws[:31, :], P.din["conv_dw_w"][l], writes=[B_dws])
                ps, bp = PS.next()
                for h in range(2):
                    S.op("pe", lambda e: e.transpose(ps[:, h * 32:h * 32 + 31], dws[:31, h * 128:(h + 1) * 128], ident_f[:31, :31]),
                         reads=[B_dws, B_id], writes=[bp])
                B_dwT = Buf("dwT")
                S.op("act", lambda e: e.copy(out=dwT[:, :, :], in_=ps[:, 0:64].rearrange("p (h w) -> p h w", h=2)), reads=[bp], writes=[B_dwT])
                for h in range(2):
                    for w in range(31):
                        S.op("dve" if (w % 2 == 0) else "pool",
                             lambda e: e.tensor_scalar(out=Dg[:, h, w, :], in0=ident_f[:], scalar1=dwT[:, h, w:w + 1], scalar2=None, op0=ALU.mult),
                             reads=[B_dwT, B_id], writes=[B_Dg])
                uin = Rot(es, nc, "uin", [128, 256], F32, 3)
                for n in range(NT):
                    ut, bu = uin.next()
                    S.dma("sp", ut[:, :], u_d[n * 128:(n + 1) * 128, :], reads=[B_kvd[n]], writes=[bu])
                    ps, bp = PS.next()
                    for h in range(2):
                        S.op("pe", lambda e: e.transpose(ps[:, h * 128:(h + 1) * 128], ut[:, h * 128:(h + 1) * 128], ident_f[:, :]),
                             reads=[bu, B_id], writes=[bp], inc=(h == 1))
                    evac(n, uT[:, :, 30 + n * 128:30 + (n + 1) * 128], ps[:, 0:256].rearrange("p (h c) -> p h c", h=2), [bp], [B_u])
                for b in range(NSB):
                    ut, bu = uin.next()
                    S.dma("sp", ut[:30, :], st_conv[l, b], writes=[bu])
                    ps, bp = PS.next()
                    for h in range(2):
                        S.op("pe", lambda e: e.transpose(ps[:, h * 32:h * 32 + 30], ut[:30, h * 128:(h + 1) * 128], ident_f[:30, :30]),
                             reads=[bu, B_id], writes=[bp], inc=(h == 1))
                    evac(b, uTs[:, :, b, 0:30], ps[:, 0:64].rearrange("p (h c) -> p h c", h=2)[:, :, 0:30], [bp], [B_u])
                    ut, bu = uin.next()
                    S.dma("sp", ut[:8, :], u_d[T + 8 * b:T + 8 * b + 8, :], reads=[B_kvd[NT + b]], writes=[bu])
                    ps, bp = PS.next()
                    for h in range(2):
                        S.op("pe", lambda e: e.transpose(ps[:, h * 32:h * 32 + 8], ut[:8, h * 128:(h + 1) * 128], ident_f[:8, :8]),
                             reads=[bu, B_id], writes=[bp], inc=(h == 1))
                    evac(b + 1, uTs[:, :, b, 30:38], ps[:, 0:64].rearrange("p (h c) -> p h c", h=2)[:, :, 0:8], [bp], [B_u])
                yv = Rot(es, nc, "cyv", [128, 2, 512], F32, 2)
                ysq = Rot(es, nc, "cysq", [128, 2, 512], F32, 2)
                stt = Rot(es, nc, "cst", [128, 4, 512], F32, 2)
                yo = Rot(es, nc, "cyo", [128, 2, 512], BF16, 2)

                def conv_chunk(rhs_fn, N, dst0):
                    y, by = yv.next()
                    q2, bq2 = ysq.next()
                    for h in range(2):
                        ps, bp = PS.next()
                        for w in range(31):
                            S.op("pe", lambda e: e.matmul(ps[:, :N], lhsT=Dg[:, h, w, :], rhs=rhs_fn(h, w), start=(w == 0), stop=(w == 30)),
                                 reads=[B_Dg, B_u], writes=[bp], inc=(w == 30))
                        S.op("act", lambda e: e.activation(out=y[:, h, :N], in_=ps[:, :N], func=AF.Identity, bias=prm[:, 0, h:h + 1], scale=1.0),
                             reads=[bp, B_prm], writes=[by])
                        S.op("act", lambda e: e.activation(out=q2[:, h, :N], in_=ps[:, :N], func=AF.Square, bias=prm[:, 0, h:h + 1], scale=1.0),
                             reads=[bp, B_prm], writes=[bq2])
                    psm, bpm = PS.next()
                    for h in range(2):
                        S.op("pe", lambda e: e.matmul(psm[:, :N], lhsT=onesF[:], rhs=y[:, h, :N], start=(h == 0), stop=(h == 1)),
                             reads=[B_ones, by], writes=[bpm], inc=(h == 1))
                    pss, bps_ = PS.next()
                    for h in range(2):
                        S.op("pe", lambda e: e.matmul(pss[:, :N], lhsT=onesF[:], rhs=q2[:, h, :N], start=(h == 0), stop=(h == 1)),
                             reads=[B_ones, bq2], writes=[bps_], inc=(h == 1))
                    st_, bst = stt.next()
                    S.op("act", lambda e: e.copy(out=st_[:, 0, :N], in_=psm[:, :N]), reads=[bpm], writes=[bst])
                    S.op("pool", lambda e: e.tensor_tensor(out=st_[:, 1, :N], in0=st_[:, 0, :N], in1=st_[:, 0, :N], op=ALU.mult), reads=[bst], writes=[bst])
                    S.op("dve", lambda e: e.tensor_tensor(out=st_[:, 2, :N], in0=pss[:, :N], in1=st_[:, 1, :N], op=ALU.subtract), reads=[bps_, bst], writes=[bst])
                    S.op("dve", lambda e: e.tensor_scalar(out=st_[:, 2, :N], in0=st_[:, 2, :N], scalar1=0.0, scalar2=EPS, op0=ALU.max, op1=ALU.add), reads=[bst], writes=[bst])
                    S.op("act", lambda e: e.activation(out=st_[:, 3, :N], in_=st_[:, 2, :N], func=AF.Sqrt), reads=[bst], writes=[bst])
                    S.op("dve", lambda e: e.reciprocal(out=st_[:, 3, :N], in_=st_[:, 3, :N]), reads=[bst], writes=[bst])
                    o_, bo = yo.next()
                    for h in range(2):
                        S.op("dve", lambda e: e.tensor_tensor(out=y[:, h, :N], in0=y[:, h, :N], in1=st_[:, 0, :N], op=ALU.subtract), reads=[bst, by], writes=[by])
                        S.op("pool", lambda e: e.tensor_tensor(out=y[:, h, :N], in0=y[:, h, :N], in1=st_[:, 3, :N], op=ALU.mult), reads=[bst, by], writes=[by])
                        S.op("act", lambda e: e.activation(out=o_[:, h, :N], in_=y[:, h, :N], func=AF.Silu, bias=prm[:, 2, h:h + 1], scale=prm[:, 1, h:h + 1]),
                             reads=[by, B_prm], writes=[bo])
                    S.dma("sp", mixT_d[:, 0:2, dst0:dst0 + N], o_[:, :, :N], reads=[bo], writes=[B_mix[0]])

                for c in range(T // 512):
                    conv_chunk(lambda h, w, c=c: uT[:, h, c * 512 + w:c * 512 + w + 512], 512, c * 512)
                for b in range(NSB):
                    conv_chunk(lambda h, w, b=b: uTs[:, h, b, w:w + 8], 8, T + 8 * b)

        def phase_nsa(l):
            with ExitStack() as es:
                KT = sbt(es, "KT", [128, 2, 8208], BF16)
                KT4 = KT[:].rearrange("p a c -> p (a c)").rearrange("p (a c) -> p a c", a=4)
                B_K = Bufs("KT")
                svA = sbt(es, "svA", [128, 65, 2, 65], BF16)
                wvA = sbt(es, "wvA", [128, 32, 2, 65], BF16)
                B_V = Bufs("VA")
                S.op("pool", lambda e: e.memset(svA[:, :, :, 64:65], 1.0), writes=[B_V["s"]])
                S.op("pool", lambda e: e.memset(wvA[:, :, :, 64:65], 1.0), writes=[B_V["w"]])
                Em = sbt(es, "Em", [128, 64, 128], BF16)
                B_E = Buf("E")
                S.dma("pool", Em[:, 0:32, :], P.din["Ep"], writes=[B_E])
                M0, B_M0 = cload(es, "M0", [128, 503], dt=F32)
                Ft, B_Ft = cload(es, "Ftab", [128, 32, 64], dt=F32)
                Fs, B_Fs = cload(es, "Fs", [128, 129], dt=F32)
                Cm4, B_Cm4 = cload(es, "Cm4", [128, 4, 128])
                Bu4, B_Bu4 = cload(es, "Bu4", [128, 4, 128])
                CsN, B_CsN = cload(es, "CsN", [8, 4, 8])
                WsN, B_WsN = cload(es, "WsN", [128, 4, 8])
                cosN, B_cos = cload(es, "cosN", [128, NT + 1, 32], dt=F32)
                sinN, B_sin = cload(es, "sinN", [128, NT + 1, 32], dt=F32)
                B_tab = [B_cos, B_sin]
                wq, bwq = load_w_cols(es, "wq", l, C_NQ, 512)
                wg, bwg = load_w_cols(es, "wg", l, C_GT, 24)
                W1 = [sbt(es, "W1k", [128, 32, 128], BF16), sbt(es, "W1v", [128, 32, 128], BF16)]
                B_W1 = Buf("W1")
                for kv, nm in enumerate(("cmp_w1_k", "cmp_w1_v")):
                    src = P.din[nm][l].rearrange("(r d) h -> d r h", d=64)
                    for hf in range(2):
                        S.dma("pool", W1[kv][64 * hf:64 * hf + 64, :, :], src, writes=[B_W1])
                W2kz = sbt(es, "W2kz", [128, 2, 128], BF16)
                W2v = sbt(es, "W2v", [128, 64], BF16)
                B_W2 = Buf("W2")
                S.op("pool", lambda e: e.memset(W2kz[:], 0.0), writes=[B_W2])
                for g in range(2):
                    S.dma("pool", W2kz[:, g, 64 * g:64 * g + 64], P.din["cmp_w2_k"][l], writes=[B_W2])
                S.dma("pool", W2v[:, :], P.din["cmp_w2_v"][l], writes=[B_W2])
                b1t = sbt(es, "b1t", [128, 2], F32)
                c1 = sbt(es, "c1", [128, 2], F32)
                B_c1 = Buf("c1")
                pes = sbt(es, "pes", [32, 2, 64], F32)
                peT = sbt(es, "peT", [64, 2, 32], BF16)
                for kv, (nb_, npe) in enumerate((("cmp_b1_k", "cmp_pe_k"), ("cmp_b1_v", "cmp_pe_v"))):
                    S.dma("sp", b1t[:, kv:kv + 1], P.din[nb_][l].rearrange("(p o) -> p o", o=1), writes=[B_c1])
                    S.dma("sp", pes[:, kv, :], P.din[npe][l], writes=[B_c1])
                ps, bp = PS.next()
                for kv in range(2):
                    S.op("pe", lambda e: e.transpose(ps[:64, kv * 32:kv * 32 + 32], pes[:32, kv, :], ident_f[:32, :32]), reads=[B_c1, B_id], writes=[bp], inc=(kv == 1))
                S.op("act", lambda e: e.copy(out=peT[:, :, :], in_=ps[:64, 0:64].rearrange("p (k r) -> p k r", k=2)), reads=[bp], writes=[B_c1])
                for kv in range(2):
                    ps, bp = PS.next()
                    for r in range(32):
                        S.op("pe", lambda e: e.matmul(ps[:, 0:1], lhsT=W1[kv][0:64, r, :], rhs=peT[0:64, kv, r:r + 1], start=(r == 0), stop=(r == 31)),
                             reads=[B_W1, B_c1], writes=[bp], inc=(r == 31))
                    S.op("dve", lambda e: e.tensor_tensor(out=c1[:, kv:kv + 1], in0=ps[:, 0:1], in1=b1t[:, kv:kv + 1], op=ALU.add), reads=[bp, B_c1], writes=[B_c1])
                kccT = sbt(es, "kccT", [128, 512], BF16)
                vcc = sbt(es, "vcc", [128, 4, 2, 64], BF16)
                B_cc = Buf("cc")
                kin = Rot(es, nc, "kin", [128, 128], F32, 4)
                gel = Rot(es, nc, "gel", [128, 512], BF16, 2)
                xtr = Rot(es, nc, "xtq", [128, 8, 128], BF16, 2)
                zq = Rot(es, nc, "zq", [128, 512], F32, 2)
                qrr = Rot(es, nc, "qr", [128, 512], F32, 2)
                rtmp = Rot(es, nc, "qtmp", [128, 1024], F32, 2)
                gts = Rot(es, nc, "gts", [128, 24], F32, 2)
                qbs = Rot(es, nc, "qb", [128, 2, 4, 128], BF16, 2)
                qTp = Rot(es, nc, "qTp", [128, 2, 4, 128], BF16, 2)
                qTs = Rot(es, nc, "qTs", [128, 2, 4, 8], BF16, 2)
                smr = Rot(es, nc, "sm", [128, 512], F32, 2)
                pfr = Rot(es, nc, "pf", [128, 512], F32, 2)
                pnr = Rot(es, nc, "pn", [128, 512], BF16, 2)
                pnT = Rot(es, nc, "pnT", [128, 4, 128], BF16, 2)
                sml = Rot(es, nc, "sml", [128, 8], F32, 8)
                ppad = sbt(es, "ppad", [128, 2, 528], F32)
                B_pp = Buf("ppad")
                scr = Rot(es, nc, "scr", [128, 3, 136], F32, 2)
                m8 = Rot(es, nc, "m8", [128, 16], F32, 2)
                selm = Rot(es, nc, "selm", [128, 2, 128], BF16, 2)
                selTp = Rot(es, nc, "selTp", [128, 4, 128], BF16, 2)
                selTs = Rot(es, nc, "selTs", [128, 2, 4, 8], BF16, 2)
                onsa = Rot(es, nc, "onsa", [128, 512], F32, 2)
                onb = Rot(es, nc, "onb", [128, 512], BF16, 2)
                ost = Rot(es, nc, "ost", [128, 4, 128], BF16, 2)
                attn.pts = Rot(es, nc, "ptn", [128, 512], BF16, 4)

                def load_kT(dst_fn, src_fn, ntiles, rows_fn, rd_fn, wb, gather_idx=None):
                    kt = 0
                    while kt < ntiles:
                        nb = min(4, ntiles - kt)
                        ps, bp = PS.next()
                        tot = 0
                        for j in range(nb):
                            rows = rows_fn(kt + j)
                            kt_, bk_ = kin.next()
                            if gather_idx is None:
                                S.dma("sp", kt_[:rows, :], src_fn(kt + j), reads=rd_fn(kt + j), writes=[bk_])
                            else:
                                S.gather(kt_[:rows, :], src_fn(kt + j), gather_idx(kt + j), reads=rd_fn(kt + j), writes=[bk_])
                            S.op("pe", lambda e: e.transpose(ps[:, j * 128:j * 128 + rows], kt_[:rows, :], ident_f[:rows, :rows]),
                                 reads=[bk_, B_id], writes=[bp], inc=(j == nb - 1))
                            tot = j * 128 + rows
                        evac(kt // 4, dst_fn(kt, tot), ps[:, :tot], [bp], [wb])
                        kt += nb

                def load_v(dstA, src_fn, ntiles, rows_fn, rd_fn, wb, gather_idx=None):
                    for kt in range(ntiles):
                        rows = rows_fn(kt)
                        kt_, bk_ = kin.next()
                        if gather_idx is None:
                            S.dma("sp", kt_[:rows, :], src_fn(kt), reads=rd_fn(kt), writes=[bk_])
                        else:
                            S.gather(kt_[:rows, :], src_fn(kt), gather_idx(kt), reads=rd_fn(kt), writes=[bk_])
                        S.op("pool", lambda e: e.tensor_copy(out=dstA[:rows, kt, :, 0:64], in_=kt_[:rows, :].rearrange("p (g d) -> p g d", g=2)),
                             reads=[bk_], writes=[wb])

                def compress(nblk, srcK, srcV, rdb):
                    taken, ks = PS.take(1)
                    (psK, bpK) = taken[0]
                    nch = (nblk + 127) // 128
                    for kv, src in enumerate((srcK, srcV)):
                        for g in range(2):
                            ps, bp = PS.next()
                            for r in range(32):
                                S.op("pe", lambda e: e.matmul(ps[:, :nblk], lhsT=W1[kv][64 * g:64 * g + 64, r, :],
                                                              rhs=src[64 * g:64 * g + 64, r:r + 16 * (nblk - 1) + 1:16], start=(r == 0), stop=(r == 31)),
                                     reads=[B_W1] + rdb, writes=[bp], inc=(r == 31))
                            ge, bge = gel.next()
                            S.op("act", lambda e: e.activation(out=ge[:, :nblk], in_=ps[:, :nblk], func=AF.Gelu, bias=c1[:, kv:kv + 1], scale=1.0),
                                 reads=[bp, B_c1], writes=[bge])
                            if kv == 0:
                                S.op("pe", lambda e: e.matmul(psK[:, :nblk], lhsT=W2kz[:, g, :], rhs=ge[:, :nblk], start=(g == 0), stop=(g == 1)),
                                     reads=[bge, B_W2], writes=[bpK])
                            else:
                                ps2, bp2 = PS.next()
                                for ch in range(nch):
                                    ncr = min(128, nblk - ch * 128)
                                    S.op("pe", lambda e: e.matmul(ps2[:ncr, ch * 64:ch * 64 + 64], lhsT=ge[:, ch * 128:ch * 128 + ncr], rhs=W2v[:, :],
                                                                  start=True, stop=True), reads=[bge, B_W2], writes=[bp2], inc=(ch == nch - 1))
                                for ch in range(nch):
                                    ncr = min(128, nblk - ch * 128)
                                    evac(ch, vcc[:ncr, ch, g, :], ps2[:ncr, ch * 64:ch * 64 + 64], [bp2], [B_cc])
                        if kv == 0:
                            evac(0, kccT[:, :nblk], psK[:, :nblk], [bpK], [B_cc])
                    PS.give(ks)

                def qtile(n, rows, c0, seq, sel_kts, win_kts, nvis, dst0):
                    samp = seq is not None
                    ti = NT if samp else n
                    xt, bx = xtr.next()
                    S.dma("sp", xt[:, :, :rows], xT_d[:, :, c0:c0 + rows], reads=[B_xT[min(n, NT)]], writes=[bx])
                    ps, bp = PS.next()
                    for kc in range(8):
                        S.op("pe", lambda e: e.matmul(ps[:rows, :512], lhsT=xt[:, kc, :rows], rhs=wq[:, kc, :], start=(kc == 0), stop=(kc == 7)),
                             reads=[bx, bwq], writes=[bp], inc=(kc == 7))
                    z, bz = zq.next()
                    S.op("act", lambda e: e.copy(out=z[:rows, :], in_=ps[:rows, :512]), reads=[bp], writes=[bz])
                    ps, bp = PS.next()
                    for kc in range(8):
                        S.op("pe", lambda e: e.matmul(ps[:rows, :24], lhsT=xt[:, kc, :rows], rhs=wg[:, kc, :], start=(kc == 0), stop=(kc == 7)),
                             reads=[bx, bwg], writes=[bp], inc=(kc == 7))
                    gt, bg = gts.next()
                    S.op("act", lambda e: e.activation(out=gt[:rows, :], in_=ps[:rows, :24], func=AF.Sigmoid), reads=[bp], writes=[bg])
                    qr, bqr = qrr.next()
                    tt, bt = rtmp.next()
                    zin = z[:rows, :].rearrange("p (h s d) -> p h s d", h=8, s=2)
                    qo = qr[:rows, :].rearrange("p (h s d) -> p h s d", h=8, s=2)
                    cb = cosN[:rows, ti, :].unsqueeze(1).broadcast_to([rows, 8, 32])
                    sb_ = sinN[:rows, ti, :].unsqueeze(1).broadcast_to([rows, 8, 32])
                    t4 = [tt[:rows, j * 256:(j + 1) * 256].rearrange("p (h d) -> p h d", h=8) for j in range(4)]
                    rope6(rows, zin[:, :, 0, :], zin[:, :, 1, :], qo[:, :, 0, :], qo[:, :, 1, :], cb, sb_, t4, [bz] + B_tab, bt, bqr)
                    qb, bqb = qbs.next()
                    for v, (src, bsrc) in enumerate(((z, bz), (qr, bqr))):
                        S.op("pool", lambda e: e.tensor_copy(out=qb[:rows, v].rearrange("t p (g d) -> t g p d", g=2),
                                                             in_=src[:rows, :].rearrange("t (g p d) -> t g p d", g=2, p=4)), reads=[bsrc], writes=[bqb])
                    ps, bp = PS.next()
                    psb = ps[:].bitcast(BF16)
                    for v in range(2):
                        for p in range(4):
                            j = v * 4 + p
                            S.op("pe", lambda e: e.transpose(psb[:, j * 128:j * 128 + rows], qb[:rows, v, p, :], ident_b[:rows, :rows]),
                                 reads=[bqb, B_id], writes=[bp], inc=(j == 7))
                    if samp:
                        qT, bqT = qTs.next()
                    else:
                        qT, bqT = qTp.next()
                    evac(0, qT[:, :, :, :rows], psb[:, :].rearrange("p (v q c) -> p v q c", v=2, q=4)[:, :, :, :rows], [bp], [bqT])
                    on, bon = onsa.next()
                    S.op("pool", lambda e: e.memset(ppad[:rows, :, :], 0.0), writes=[B_pp])
                    nch = (nvis + 127) // 128
                    for g in range(2):
                        for p in range(4):
                            h = 4 * g + p
                            if nvis == 0:
                                S.op("pool", lambda e: e.memset(on[:rows, h * 64:(h + 1) * 64], 0.0), writes=[bon])
                                continue
                            ps, bp = PS.next()
                            S.op("pe", lambda e: e.matmul(ps[:rows, :nvis], lhsT=qT[64 * g:64 * g + 64, 0, p, :rows], rhs=kccT[64 * g:64 * g + 64, :nvis],
                                                          start=True, stop=True), reads=[bqT, B_cc], writes=[bp])
                            pf, bpf = pfr.next()
                            rs, brs = sml.next()
                            if not samp:
                                sm, bsm = smr.next()
                                off = 248 - 8 * n
                                S.op("dve", lambda e: e.tensor_tensor(out=sm[:rows, :nvis], in0=ps[:rows, :nvis], in1=M0[:rows, off:off + nvis], op=ALU.add),
                                     reads=[bp, B_M0], writes=[bsm])
                                S.op("act", lambda e: e.activation(out=pf[:rows, :nvis], in_=sm[:rows, :nvis], func=AF.Exp, scale=0.125, accum_out=rs[:rows, 0:1]),
                                     reads=[bsm], writes=[bpf, brs])
                            else:
                                S.op("act", lambda e: e.activation(out=pf[:rows, :nvis], in_=ps[:rows, :nvis], func=AF.Exp, scale=0.125, accum_out=rs[:rows, 0:1]),
                                     reads=[bp], writes=[bpf, brs])
                            S.op("dve", lambda e: e.tensor_scalar(out=rs[:rows, 1:2], in0=rs[:rows, 0:1], scalar1=1e-30, scalar2=None, op0=ALU.max), reads=[brs], writes=[brs])
                            S.op("dve", lambda e: e.reciprocal(out=rs[:rows, 2:3], in_=rs[:rows, 1:2]), reads=[brs], writes=[brs])
                            pn, bpn = pnr.next()
                            S.op("dve", lambda e: e.tensor_scalar(out=pn[:rows, :nvis], in0=pf[:rows, :nvis], scalar1=rs[:rows, 2:3], scalar2=None, op0=ALU.mult),
                                 reads=[bpf, brs], writes=[bpn])
                            if p == 0:
                                S.op("pool", lambda e: e.tensor_scalar(out=ppad[:rows, g, 1:1 + nvis], in0=pf[:rows, :nvis], scalar1=rs[:rows, 2:3], scalar2=None, op0=ALU.mult),
                                     reads=[bpf, brs], writes=[B_pp])
                            else:
                                S.op("dve", lambda e: e.scalar_tensor_tensor(out=ppad[:rows, g, 1:1 + nvis], in0=pf[:rows, :nvis], scalar=rs[:rows, 2:3],
                                                                             in1=ppad[:rows, g, 1:1 + nvis], op0=ALU.mult, op1=ALU.add),
                                     reads=[bpf, brs], writes=[B_pp])
                            ps2, bp2 = PS.next()
                            ps2b = ps2[:].bitcast(BF16)
                            for ch in range(nch):
                                ncr = min(128, nvis - ch * 128)
                                S.op("pe", lambda e: e.transpose(ps2b[:ncr, ch * 128:ch * 128 + rows], pn[:rows, ch * 128:ch * 128 + ncr], ident_b[:rows, :rows]),
                                     reads=[bpn, B_id], writes=[bp2], inc=(ch == nch - 1))
                            pT, bpT = pnT.next()
                            for ch in range(nch):
                                ncr = min(128, nvis - ch * 128)
                                evac(ch, pT[:ncr, ch, :rows], ps2b[:ncr, ch * 128:ch * 128 + rows], [bp2], [bpT])
                            ps3, bp3 = PS.next()
                            for ch in range(nch):
                                ncr = min(128, nvis - ch * 128)
                                S.op("pe", lambda e: e.matmul(ps3[:rows, 0:64], lhsT=pT[:ncr, ch, :rows], rhs=vcc[:ncr, ch, g, :], start=(ch == 0), stop=(ch == nch - 1)),
                                     reads=[bpT, B_cc], writes=[bp3], inc=(ch == nch - 1))
                            S.op("dve", lambda e: e.tensor_scalar(out=on[:rows, h * 64:(h + 1) * 64], in0=ps3[:rows, 0:64], scalar1=gt[:rows, h * 3:h * 3 + 1], scalar2=None, op0=ALU.mult),
                                 reads=[bp3, bg], writes=[bon])
                    nsel = 129 if samp else 64
                    nselm = 128 if samp else 64
                    sm_, bsm_ = selm.next()
                    for g in range(2):
                        sc, bsc = scr.next()
                        sl = [ppad[:rows, g, i:i + 4 * (nsel - 1) + 1:4] for i in range(5)]
                        S.op("dve", lambda e: e.tensor_tensor(out=sc[:rows, 0, :nsel], in0=sl[0], in1=sl[1], op=ALU.add), reads=[B_pp], writes=[bsc])
                        S.op("pool", lambda e: e.tensor_tensor(out=sc[:rows, 1, :nsel], in0=sl[2], in1=sl[3], op=ALU.add), reads=[B_pp], writes=[bsc])
                        S.op("dve", lambda e: e.tensor_tensor(out=sc[:rows, 0, :nsel], in0=sc[:rows, 0, :nsel], in1=sl[4], op=ALU.add), reads=[B_pp, bsc], writes=[bsc])
                        S.op("dve", lambda e: e.tensor_tensor(out=sc[:rows, 0, :nsel], in0=sc[:rows, 0, :nsel], in1=sc[:rows, 1, :nsel], op=ALU.add), reads=[bsc], writes=[bsc])
                        ftab = Fs[:rows, :nsel] if samp else Ft[:rows, n, :]
                        S.op("dve", lambda e: e.tensor_tensor(out=sc[:rows, 0, :nsel], in0=sc[:rows, 0, :nsel], in1=ftab, op=ALU.add), reads=[bsc, B_Fs, B_Ft], writes=[bsc])
                        mm_, bmm = m8.next()
                        S.op("dve", lambda e: e.max(out=mm_[:rows, 0:8], in_=sc[:rows, 0, :nsel]), reads=[bsc], writes=[bmm])
                        S.op("dve", lambda e: e.match_replace(out=sc[:rows, 2, :nsel], in_to_replace=mm_[:rows, 0:8], in_values=sc[:rows, 0, :nsel], imm_value=-3e9),
                             reads=[bsc, bmm], writes=[bsc])
                        S.op("dve", lambda e: e.max(out=mm_[:rows, 8:16], in_=sc[:rows, 2, :nsel]), reads=[bsc], writes=[bmm])
                        smo = sm_[:rows, g, :nselm] if samp else sm_[:rows, :, :].rearrange("p g j -> p (g j)")[:, g * 64:g * 64 + 64]
                        S.op("dve", lambda e: e.tensor_scalar(out=smo, in0=sc[:rows, 0, :nselm], scalar1=mm_[:rows, 15:16], scalar2=1.0,
                                                              op0=ALU.is_ge, op1=ALU.subtract), reads=[bsc, bmm], writes=[bsm_])
                    ps, bp = PS.next()
                    psb = ps[:].bitcast(BF16)
                    if not samp:
                        S.op("pe", lambda e: e.transpose(psb[:, 0:rows], sm_[:rows, :, :].rearrange("p g j -> p (g j)")[:, 0:128], ident_b[:rows, :rows]), reads=[bsm_, B_id], writes=[bp])
                        sT, bsT = selTp.next()
                        for p in range(4):
                            evac(p, sT[:, p, :rows], psb[:, 0:rows], [bp], [bsT])
                    else:
                        for g in range(2):
                            S.op("pe", lambda e: e.transpose(psb[:, g * 8:g * 8 + rows], sm_[:rows, g, :], ident_b[:rows, :rows]), reads=[bsm_, B_id], writes=[bp], inc=(g == 1))
                        sT, bsT = selTs.next()
                        for p in range(4):
                            evac(p, sT[:, :, p, :], psb[:, 0:16].rearrange("p (g t) -> p g t", g=2), [bp], [bsT])
                    ncols = 4 * rows
                    for br, kts_desc in ((1, sel_kts), (2, win_kts)):
                        taken, ks = PS.take(2)
                        units = []
                        for g in range(2):
                            Q = qT[64 * g:64 * g + 64, 1, :, :rows].rearrange("p q c -> p (q c)")
                            units.append(dict(Q=Q, ncols=ncols, scale=0.125, nblk=4, acc=taken[g], rdQ=[bqT, bsT, B_E, B_Cm4, B_Bu4, B_CsN, B_WsN, B_id]))
                        kts = []
                        for kd in kts_desc:
                            nk = kd["nk"]
                            c_ = kd["col"]
                            arr = kd["arr"]
                            K = [arr[64 * g:64 * g + 64, c_:c_ + nk] for g in range(2)]
                            V = [kd["VA"][:nk, kd["vt"], g, :] for g in range(2)]
                            masks = []
                            for g in range(2):
                                ml = []
                                for mk in kd["masks"]:
                                    if mk[0] == "E":
                                        ml.append((Em[64 * g:64 * g + 64, mk[1], :nk], sT[64 * g:64 * g + 64, :, :rows].rearrange("p q c -> p (q c)")))
                                    elif mk[0] == "Es":
                                        ml.append((Em[:, mk[1], :nk], sT[:, g, :, :].rearrange("p q c -> p (q c)")))
                                    elif mk[0] == "C":
                                        ml.append((ident_b[:nk, :nk], Cm4[:nk, :, :].rearrange("p q c -> p (q c)")))
                                    elif mk[0] == "B":
                                        ml.append((ident_b[:nk, :nk], Bu4[:nk, :, :].rearrange("p q c -> p (q c)")))
                                    elif mk[0] == "Cs":
                                        ml.append((ident_b[:nk, :nk], CsN[:nk, :, :].rearrange("p q c -> p (q c)")))
                                    elif mk[0] == "Ws":
                                        ml.append((ident_b[:nk, :nk], WsN[:nk, :, :].rearrange("p q c -> p (q c)")))
                                masks.append(ml)
                            kts.append(dict(nk=nk, K=K, V=V, masks=masks, rd=kd["rd"]))
                        attn(units, kts, rows)
                        for g in range(2):
                            acc, bacc = taken[g]
                            for p in range(4):
                                h = 4 * g + p
                                rs, brs = sml.next()
                                S.op("dve", lambda e: e.tensor_scalar(out=rs[:rows, 0:1], in0=acc[:rows, p * 65 + 64:p * 65 + 65], scalar1=1e-30, scalar2=None, op0=ALU.max),
                                     reads=[bacc], writes=[brs])
                                S.op("dve", lambda e: e.reciprocal(out=rs[:rows, 1:2], in_=rs[:rows, 0:1]), reads=[brs], writes=[brs])
                                S.op("dve", lambda e: e.tensor_tensor(out=rs[:rows, 2:3], in0=rs[:rows, 1:2], in1=gt[:rows, h * 3 + br:h * 3 + br + 1], op=ALU.mult),
                                     reads=[brs, bg], writes=[brs])
                                S.op("dve", lambda e: e.scalar_tensor_tensor(out=on[:rows, h * 64:(h + 1) * 64], in0=acc[:rows, p * 65:p * 65 + 64], scalar=rs[:rows, 2:3],
                                                                             in1=on[:rows, h * 64:(h + 1) * 64], op0=ALU.mult, op1=ALU.add),
                                     reads=[bacc, brs, bon], writes=[bon])
                        PS.give(ks)
                    ob, bob = onb.next()
                    S.op("pool", lambda e: e.tensor_copy(out=ob[:rows, :], in_=on[:rows, :]), reads=[bon], writes=[bob])
                    ps, bp = PS.next()
                    psb = ps[:].bitcast(BF16)
                    for j in range(4):
                        S.op("pe", lambda e: e.transpose(psb[:, j * 128:j * 128 + rows], ob[:rows, j * 128:(j + 1) * 128], ident_b[:rows, :rows]),
                             reads=[bob, B_id], writes=[bp], inc=(j == 3))
                    os_, bos = ost.next()
                    evac(1, os_[:, :, :rows], psb[:, 0:512].rearrange("p (j c) -> p j c", j=4)[:, :, :rows], [bp], [bos])
                    S.dma("sp", mixT_d[:, 2:6, dst0:dst0 + rows], os_[:, :, :rows], reads=[bos], writes=[B_mix[1]])

                full = lambda kt: 128
                srcs = ((o_p["sel_k"][l], 0), (wk_d, 1), (o_p["cmp_k"][l], 2), (o_p["cmp_v"][l], 3))
                for src, a in srcs:
                    load_kT(lambda kt0, tot, a=a: KT4[:, a, kt0 * 128:kt0 * 128 + tot], lambda kt, src=src: src[kt * 128:(kt + 1) * 128, :],
                            NT, full, lambda kt: [B_kvd[kt]], B_K[a])
                load_v(svA, lambda kt: o_p["sel_v"][l, kt * 128:(kt + 1) * 128, :], NT, full, lambda kt: [B_kvd[kt]], B_V["s"])
                load_v(wvA, lambda kt: wv_d[kt * 128:(kt + 1) * 128, :], NT, full, lambda kt: [B_kvd[kt]], B_V["w"])
                compress(255, KT4[:, 2, :], KT4[:, 3, :], [B_K[2], B_K[3]])
                for n in range(NT):
                    sel_kts = []
                    for kt in range(n + 1):
                        mk = [("E", kt)] + ([("C",)] if kt == n else [])
                        sel_kts.append(dict(nk=128, arr=KT4[:, 0, :], col=kt * 128, VA=svA, vt=kt, masks=mk, rd=[B_K[0], B_V["s"]]))
                    win_kts = []
                    for kt in range(max(0, n - 4), n + 1):
                        mk = []
                        if kt == n - 4:
                            mk.append(("B",))
                        if kt == n:
                            mk.append(("C",))
                        win_kts.append(dict(nk=128, arr=KT4[:, 1, :], col=kt * 128, VA=wvA, vt=kt, masks=mk, rd=[B_K[1], B_V["w"]]))
                    nvis = min(255, max(0, 8 * n + 7))
                    qtile(n, 128, n * 128, None, sel_kts, win_kts, nvis, n * 128)
                S.barrier()
                S.dma("pool", Em[:, :, :], P.din["Es"], writes=[B_E])
                ptb = sbt(es, "ptb", [128, NSB, NPG], I32)
                idx = sbt(es, "idx", [128, NSB, NPG], I32)
                iot = sbt(es, "iot", [128, 1], I32)
                B_idx = Buf("idx")
                S.dma("sp", ptb[:].rearrange("p b j -> p (b j)"), pt.rearrange("b j -> (b j)").unsqueeze(0).broadcast_to([128, NSB * NPG]), writes=[B_idx])
                S.op("pool", lambda e: e.iota(iot[:], pattern=[[0, 1]], base=l * NPOOL * 128, channel_multiplier=1), writes=[B_idx])
                S.op("dve", lambda e: e.tensor_scalar(out=idx[:].rearrange("p b j -> p (b j)"), in0=ptb[:].rearrange("p b j -> p (b j)"), scalar1=128.0,
                                                      scalar2=iot[:, 0:1], op0=ALU.mult, op1=ALU.add), reads=[B_idx], writes=[B_idx])
                for b in range(NSB):
                    gi = lambda kt, b=b: idx[:, b, kt:kt + 1]
                    load_kT(lambda kt0, tot: KT[:, 0, kt0 * 128:kt0 * 128 + tot], lambda kt: c_cmp_k, NPG, full, lambda kt: [B_idx], B_K["a0"], gather_idx=gi)
                    load_kT(lambda kt0, tot: KT[:, 1, kt0 * 128:kt0 * 128 + tot], lambda kt: c_cmp_v, NPG, full, lambda kt: [B_idx], B_K["a1"], gather_idx=gi)
                    compress(511, KT[:, 0, :], KT[:, 1, :], [B_K["a0"], B_K["a1"]])
                    S.barrier()
                    load_kT(lambda kt0, tot: KT[:, 0, kt0 * 128:kt0 * 128 + tot], lambda kt: c_sel_k, NPG, full, lambda kt: [B_idx], B_K["b0"], gather_idx=gi)
                    load_kT(lambda kt0, tot: KT[:, 0, PAST:PAST + 8], lambda kt: o_s["sel_k"][l, b], 1, lambda kt: 8, lambda kt: [B_kvd[NT + b]], B_K["b0"])
                    load_kT(lambda kt0, tot: KT[:, 1, kt0 * 128:kt0 * 128 + tot], lambda kt: st_win_k[l, b, kt * 128:(kt + 1) * 128, :], 4, full, lambda kt: [], B_K["b1"])
                    load_kT(lambda kt0, tot: KT[:, 1, 512:520], lambda kt: o_s["win_k"][l, b, 504:512, :], 1, lambda kt: 8, lambda kt: [B_kvd[NT + b]], B_K["b1"])
                    load_v(svA, lambda kt: c_sel_v, NPG, full, lambda kt: [B_idx], B_V["s"], gather_idx=gi)
                    kt_, bk_ = kin.next()
                    S.dma("sp", kt_[:8, :], o_s["sel_v"][l, b], reads=[B_kvd[NT + b]], writes=[bk_])
                    S.op("pool", lambda e: e.tensor_copy(out=svA[:8, 64, :, 0:64], in_=kt_[:8, :].rearrange("p (g d) -> p g d", g=2)), reads=[bk_], writes=[B_V["s"]])
                    load_v(wvA, lambda kt: st_win_v[l, b, kt * 128:(kt + 1) * 128, :], 4, full, lambda kt: [], B_V["w"])
                    kt_, bk_ = kin.next()
                    S.dma("sp", kt_[:8, :], o_s["win_v"][l, b, 504:512, :], reads=[B_kvd[NT + b]], writes=[bk_])
                    S.op("pool", lambda e: e.tensor_copy(out=wvA[:8, 4, :, 0:64], in_=kt_[:8, :].rearrange("p (g d) -> p g d", g=2)), reads=[bk_], writes=[B_V["w"]])
                    sel_kts = [dict(nk=128, arr=KT[:, 0, :], col=kt * 128, VA=svA, vt=kt, masks=[("Es", kt)], rd=[B_K["b0"], B_V["s"]]) for kt in range(NPG)]
                    sel_kts.append(dict(nk=8, arr=KT[:, 0, :], col=PAST, VA=svA, vt=64, masks=[("Cs",)], rd=[B_K["b0"], B_V["s"]]))
                    win_kts = [dict(nk=128, arr=KT[:, 1, :], col=kt * 128, VA=wvA, vt=kt, masks=([("Ws",)] if kt == 0 else []), rd=[B_K["b1"], B_V["w"]]) for kt in range(4)]
                    win_kts.append(dict(nk=8, arr=KT[:, 1, :], col=512, VA=wvA, vt=4, masks=[("Cs",)], rd=[B_K["b1"], B_V["w"]]))
                    qtile(NT + b, 8, T + 8 * b, b, sel_kts, win_kts, 511, T + 8 * b)
                    S.barrier()

        def phase_diff(l):
            lam_init = 0.8 - 0.6 * math.exp(-0.3 * l)
            with ExitStack() as es:
                dkT = sbt(es, "dkT", [128, 2, 8208], BF16)
                dvA = sbt(es, "dvA", [128, 65, 4, 65], BF16)
                B_K = Buf("dkT")
                B_V = Buf("dvA")
                S.op("pool", lambda e: e.memset(dvA[:, :, :, 64:65], 1.0), writes=[B_V])
                CmD, B_CmD = cload(es, "CmD", [128, 4, 4, 128])
                CsN, B_CsN = cload(es, "CsN", [8, 4, 8])
                cosD, B_cos = cload(es, "cosD", [128, NT + 1, 16], dt=F32)
                sinD, B_sin = cload(es, "sinD", [128, NT + 1, 16], dt=F32)
                wdq, bwdq = load_w_cols(es, "wdq", l, C_DQ, 256)
                lin = sbt(es, "lin", [128, 4, 32], F32)
                lt = sbt(es, "lt", [128, 2, 32], F32)
                lam = sbt(es, "lam", [128, 8], F32)
                sgl = sbt(es, "sgl", [128, 64], F32)
                B_l = Buf("lam")
                for i, nm in enumerate(("diff_lq1", "diff_lk1", "diff_lq2", "diff_lk2")):
                    S.dma("sp", lin[:, i, :], P.din[nm][l:l + 1, :].broadcast_to([128, 32]), writes=[B_l])
                S.dma("sp", sgl[:, :], P.din["diff_subln_g"][l:l + 1, :].broadcast_to([128, 64]), writes=[B_l])
                for j in range(2):
                    S.op("dve", lambda e: e.tensor_tensor(out=lt[:, j, :], in0=lin[:, 2 * j, :], in1=lin[:, 2 * j + 1, :], op=ALU.mult), reads=[B_l], writes=[B_l])
                    S.op("dve", lambda e: e.tensor_reduce(out=lam[:, j:j + 1], in_=lt[:, j, :], axis=mybir.AxisListType.X, op=ALU.add), reads=[B_l], writes=[B_l])
                S.op("act", lambda e: e.activation(out=lam[:, 2:4], in_=lam[:, 0:2], func=AF.Exp), reads=[B_l], writes=[B_l])
                S.op("dve", lambda e: e.tensor_tensor(out=lam[:, 4:5], in0=lam[:, 2:3], in1=lam[:, 3:4], op=ALU.subtract), reads=[B_l], writes=[B_l])
                S.op("dve", lambda e: e.tensor_scalar(out=lam[:, 5:6], in0=lam[:, 4:5], scalar1=lam_init, scalar2=-1.0, op0=ALU.add, op1=ALU.mult), reads=[B_l], writes=[B_l])
                S.op("dve", lambda e: e.tensor_scalar(out=sgl[:, :], in0=sgl[:, :], scalar1=1.0 - lam_init, scalar2=None, op0=ALU.mult), reads=[B_l], writes=[B_l])
                kin = Rot(es, nc, "dkin", [128, 256], F32, 4)
                xtr = Rot(es, nc, "xtd", [128, 8, 128], BF16, 2)
                zq = Rot(es, nc, "zdq", [128, 256], F32, 2)
                qrr = Rot(es, nc, "dqr", [128, 256], F32, 2)
                rtmp = Rot(es, nc, "dtmp", [128, 512], F32, 2)
                qds = Rot(es, nc, "qd", [128, 2, 256], BF16, 2)
                QTp = Rot(es, nc, "QTp", [128, 2, 2, 4, 128], BF16, 2)
                QTs = Rot(es, nc, "QTs", [128, 2, 2, 1, 8], BF16, 2)
                odr = Rot(es, nc, "od", [128, 4, 256], F32, 2)
                odb = Rot(es, nc, "odb", [128, 256], BF16, 2)
                ost = Rot(es, nc, "dost", [128, 2, 128], BF16, 2)
                sml = Rot(es, nc, "dsml", [128, 8], F32, 8)
                junk = Rot(es, nc, "djunk", [128, 64], F32, 2)
                attn.pts = Rot(es, nc, "ptd", [128, 512], BF16, 4)

                def load_k(ntiles, src_fn, rows_fn, rd_fn, col_fn, gather_idx=None):
                    for kt in range(ntiles):
                        rows = rows_fn(kt)
                        kt_, bk_ = kin.next()
                        if gather_idx is None:
                            S.dma("sp", kt_[:rows, :], src_fn(kt), reads=rd_fn(kt), writes=[bk_])
                        else:
                            S.gather(kt_[:rows, :], src_fn(kt), gather_idx(kt), reads=rd_fn(kt), writes=[bk_])
                        ps, bp = PS.next()
                        for hb in range(2):
                            S.op("pe", lambda e: e.transpose(ps[:, hb * 128:hb * 128 + rows], kt_[:rows, hb * 128:(hb + 1) * 128], ident_f[:rows, :rows]),
                                 reads=[bk_, B_id], writes=[bp], inc=(hb == 1))
                        c_ = col_fn(kt)
                        evac(kt, dkT[:, :, c_:c_ + rows], ps[:, 0:256].rearrange("p (h c) -> p h c", h=2)[:, :, :rows], [bp], [B_K])

                def load_v(ntiles, src_fn, rows_fn, rd_fn, vt_fn, gather_idx=None):
                    for kt in range(ntiles):
                        rows = rows_fn(kt)
                        kt_, bk_ = kin.next()
                        if gather_idx is None:
                            S.dma("sp", kt_[:rows, :], src_fn(kt), reads=rd_fn(kt), writes=[bk_])
                        else:
                            S.gather(kt_[:rows, :], src_fn(kt), gather_idx(kt), reads=rd_fn(kt), writes=[bk_])
                        S.op("pool", lambda e: e.tensor_copy(out=dvA[:rows, vt_fn(kt), :, 0:64], in_=kt_[:rows, :].rearrange("p (h d) -> p h d", h=4)),
                             reads=[bk_], writes=[B_V])

                def qchunk(tiles, rows, samp, kts_desc, dst_cols):
                    nj = len(tiles)
                    QT, bQT = (QTs.next() if samp else QTp.next())
                    for j, (n, c0) in enumerate(tiles):
                        ti = NT if samp else n
                        xt, bx = xtr.next()
                        S.dma("sp", xt[:, :, :rows], xT_d[:, :, c0:c0 + rows], reads=[B_xT[min(n, NT)]], writes=[bx])
                        ps, bp = PS.next()
                        for kc in range(8):
                            S.op("pe", lambda e: e.matmul(ps[:rows, :256], lhsT=xt[:, kc, :rows], rhs=wdq[:, kc, :], start=(kc == 0), stop=(kc == 7)),
                                 reads=[bx, bwdq], writes=[bp], inc=(kc == 7))
                        z, bz = zq.next()
                        S.op("act", lambda e: e.copy(out=z[:rows, :], in_=ps[:rows, :256]), reads=[bp], writes=[bz])
                        qr, bqr = qrr.next()
                        tt, bt = rtmp.next()
                        zin = z[:rows, :].rearrange("p (h s d) -> p h s d", h=8, s=2)
                        qo = qr[:rows, :].rearrange("p (h s d) -> p h s d", h=8, s=2)
                        cb = cosD[:rows, ti, :].unsqueeze(1).broadcast_to([rows, 8, 16])
                        sb_ = sinD[:rows, ti, :].unsqueeze(1).broadcast_to([rows, 8, 16])
                        t4 = [tt[:rows, k * 128:(k + 1) * 128].rearrange("p (h d) -> p h d", h=8) for k in range(4)]
                        rope6(rows, zin[:, :, 0, :], zin[:, :, 1, :], qo[:, :, 0, :], qo[:, :, 1, :], cb, sb_, t4, [bz, B_cos, B_sin], bt, bqr)
                        qd, bqd = qds.next()
                        S.op("pool", lambda e: e.memset(qd[:rows, :, :], 0.0), writes=[bqd])
                        for i in range(2):
                            S.op("pool", lambda e: e.tensor_copy(out=qd[:rows, i, :].rearrange("p (h s d) -> p h s d", h=4, s=2)[:, :, i, :],
                                                                 in_=qr[:rows, :].rearrange("p (h s d) -> p h s d", h=4, s=2)[:, :, i, :]), reads=[bqr], writes=[bqd])
                        ps, bp = PS.next()
                        psb = ps[:].bitcast(BF16)
                        for i in range(2):
                            for hb in range(2):
                                k_ = i * 2 + hb
                                S.op("pe", lambda e: e.transpose(psb[:, k_ * 128:k_ * 128 + rows], qd[:rows, i, hb * 128:(hb + 1) * 128], ident_b[:rows, :rows]),
                                     reads=[bqd, B_id], writes=[bp], inc=(k_ == 3))
                        evac(j, QT[:, :, :, j, :rows], psb[:, 0:512].rearrange("p (i h c) -> p i h c", i=2, h=2)[:, :, :, :rows], [bp], [bQT])
                    od, bod = odr.next()
                    ncols = nj * rows
                    for hb in range(2):
                        taken, ks = PS.take(4)
                        units = []
                        for hh in range(2):
                            for i in range(2):
                                Q = QT[64 * hh:64 * hh + 64, i, hb, :, :rows].rearrange("p j c -> p (j c)")
                                units.append(dict(Q=Q, ncols=ncols, scale=32 ** -0.5, nblk=nj, acc=taken[hh * 2 + i], rdQ=[bQT, B_CmD, B_CsN, B_id]))
                        kts = []
                        for kd in kts_desc:
                            nk, c_ = kd["nk"], kd["col"]
                            K, V, masks = [], [], []
                            for hh in range(2):
                                for i in range(2):
                                    K.append(dkT[64 * hh:64 * hh + 64, hb, c_:c_ + nk])
                                    V.append(dvA[:nk, kd["vt"], hb * 2 + hh, :])
                                    if kd["mask"] is None:
                                        masks.append([])
                                    elif kd["mask"][0] == "D":
                                        masks.append([(ident_b[:nk, :nk], CmD[:nk, kd["mask"][1], :, :].rearrange("p j c -> p (j c)"))])
                                    else:
                                        masks.append([(ident_b[:nk, :nk], CsN[:nk, 0, :])])
                            kts.append(dict(nk=nk, K=K, V=V, masks=masks, rd=[B_K, B_V]))
                        attn(units, kts, rows)
                        for hh in range(2):
                            h = hb * 2 + hh
                            a0, b0 = taken[hh * 2]
                            a1, b1 = taken[hh * 2 + 1]
                            for j in range(nj):
                                rs, brs = sml.next()
                                S.op("dve", lambda e: e.tensor_scalar(out=rs[:rows, 0:1], in0=a0[:rows, j * 65 + 64:j * 65 + 65], scalar1=1e-30, scalar2=None, op0=ALU.max), reads=[b0], writes=[brs])
                                S.op("dve", lambda e: e.tensor_scalar(out=rs[:rows, 1:2], in0=a1[:rows, j * 65 + 64:j * 65 + 65], scalar1=1e-30, scalar2=None, op0=ALU.max), reads=[b1], writes=[brs])
                                S.op("dve", lambda e: e.reciprocal(out=rs[:rows, 2:4], in_=rs[:rows, 0:2]), reads=[brs], writes=[brs])
                                S.op("dve", lambda e: e.tensor_tensor(out=rs[:rows, 3:4], in0=rs[:rows, 3:4], in1=lam[:rows, 5:6], op=ALU.mult), reads=[brs, B_l], writes=[brs])
                                o_ = od[:rows, j, h * 64:(h + 1) * 64]
                                S.op("dve", lambda e: e.tensor_scalar(out=o_, in0=a0[:rows, j * 65:j * 65 + 64], scalar1=rs[:rows, 2:3], scalar2=None, op0=ALU.mult), reads=[b0, brs], writes=[bod])
                                S.op("dve", lambda e: e.scalar_tensor_tensor(out=o_, in0=a1[:rows, j * 65:j * 65 + 64], scalar=rs[:rows, 3:4], in1=o_, op0=ALU.mult, op1=ALU.add),
                                     reads=[b1, brs, bod], writes=[bod])
                                jk, bjk = junk.next()
                                S.op("act", lambda e: e.activation(out=jk[:rows, :], in_=o_, func=AF.Square, accum_out=rs[:rows, 4:5]), reads=[bod], writes=[bjk, brs])
                                S.op("dve", lambda e: e.tensor_scalar(out=rs[:rows, 5:6], in0=rs[:rows, 4:5], scalar1=1.0 / 64.0, scalar2=EPS, op0=ALU.mult, op1=ALU.add), reads=[brs], writes=[brs])
                                S.op("act", lambda e: e.activation(out=rs[:rows, 6:7], in_=rs[:rows, 5:6], func=AF.Sqrt), reads=[brs], writes=[brs])
                                S.op("dve", lambda e: e.reciprocal(out=rs[:rows, 7:8], in_=rs[:rows, 6:7]), reads=[brs], writes=[brs])
                                S.op("dve", lambda e: e.scalar_tensor_tensor(out=o_, in0=o_, scalar=rs[:rows, 7:8], in1=sgl[:rows, :], op0=ALU.mult, op1=ALU.mult),
                                     reads=[bod, brs, B_l], writes=[bod])
                        PS.give(ks)
                    for j in range(nj):
                        ob, bob = odb.next()
                        S.op("pool", lambda e: e.tensor_copy(out=ob[:rows, :], in_=od[:rows, j, :]), reads=[bod], writes=[bob])
                        ps, bp = PS.next()
                        psb = ps[:].bitcast(BF16)
                        for k_ in range(2):
                            S.op("pe", lambda e: e.transpose(psb[:, k_ * 128:k_ * 128 + rows], ob[:rows, k_ * 128:(k_ + 1) * 128], ident_b[:rows, :rows]),
                                 reads=[bob, B_id], writes=[bp], inc=(k_ == 1))
                        os_, bos = ost.next()
                        evac(j, os_[:, :, :rows], psb[:, 0:256].rearrange("p (k c) -> p k c", k=2)[:, :, :rows], [bp], [bos])
                        S.dma("sp", mixT_d[:, 6:8, dst_cols[j]:dst_cols[j] + rows], os_[:, :, :rows], reads=[bos], writes=[B_mix[2]])

                full = lambda kt: 128
                load_k(NT, lambda kt: o_p["diff_k"][l, kt * 128:(kt + 1) * 128, :], full, lambda kt: [B_kvd[kt]], lambda kt: kt * 128)
                load_v(NT, lambda kt: o_p["diff_v"][l, kt * 128:(kt + 1) * 128, :], full, lambda kt: [B_kvd[kt]], lambda kt: kt)
                for qc in range(NT // 4):
                    kts_desc = [dict(nk=128, col=kt * 128, vt=kt, mask=(("D", kt - 4 * qc) if kt >= 4 * qc else None)) for kt in range(4 * qc + 4)]
                    qchunk([(4 * qc + j, (4 * qc + j) * 128) for j in range(4)], 128, False, kts_desc, [(4 * qc + j) * 128 for j in range(4)])
                S.barrier()
                ptb = sbt(es, "dptb", [128, NSB, NPG], I32)
                idx = sbt(es, "didx", [128, NSB, NPG], I32)
                iot = sbt(es, "diot", [128, 1], I32)
                B_idx = Buf("didx")
                S.dma("sp", ptb[:].rearrange("p b j -> p (b j)"), pt.rearrange("b j -> (b j)").unsqueeze(0).broadcast_to([128, NSB * NPG]), writes=[B_idx])
                S.op("pool", lambda e: e.iota(iot[:], pattern=[[0, 1]], base=l * NPOOL * 128, channel_multiplier=1), writes=[B_idx])
                S.op("dve", lambda e: e.tensor_scalar(out=idx[:].rearrange("p b j -> p (b j)"), in0=ptb[:].rearrange("p b j -> p (b j)"), scalar1=128.0,
                                                      scalar2=iot[:, 0:1], op0=ALU.mult, op1=ALU.add), reads=[B_idx], writes=[B_idx])
                for b in range(NSB):
                    gi = lambda kt, b=b: idx[:, b, kt:kt + 1]
                    load_k(NPG, lambda kt: c_diff_k, full, lambda kt: [B_idx], lambda kt: kt * 128, gather_idx=gi)
                    load_k(1, lambda kt: o_s["diff_k"][l, b], lambda kt: 8, lambda kt: [B_kvd[NT + b]], lambda kt: PAST)
                    load_v(NPG, lambda kt: c_diff_v, full, lambda kt: [B_idx], lambda kt: kt, gather_idx=gi)
                    load_v(1, lambda kt: o_s["diff_v"][l, b], lambda kt: 8, lambda kt: [B_kvd[NT + b]], lambda kt: 64)
                    kts_desc = [dict(nk=128, col=kt * 128, vt=kt, mask=None) for kt in range(NPG)]
                    kts_desc.append(dict(nk=8, col=PAST, vt=64, mask=("S",)))
                    qchunk([(NT + b, T + 8 * b)], 8, True, kts_desc, [T + 8 * b])
                    S.barrier()

        def phase_c(l):
            with ExitStack() as es:
                wo = sbt(es, "wo", [128, 8, D], BF16)
                W1 = sbt(es, "fW1", [128, 8, DFF], BF16)
                W3 = sbt(es, "fW3", [128, 8, DFF], BF16)
                W2 = sbt(es, "fW2", [128, 22, D], BF16)
                B_w = Buf("fw")
                src = P.din["w_out"][l].rearrange("(kc p) c -> p kc c", p=128)
                for kc in range(8):
                    S.dma("pool", wo[:, kc, :], src[:, kc, :], writes=[B_w])
                for wt, nm in ((W1, "ffn_w1"), (W3, "ffn_w3")):
                    src = P.din[nm][l].rearrange("(kc p) c -> p kc c", p=128)
                    for kc in range(8):
                        for hf in range(2):
                            S.dma("pool", wt[:, kc, hf * 1408:(hf + 1) * 1408], src[:, kc, hf * 1408:(hf + 1) * 1408], writes=[B_w])
                src = P.din["ffn_w2"][l].rearrange("(fc p) c -> p fc c", p=128)
                for fc in range(22):
                    S.dma("pool", W2[:, fc, :], src[:, fc, :], writes=[B_w])
                lnp = sbt(es, "lnp", [128, 4, D], F32)
                B_ln = Buf("lnp")
                for i, nm in enumerate(("ln1_g", "ln1_b", "ln2_g", "ln2_b")):
                    S.dma("sp", lnp[:, i, :], P.din[nm][l:l + 1, :].broadcast_to([128, D]), writes=[B_ln])
                mxr = Rot(es, nc, "mx", [128, 8, 128], BF16, 2)
                xin = Rot(es, nc, "cx", [128, D], F32, 2)
                xar = Rot(es, nc, "cxa", [128, D], F32, 2)
                xnr = Rot(es, nc, "cxn", [128, D], F32, 2)
                xnT = Rot(es, nc, "cxnT", [128, 8, 128], BF16, 1)
                hTr = Rot(es, nc, "chT", [128, 22, 128], BF16, 1)
                sgr = Rot(es, nc, "csg", [128, 512], F32, 2)
                str_ = Rot(es, nc, "cstat", [128, 24], F32, 4)

                def layer_norm(rows, src_, bsrc, dst, bdst, gi):
                    st, bst = str_.next()
                    for c in range(2):
                        S.op("dve", lambda e: e.bn_stats(out=st[:rows, c * 6:(c + 1) * 6], in_=src_[:rows, c * 512:(c + 1) * 512]), reads=[bsrc], writes=[bst])
                    S.op("dve", lambda e: e.bn_aggr(out=st[:rows, 12:14], in_=st[:rows, 0:12]), reads=[bst], writes=[bst])
                    S.op("dve", lambda e: e.tensor_scalar(out=st[:rows, 14:15], in0=st[:rows, 13:14], scalar1=EPS, scalar2=None, op0=ALU.add), reads=[bst], writes=[bst])
                    S.op("act", lambda e: e.activation(out=st[:rows, 15:16], in_=st[:rows, 14:15], func=AF.Sqrt), reads=[bst], writes=[bst])
                    S.op("dve", lambda e: e.reciprocal(out=st[:rows, 16:17], in_=st[:rows, 15:16]), reads=[bst], writes=[bst])
                    S.op("dve", lambda e: e.tensor_scalar(out=dst[:rows, :], in0=src_[:rows, :], scalar1=st[:rows, 12:13], scalar2=st[:rows, 16:17], op0=ALU.subtract, op1=ALU.mult),
                         reads=[bsrc, bst], writes=[bdst])
                    S.op("pool", lambda e: e.tensor_tensor(out=dst[:rows, :], in0=dst[:rows, :], in1=lnp[:rows, gi, :], op=ALU.mult), reads=[bdst, B_ln], writes=[bdst])
                    S.op("pool", lambda e: e.tensor_tensor(out=dst[:rows, :], in0=dst[:rows, :], in1=lnp[:rows, gi + 1, :], op=ALU.add), reads=[bdst, B_ln], writes=[bdst])

                for n in range(NT + 1):
                    rows = 128 if n < NT else NS
                    t0 = n * 128
                    mx, bmx = mxr.next()
                    S.dma("sp", mx[:, :, :rows], mixT_d[:, :, t0:t0 + rows], reads=[B_mix[0], B_mix[1], B_mix[2]], writes=[bmx])
                    x_, bx = xin.next()
                    if l == 0:
                        S.dma("sp", x_[:rows, :], (xp[t0:t0 + rows, :] if n < NT else xs), writes=[bx])
                    else:
                        S.dma("sp", x_[:rows, :], xmid[t0:t0 + rows, :], reads=[B_xmid[n]], writes=[bx])
                    xa, bxa = xar.next()
                    for hf in range(2):
                        ps, bp = PS.next()
                        for kc in range(8):
                            S.op("pe", lambda e: e.matmul(ps[:rows, :], lhsT=mx[:, kc, :rows], rhs=wo[:, kc, hf * 512:(hf + 1) * 512], start=(kc == 0), stop=(kc == 7)),
                                 reads=[bmx, B_w], writes=[bp], inc=(kc == 7))
                        S.op("dve", lambda e: e.scalar_tensor_tensor(out=xa[:rows, hf * 512:(hf + 1) * 512], in0=x_[:rows, hf * 512:(hf + 1) * 512], scalar=DN_ALPHA,
                                                                     in1=ps[:rows, :], op0=ALU.mult, op1=ALU.add), reads=[bx, bp], writes=[bxa])
                    xn, bxn = xnr.next()
                    layer_norm(rows, xa, bxa, xn, bxn, 0)
                    xT_, bxT = xnT.next()
                    for hf in range(2):
                        ps, bp = PS.next()
                        for j in range(4):
                            kc = hf * 4 + j
                            S.op("pe", lambda e: e.transpose(ps[:, j * 128:j * 128 + rows], xn[:rows, kc * 128:(kc + 1) * 128], ident_f[:rows, :rows]),
                                 reads=[bxn, B_id], writes=[bp], inc=(j == 3))
                        evac(hf, xT_[:, hf * 4:hf * 4 + 4, :rows], ps[:].rearrange("p (j c) -> p j c", j=4)[:, :, :rows], [bp], [bxT])
                    hT, bhT = hTr.next()
                    for f0 in range(0, 22, 4):
                        nf = min(4, 22 - f0)
                        ps1, bp1 = PS.next()
                        ps3, bp3 = PS.next()
                        for (ps_, bp_, wt) in ((ps1, bp1, W1), (ps3, bp3, W3)):
                            for f in range(nf):
                                for kc in range(8):
                                    S.op("pe", lambda e: e.matmul(ps_[:, f * 128:f * 128 + rows], lhsT=wt[:, kc, (f0 + f) * 128:(f0 + f + 1) * 128], rhs=xT_[:, kc, :rows],
                                                                  start=(kc == 0), stop=(kc == 7)), reads=[bxT, B_w], writes=[bp_], inc=(kc == 7 and f == nf - 1))
                        sg, bsg = sgr.next()
                        v1 = ps1[:, 0:nf * 128].rearrange("p (f c) -> p f c", f=nf)[:, :, :rows]
                        v3 = ps3[:, 0:nf * 128].rearrange("p (f c) -> p f c", f=nf)[:, :, :rows]
                        sv = sg[:, 0:nf * 128].rearrange("p (f c) -> p f c", f=nf)[:, :, :rows]
                        S.op("act", lambda e: e.activation(out=sv, in_=v1, func=AF.Silu), reads=[bp1], writes=[bsg])
                        S.op("dve", lambda e: e.tensor_tensor(out=hT[:, f0:f0 + nf, :rows], in0=sv, in1=v3, op=ALU.mult), reads=[bsg, bp3], writes=[bhT])
                    xb, bxb = xar.next()
                    for hf in range(2):
                        ps, bp = PS.next()
                        for fc in range(22):
                            S.op("pe", lambda e: e.matmul(ps[:rows, :], lhsT=hT[:, fc, :rows], rhs=W2[:, fc, hf * 512:(hf + 1) * 512], start=(fc == 0), stop=(fc == 21)),
                                 reads=[bhT, B_w], writes=[bp], inc=(fc == 21))
                        S.op("dve", lambda e: e.scalar_tensor_tensor(out=xb[:rows, hf * 512:(hf + 1) * 512], in0=xn[:rows, hf * 512:(hf + 1) * 512], scalar=DN_ALPHA,
                                                                     in1=ps[:rows, :], op0=ALU.mult, op1=ALU.add), reads=[bxn, bp], writes=[bxb])
                    y_, by = xin.next()
                    layer_norm(rows, xb, bxb, y_, by, 2)
                    if l == DEPTH - 1:
                        P.store((y_p[t0:t0 + rows, :] if n < NT else y_s), y_[:rows, :], [by])
                    else:
                        S.dma("sp", xmid[t0:t0 + rows, :], y_[:rows, :], reads=[by], writes=[B_xmid[n]])

        for l in range(DEPTH):
            phase_xT(l)
            S.barrier()
            phase_kv_outputs(l)
            S.barrier()
            phase_conv(l)
            S.barrier()
            phase_nsa(l)
            S.barrier()
            phase_diff(l)
            S.barrier()
            phase_c(l)
            S.barrier()
        S.barrier()
    S.close()
    return P


_PROG = None


def kernel(**inputs):
    global _PROG
    if _PROG is None:
        _PROG = build_program()
    P = _PROG
    f32 = np.float32
    consts = _consts()
    g = lambda k: np.asarray(inputs[k])
    cnames = (("c_cmp_k", "cache_nsa_cmp_k", 128), ("c_cmp_v", "cache_nsa_cmp_v", 128), ("c_sel_k", "cache_nsa_sel_k", 128),
              ("c_sel_v", "cache_nsa_sel_v", 128), ("c_diff_k", "cache_diff_k", 256), ("c_diff_v", "cache_diff_v", 256))
    shared = {}
    if not KDEV:
        for dn, sn, w_ in cnames:
            shared[dn] = g(sn).reshape(DEPTH * NPOOL * 128, w_)
    for nm in P.din:
        if nm in inputs and nm not in shared:
            shared[nm] = g(nm)
    shared.update(consts)
    in_maps = []
    for c in range(8):
        m = dict(shared)
        m["xp"] = g("x_prompt")[c % 4]
        sl = slice(4 * c, 4 * c + 4)
        m["xs"] = g("x_sample")[sl].reshape(NS, D)
        m["st_win_k"] = np.ascontiguousarray(g("state_nsa_win_k")[:, sl].reshape(DEPTH, NSB, 512, 128))
        m["st_win_v"] = np.ascontiguousarray(g("state_nsa_win_v")[:, sl].reshape(DEPTH, NSB, 512, 128))
        m["st_conv"] = np.ascontiguousarray(g("state_conv")[:, sl])
        ptc = np.ascontiguousarray(g("page_table")[sl]).astype(np.int32)
        if KDEV:
            flat = ptc.reshape(-1)
            for dn, sn, w_ in cnames:
                m[dn] = np.ascontiguousarray(g(sn)[:, flat]).reshape(DEPTH * NPOOL * 128, w_)
            ptc = np.arange(NSB * NPG, dtype=np.int32).reshape(NSB, NPG)
        m["pt"] = ptc
        in_maps.append({k: m[k] for k in P.din})
    res = run_bass_kernel_spmd(P.nc, in_maps, core_ids=list(range(8))).results

    def pgather(name, shape):
        return np.stack([res[c][name] for c in range(4)], axis=1).reshape(shape)

    def sgather(name, shape):
        return np.concatenate([res[c][name] for c in range(8)], axis=1).reshape(shape)

    y_prompt = np.stack([res[c]["y_p"] for c in range(4)], axis=0)
    y_sample = np.concatenate([res[c]["y_s"].reshape(NSB, 8, D) for c in range(8)], axis=0)
    outs = [y_prompt, y_sample]
    for nm, shp in (("cmp_k", (DEPTH, 4, T, 2, 64)), ("cmp_v", (DEPTH, 4, T, 2, 64)), ("sel_k", (DEPTH, 4, T, 2, 64)),
                    ("sel_v", (DEPTH, 4, T, 2, 64)), ("diff_k", (DEPTH, 4, T, 4, 64)), ("diff_v", (DEPTH, 4, T, 4, 64)),
                    ("win_k", (DEPTH, 4, 512, 2, 64)), ("win_v", (DEPTH, 4, 512, 2, 64)), ("conv", (DEPTH, 4, 30, 256))):
        outs.append(pgather("p_" + nm, shp))
    for nm, shp in (("cmp_k", (DEPTH, 32, 8, 2, 64)), ("cmp_v", (DEPTH, 32, 8, 2, 64)), ("sel_k", (DEPTH, 32, 8, 2, 64)),
                    ("sel_v", (DEPTH, 32, 8, 2, 64)), ("diff_k", (DEPTH, 32, 8, 4, 64)), ("diff_v", (DEPTH, 32, 8, 4, 64)),
                    ("win_k", (DEPTH, 32, 512, 2, 64)), ("win_v", (DEPTH, 32, 512, 2, 64)), ("conv", (DEPTH, 32, 30, 256))):
        outs.append(sgather("s_" + nm, shp))
    return tuple(np.ascontiguousarray(o, dtype=f32) for o in outs)
```

```python
import math
from contextlib import ExitStack

import numpy as np
import ml_dtypes
import concourse.bass as bass
import concourse.mybir as mybir
from concourse.bass_utils import run_bass_kernel_spmd

F32 = mybir.dt.float32
BF16 = mybir.dt.bfloat16
I32 = mybir.dt.int32
AF = mybir.ActivationFunctionType
ALU = mybir.AluOpType

D = 1024
T = 4096
NT = 32
NSB = 4
NS = 32
XC = T + NS
DEPTH = 2
import os
KDEV = os.environ.get('KDEV', '') == '1'
NPOOL = 256 if KDEV else 2560
PAST = 8192
NPG = 64
NIN = 2584
DFF = 2816
NEG = -30000.0
EPS = 1e-5
DN_ALPHA = (2 * DEPTH) ** 0.25
THETA = 10000.0


class Buf:
    __slots__ = ("w", "r", "name")

    def __init__(self, name=""):
        self.w = None
        self.r = {}
        self.name = name


class Bufs(dict):
    def __init__(self, name):
        super().__init__()
        self.name = name

    def __missing__(self, k):
        b = Buf(f"{self.name}{k}")
        self[k] = b
        return b


class Sched:
    def __init__(self, nc, ndma=8):
        self.nc = nc
        self.eng = {"pe": nc.tensor, "act": nc.scalar, "dve": nc.vector, "pool": nc.gpsimd, "sp": nc.sync}
        self.sem, self.cnt = {}, {}
        self.waited = {k: {} for k in self.eng}
        self._ctx = []
        for k in ("pe", "act", "dve", "pool"):
            cm = nc.semaphore("s_" + k)
            self.sem[k] = cm.__enter__()
            self._ctx.append(cm)
            self.cnt[k] = 0
        self.dq = {}
        for q in ("sp", "pool"):
            ring = []
            for i in range(ndma):
                cm = nc.semaphore(f"d_{q}{i}")
                ring.append(cm.__enter__())
                self._ctx.append(cm)
            self.dq[q] = dict(ring=ring, n=0)

    def close(self):
        for cm in reversed(self._ctx):
            cm.__exit__(None, None, None)

    def _wait(self, e, tok):
        sem, val, key = tok
        w = self.waited[e]
        if w.get(key, 0) >= val:
            return
        self.eng[e].wait_ge(sem, val)
        w[key] = val

    @staticmethod
    def _flat(bs):
        out = []
        for b in bs:
            if isinstance(b, (list, tuple)):
                out.extend(Sched._flat(b))
            else:
                out.append(b)
        return out

    def _deps(self, e, reads, writes):
        reads, writes = self._flat(reads), self._flat(writes)
        for b in reads:
            t = b.w
            if t is not None and not (e == "pe" and t[2] == "pe"):
                self._wait(e, t)
        for b in writes:
            t = b.w
            if t is not None and not (e == "pe" and t[2] == "pe"):
                self._wait(e, t)
            for t in b.r.values():
                if not (e == "pe" and t[2] == "pe"):
                    self._wait(e, t)

    def _commit(self, tok, reads, writes):
        reads, writes = self._flat(reads), self._flat(writes)
        for b in reads:
            o = b.r.get(tok[2])
            if o is None or o[1] < tok[1]:
                b.r[tok[2]] = tok
        for b in writes:
            b.w = tok
            b.r = {}

    def op(self, e, fn, reads=(), writes=(), inc=True):
        self._deps(e, reads, writes)
        ins = fn(self.eng[e])
        if inc:
            self.cnt[e] += 1
            ins.then_inc(self.sem[e], 1)
            tok = (self.sem[e], self.cnt[e], e)
        else:
            tok = (self.sem[e], self.cnt[e] + 1, e)
        self._commit(tok, reads, writes)
        return tok

    def _dma_tok(self, q):
        d = self.dq[q]
        R = len(d["ring"])
        i = d["n"]
        slot = i % R
        sem = d["ring"][slot]
        key = f"{q}{slot}"
        if i >= R:
            self._wait(q, (sem, 16 * (i // R), key))
        return d, sem, (sem, 16 * (i // R + 1), key)

    def dma(self, q, out, in_, reads=(), writes=(), **kw):
        d, sem, tok = self._dma_tok(q)
        self._deps(q, reads, writes)
        ins = self.eng[q].dma_start(out=out, in_=in_, **kw)
        ins.then_inc(sem, 16)
        d["n"] += 1
        self._commit(tok, reads, writes)
        return tok

    def gather(self, out, src2d, idx_col, reads=(), writes=()):
        q = "pool"
        d, sem, tok = self._dma_tok(q)
        self._deps(q, reads, writes)
        ins = self.nc.gpsimd.indirect_dma_start(out=out, out_offset=None, in_=src2d,
                                                 in_offset=bass.IndirectOffsetOnAxis(ap=idx_col, axis=0))
        ins.then_inc(sem, 16)
        d["n"] += 1
        self._commit(tok, reads, writes)
        return tok

    def barrier(self):
        toks = [(self.sem[k], self.cnt[k], k) for k in ("pe", "act", "dve", "pool") if self.cnt[k] > 0]
        for q, d in self.dq.items():
            R = len(d["ring"])
            for slot in range(min(R, d["n"])):
                n_on = (d["n"] - 1 - slot) // R + 1
                toks.append((d["ring"][slot], 16 * n_on, f"{q}{slot}"))
        for e in self.eng:
            for t in toks:
                if t[2] != e:
                    self._wait(e, t)


def _rope_tab(half):
    inv = (np.float32(THETA) ** (-np.arange(half, dtype=np.float32) / np.float32(half))).astype(np.float32)
    pos = np.zeros((128, NT + 1), np.float32)
    for n in range(NT):
        pos[:, n] = 128 * n + np.arange(128)
    pos[:, NT] = PAST + (np.arange(128) % 8)
    ang = (pos[:, :, None] * inv[None, None, :]).astype(np.float32)
    return np.cos(ang).astype(np.float32), np.sin(ang).astype(np.float32)


def _consts():
    c = {}
    c["ident_f"] = np.eye(128, dtype=np.float32)
    c["cosN"], c["sinN"] = _rope_tab(32)
    c["cosD"], c["sinD"] = _rope_tab(16)
    BIG = -NEG
    k = np.arange(128)
    Ep = np.zeros((128, 32, 128), np.float32)
    for kt in range(32):
        for kk in range(128):
            j = 2 * kt + kk // 64
            Ep[j, kt, kk] = BIG
            Ep[64 + j, kt, kk] = BIG
    c["Ep"] = Ep
    Es = np.zeros((128, 64, 128), np.float32)
    for kt in range(64):
        for kk in range(128):
            Es[2 * kt + kk // 64, kt, kk] = BIG
    c["Es"] = Es
    cp = np.arange(503)
    c["M0"] = np.where(16 * (cp[None, :] - 248) + 31 <= k[:, None], 0.0, NEG).astype(np.float32)
    Ft = np.zeros((128, 32, 64), np.float32)
    jj = np.arange(64)
    for n in range(32):
        t = 128 * n + k
        cur = (t // 64)[:, None]
        forced = (jj[None, :] == 0) | (jj[None, :] == cur) | (jj[None, :] == cur - 1)
        Ft[:, n, :] = np.where(jj[None, :] > cur, -2e9, np.where(forced, 1e9, 0.0))
    c["Ftab"] = Ft
    Fs = np.zeros((128, 129), np.float32)
    Fs[:, [0, 127, 128]] = 1e9
    c["Fs"] = Fs
    q = np.arange(128)
    cm = np.where(k[:, None] <= q[None, :], 0.0, NEG).astype(np.float32)
    c["Cm4"] = np.ascontiguousarray(np.broadcast_to(cm[:, None, :], (128, 4, 128)))
    bu = np.where(k[:, None] > q[None, :], 0.0, NEG).astype(np.float32)
    c["Bu4"] = np.ascontiguousarray(np.broadcast_to(bu[:, None, :], (128, 4, 128)))
    t8 = np.arange(8)
    cs = np.where(t8[:, None] <= t8[None, :], 0.0, NEG).astype(np.float32)
    c["CsN"] = np.ascontiguousarray(np.broadcast_to(cs[:, None, :], (8, 4, 8)))
    ws = np.where(k[:, None] > t8[None, :], 0.0, NEG).astype(np.float32)
    c["WsN"] = np.ascontiguousarray(np.broadcast_to(ws[:, None, :], (128, 4, 8)))
    CmD = np.zeros((128, 4, 4, 128), np.float32)
    for r in range(4):
        for j in range(4):
            CmD[:, r, j, :] = np.where(128 * r + k[:, None] <= 128 * j + q[None, :], 0.0, NEG)
    c["CmD"] = CmD
    return c


CONST_SPECS = {
    "ident_f": ([128, 128], F32),
    "Ep": ([128, 32, 128], F32), "Es": ([128, 64, 128], F32), "M0": ([128, 503], F32), "Ftab": ([128, 32, 64], F32), "Fs": ([128, 129], F32),
    "Cm4": ([128, 4, 128], F32), "Bu4": ([128, 4, 128], F32), "CsN": ([8, 4, 8], F32), "WsN": ([128, 4, 8], F32), "CmD": ([128, 4, 4, 128], F32),
    "cosN": ([128, NT + 1, 32], F32), "sinN": ([128, NT + 1, 32], F32),
    "cosD": ([128, NT + 1, 16], F32), "sinD": ([128, NT + 1, 16], F32),
}

C_CA, C_CG, C_NQ, C_CK, C_CV, C_SK, C_SV, C_WK, C_WV, C_GT, C_DQ, C_DK, C_DV = (
    0, 256, 512, 1024, 1152, 1280, 1408, 1536, 1664, 1792, 1816, 2072, 2328)


class Prog:
    def __init__(self):
        self.nc = nc = bass.Bass("TRN2", target_bir_lowering=False)
        self.S = Sched(nc)
        self.din, self.dout = {}, {}
        self.outbufs = []

    def inp(self, name, shape, dt=F32):
        self.din[name] = self.nc.dram_tensor(name, list(shape), dt, kind="ExternalInput").ap()
        return self.din[name]

    def outp(self, name, shape, dt=F32):
        self.dout[name] = self.nc.dram_tensor(name, list(shape), dt, kind="ExternalOutput").ap()
        return self.dout[name]

    def scratch(self, name, shape, dt):
        return self.nc.dram_tensor(name, list(shape), dt, kind="Internal").ap()

    def store(self, out_ap, in_ap, reads, q="sp"):
        b = Buf("o")
        self.S.dma(q, out_ap, in_ap, reads=reads, writes=[b])
        return b


_UC = [0]


def U(name):
    _UC[0] += 1
    return f"{name}_{_UC[0]}"


class Rot:
    def __init__(self, es, nc, name, shape, dt, n):
        self.t = [es.enter_context(nc.sbuf_tensor(U(f"{name}{i}"), list(shape), dt)) for i in range(n)]
        self.b = [Buf(f"{name}{i}") for i in range(n)]
        self.i = 0

    def next(self):
        k = self.i % len(self.t)
        self.i += 1
        return self.t[k], self.b[k]


class PsumRot:
    def __init__(self, es, nc, n=8):
        self.t = [es.enter_context(nc.psum_tensor(f"psb{i}", [128, 512], F32)) for i in range(n)]
        self.b = [Buf(f"psb{i}") for i in range(n)]
        self.free = list(range(n))
        self.i = 0

    def next(self):
        k = self.free[self.i % len(self.free)]
        self.i += 1
        return self.t[k], self.b[k]

    def take(self, n):
        ks = self.free[-n:]
        self.free = self.free[:-n]
        return [(self.t[k], self.b[k]) for k in ks], ks

    def give(self, ks):
        self.free = self.free + list(ks)


def build_program():
    P = Prog()
    nc, S = P.nc, P.S
    xp = P.inp("xp", [T, D])
    xs = P.inp("xs", [NS, D])
    c_cmp_k = P.inp("c_cmp_k", [DEPTH * NPOOL * 128, 128])
    c_cmp_v = P.inp("c_cmp_v", [DEPTH * NPOOL * 128, 128])
    c_sel_k = P.inp("c_sel_k", [DEPTH * NPOOL * 128, 128])
    c_sel_v = P.inp("c_sel_v", [DEPTH * NPOOL * 128, 128])
    c_diff_k = P.inp("c_diff_k", [DEPTH * NPOOL * 128, 256])
    c_diff_v = P.inp("c_diff_v", [DEPTH * NPOOL * 128, 256])
    st_win_k = P.inp("st_win_k", [DEPTH, NSB, 512, 128])
    st_win_v = P.inp("st_win_v", [DEPTH, NSB, 512, 128])
    st_conv = P.inp("st_conv", [DEPTH, NSB, 30, 256])
    pt = P.inp("pt", [NSB, NPG], I32)
    w_in = P.inp("w_in", [DEPTH, D, NIN])

    for nm, shp in (("conv_dw_w", [DEPTH, 31, 256]), ("conv_dw_b", [DEPTH, 256]), ("conv_ln_g", [DEPTH, 256]), ("conv_ln_b", [DEPTH, 256]),
                    ("cmp_pe_k", [DEPTH, 32, 64]), ("cmp_w1_k", [DEPTH, 2048, 128]), ("cmp_b1_k", [DEPTH, 128]), ("cmp_w2_k", [DEPTH, 128, 64]),
                    ("cmp_pe_v", [DEPTH, 32, 64]), ("cmp_w1_v", [DEPTH, 2048, 128]), ("cmp_b1_v", [DEPTH, 128]), ("cmp_w2_v", [DEPTH, 128, 64]),
                    ("diff_lq1", [DEPTH, 32]), ("diff_lk1", [DEPTH, 32]), ("diff_lq2", [DEPTH, 32]), ("diff_lk2", [DEPTH, 32]),
                    ("diff_subln_g", [DEPTH, 64]), ("w_out", [DEPTH, D, D]), ("ln1_g", [DEPTH, D]), ("ln1_b", [DEPTH, D]),
                    ("ln2_g", [DEPTH, D]), ("ln2_b", [DEPTH, D]), ("ffn_w1", [DEPTH, D, DFF]), ("ffn_w3", [DEPTH, D, DFF]), ("ffn_w2", [DEPTH, DFF, D])):
        P.inp(nm, shp)
    for nm, shp in CONST_SPECS.items():
        P.inp(nm, shp[0], shp[1])

    y_p = P.outp("y_p", [T, D])
    y_s = P.outp("y_s", [NS, D])
    o_p = {}
    for nm, w_ in (("cmp_k", 128), ("cmp_v", 128), ("sel_k", 128), ("sel_v", 128), ("diff_k", 256), ("diff_v", 256)):
        o_p[nm] = P.outp("p_" + nm, [DEPTH, T, w_])
    o_p["win_k"] = P.outp("p_win_k", [DEPTH, 512, 128])
    o_p["win_v"] = P.outp("p_win_v", [DEPTH, 512, 128])
    o_p["conv"] = P.outp("p_conv", [DEPTH, 30, 256])
    o_s = {}
    for nm, w_ in (("cmp_k", 128), ("cmp_v", 128), ("sel_k", 128), ("sel_v", 128), ("diff_k", 256), ("diff_v", 256)):
        o_s[nm] = P.outp("s_" + nm, [DEPTH, NSB, 8, w_])
    o_s["win_k"] = P.outp("s_win_k", [DEPTH, NSB, 512, 128])
    o_s["win_v"] = P.outp("s_win_v", [DEPTH, NSB, 512, 128])
    o_s["conv"] = P.outp("s_conv", [DEPTH, NSB, 30, 256])

    xT_d = P.scratch("xT_d", [128, 8, XC], BF16)
    xmid = P.scratch("xmid", [XC, D], F32)
    wk_d = P.scratch("wk_d", [T, 128], F32)
    wv_d = P.scratch("wv_d", [T, 128], F32)
    u_d = P.scratch("u_d", [XC, 256], F32)
    mixT_d = P.scratch("mixT_d", [128, 8, XC], BF16)
    B_kvd = {k: [] for k in range(NT + NSB)}
    B_mix = Bufs("mix")
    B_xT = Bufs("xTd")
    B_xmid = Bufs("xmid")

    with ExitStack() as gs:
        PS = PsumRot(gs, nc)
        ident_f = gs.enter_context(nc.sbuf_tensor(U("sb_ident_f"), [128, 128], F32))
        ident_b = gs.enter_context(nc.sbuf_tensor(U("sb_ident_b"), [128, 128], BF16))
        B_id = Buf("ident")
        S.dma("sp", ident_f[:], P.din["ident_f"], writes=[B_id])
        S.op("dve", lambda e: e.tensor_copy(out=ident_b[:], in_=ident_f[:]), reads=[B_id], writes=[B_id])

        def evac(i, out, in_, reads, writes):
            if i % 2 == 0:
                return S.op("act", lambda e: e.copy(out=out, in_=in_), reads=reads, writes=writes)
            return S.op("dve", lambda e: e.tensor_copy(out=out, in_=in_), reads=reads, writes=writes)

        def phase_xT(l):
            with ExitStack() as es:
                xin = Rot(es, nc, "xin", [128, D], F32, 3)
                xts = Rot(es, nc, "xts", [128, 8, 128], BF16, 3)
                for n in range(NT + 1):
                    rows = 128 if n < NT else NS
                    t0 = n * 128
                    if l == 0:
                        src = xp[t0:t0 + rows, :] if n < NT else xs
                        rd = []
                    else:
                        src = xmid[t0:t0 + rows, :]
                        rd = [B_xmid[n]]
                    xt, bx = xin.next()
                    S.dma("sp", xt[:rows, :], src, reads=rd, writes=[bx])
                    st, bs = xts.next()
                    for hf in range(2):
                        ps, bp = PS.next()
                        for j in range(4):
                            kc = hf * 4 + j
                            S.op("pe", lambda e: e.transpose(ps[:, j * 128:j * 128 + rows], xt[:rows, kc * 128:(kc + 1) * 128],
                                                             ident_f[:rows, :rows]),
                                 reads=[bx, B_id], writes=[bp], inc=(j == 3))
                        evac(hf, st[:, hf * 4:hf * 4 + 4, :rows],
                             ps[:].rearrange("p (j c) -> p j c", j=4)[:, :, :rows], [bp], [bs])
                    S.dma("sp", xT_d[:, :, t0:t0 + rows], st[:, :, :rows], reads=[bs], writes=[B_xT[n]])

        def rope_tm(rows, out4, in4, cos, sin, nh, half, tmp):
            tt, bt = tmp
            n = nh * half
            cb = cos.unsqueeze(1).broadcast_to([rows, nh, half])
            sb_ = sin.unsqueeze(1).broadcast_to([rows, nh, half])
            x1, x2 = in4[:, :, 0, :], in4[:, :, 1, :]
            tv = [tt[:rows, k * n:(k + 1) * n].rearrange("p (h d) -> p h d", h=nh) for k in range(4)]
            return x1, x2, cb, sb_, tv

        def load_w_cols(es, name, l, c0, ncols):
            wt = es.enter_context(nc.sbuf_tensor(U(name), [128, 8, ncols], BF16))
            bw = Buf(name)
            src = w_in[l].rearrange("(kc p) c -> p kc c", p=128)
            for kc in range(8):
                S.dma("pool", wt[:, kc, :], src[:, kc, c0:c0 + ncols], writes=[bw])
            return wt, bw

        def phase_kv_outputs(l):
            with ExitStack() as es:
                wkv, bwkv = load_w_cols(es, "wkv", l, C_CK, C_GT - C_CK)
                wdf, bwdf = load_w_cols(es, "wdf", l, C_DK, 512)
                wcv, bwcv = load_w_cols(es, "wcv", l, C_CA, 512)
                cosN = es.enter_context(nc.sbuf_tensor(U("sb_cosN"), [128, NT + 1, 32], F32))
                sinN = es.enter_context(nc.sbuf_tensor(U("sb_sinN"), [128, NT + 1, 32], F32))
                cosD = es.enter_context(nc.sbuf_tensor(U("sb_cosD"), [128, NT + 1, 16], F32))
                sinD = es.enter_context(nc.sbuf_tensor(U("sb_sinD"), [128, NT + 1, 16], F32))
                B_tab = Buf("tabs")
                for t_, nm in ((cosN, "cosN"), (sinN, "sinN"), (cosD, "cosD"), (sinD, "sinD")):
                    S.dma("sp", t_[:], P.din[nm], writes=[B_tab])
                xtr = Rot(es, nc, "xtl", [128, 8, 128], BF16, 3)
                zs = Rot(es, nc, "zs", [128, 768 + 512 + 512], F32, 3)
                kr = Rot(es, nc, "kr", [128, 2 * 128 + 256], F32, 3)
                tmp = Rot(es, nc, "rtmp", [128, 4 * 256], F32, 2)

                def do_tile(n, rows, c0, seq):
                    xt, bx = xtr.next()
                    S.dma("sp", xt[:, :, :rows], xT_d[:, :, c0:c0 + rows], reads=[B_xT[min(n, NT)]], writes=[bx])
                    z, bz = zs.next()
                    pieces = [(wkv, bwkv, 0, 512, 0), (wkv, bwkv, 512, 256, 512), (wdf, bwdf, 0, 512, 768), (wcv, bwcv, 0, 512, 1280)]
                    for i, (wt, bw, wc0, wn, zc0) in enumerate(pieces):
                        ps, bp = PS.next()
                        for kc in range(8):
                            S.op("pe", lambda e: e.matmul(ps[:rows, :wn], lhsT=xt[:, kc, :rows], rhs=wt[:, kc, wc0:wc0 + wn],
                                                          start=(kc == 0), stop=(kc == 7)),
                                 reads=[bx, bw], writes=[bp], inc=(kc == 7))
                        evac(i, z[:rows, zc0:zc0 + wn], ps[:rows, :wn], [bp], [bz])
                    k, bk = kr.next()
                    tt, bt = tmp.next()
                    ti = NT if seq is not None else n
                    zin = z[:rows, 256:768].rearrange("p (a g h d) -> p a g h d", a=4, g=2, h=2)[:, 0:4:2]
                    kout = k[:rows, 0:256].rearrange("p (a g h d) -> p a g h d", a=2, g=2, h=2)
                    cb = cosN[:rows, ti, :].unsqueeze(1).broadcast_to([rows, 2, 32])
                    sb_ = sinN[:rows, ti, :].unsqueeze(1).broadcast_to([rows, 2, 32])
                    for a in range(2):
                        x1, x2 = zin[:, a, :, 0, :], zin[:, a, :, 1, :]
                        o1, o2 = kout[:, a, :, 0, :], kout[:, a, :, 1, :]
                        t = [tt[:rows, (a * 4 + j) * 64:(a * 4 + j + 1) * 64].rearrange("p (g d) -> p g d", g=2) for j in range(4)]
                        S.op("dve", lambda e: e.tensor_tensor(out=t[0], in0=x1, in1=cb, op=ALU.mult), reads=[bz, B_tab], writes=[bt])
                        S.op("pool", lambda e: e.tensor_tensor(out=t[1], in0=x2, in1=sb_, op=ALU.mult), reads=[bz, B_tab], writes=[bt])
                        S.op("dve", lambda e: e.tensor_tensor(out=t[2], in0=x2, in1=cb, op=ALU.mult), reads=[bz, B_tab], writes=[bt])
                        S.op("pool", lambda e: e.tensor_tensor(out=t[3], in0=x1, in1=sb_, op=ALU.mult), reads=[bz, B_tab], writes=[bt])
                        S.op("dve", lambda e: e.tensor_tensor(out=o1, in0=t[0], in1=t[1], op=ALU.subtract), reads=[bt], writes=[bk])
                        S.op("pool", lambda e: e.tensor_tensor(out=o2, in0=t[2], in1=t[3], op=ALU.add), reads=[bt], writes=[bk])
                    zd = z[:rows, 768:1024].rearrange("p (h s d) -> p h s d", h=8, s=2)
                    kd = k[:rows, 256:512].rearrange("p (h s d) -> p h s d", h=8, s=2)
                    cbd = cosD[:rows, ti, :].unsqueeze(1).broadcast_to([rows, 8, 16])
                    sbd = sinD[:rows, ti, :].unsqueeze(1).broadcast_to([rows, 8, 16])
                    x1, x2 = zd[:, :, 0, :], zd[:, :, 1, :]
                    t = [tt[:rows, 512 + j * 128:512 + (j + 1) * 128].rearrange("p (h d) -> p h d", h=8) for j in range(4)]
                    S.op("dve", lambda e: e.tensor_tensor(out=t[0], in0=x1, in1=cbd, op=ALU.mult), reads=[bz, B_tab], writes=[bt])
                    S.op("pool", lambda e: e.tensor_tensor(out=t[1], in0=x2, in1=sbd, op=ALU.mult), reads=[bz, B_tab], writes=[bt])
                    S.op("dve", lambda e: e.tensor_tensor(out=t[2], in0=x2, in1=cbd, op=ALU.mult), reads=[bz, B_tab], writes=[bt])
                    S.op("pool", lambda e: e.tensor_tensor(out=t[3], in0=x1, in1=sbd, op=ALU.mult), reads=[bz, B_tab], writes=[bt])
                    S.op("dve", lambda e: e.tensor_tensor(out=kd[:, :, 0, :], in0=t[0], in1=t[1], op=ALU.subtract), reads=[bt], writes=[bk])
                    S.op("pool", lambda e: e.tensor_tensor(out=kd[:, :, 1, :], in0=t[2], in1=t[3], op=ALU.add), reads=[bt], writes=[bk])
                    S.op("act", lambda e: e.activation(out=z[:rows, 1536:1792], in_=z[:rows, 1536:1792], func=AF.Sigmoid), reads=[bz], writes=[bz])
                    S.op("pool", lambda e: e.tensor_tensor(out=z[:rows, 1536:1792], in0=z[:rows, 1280:1536], in1=z[:rows, 1536:1792], op=ALU.mult),
                         reads=[bz], writes=[bz])
                    B_kvd[n].clear()

                    def st_(o_, i_, rd_):
                        B_kvd[n].append(P.store(o_, i_, rd_))
                    if seq is None:
                        r0 = n * 128
                        st_(o_p["cmp_k"][l, r0:r0 + 128, :], z[:, 0:128], [bz])
                        st_(o_p["cmp_v"][l, r0:r0 + 128, :], z[:, 128:256], [bz])
                        st_(o_p["sel_k"][l, r0:r0 + 128, :], k[:, 0:128], [bk])
                        st_(o_p["sel_v"][l, r0:r0 + 128, :], z[:, 384:512], [bz])
                        st_(o_p["diff_k"][l, r0:r0 + 128, :], k[:, 256:512], [bk])
                        st_(o_p["diff_v"][l, r0:r0 + 128, :], z[:, 1024:1280], [bz])
                        if n >= NT - 4:
                            w0 = (n - (NT - 4)) * 128
                            st_(o_p["win_k"][l, w0:w0 + 128, :], k[:, 128:256], [bk])
                            st_(o_p["win_v"][l, w0:w0 + 128, :], z[:, 640:768], [bz])
                        if n == NT - 1:
                            st_(o_p["conv"][l, :, :], z[98:128, 1536:1792], [bz])
                        st_(wk_d[r0:r0 + 128, :], k[:, 128:256], [bk])
                        st_(wv_d[r0:r0 + 128, :], z[:, 640:768], [bz])
                        st_(u_d[r0:r0 + 128, :], z[:, 1536:1792], [bz])
                    else:
                        b = seq
                        st_(o_s["cmp_k"][l, b], z[:8, 0:128], [bz])
                        st_(o_s["cmp_v"][l, b], z[:8, 128:256], [bz])
                        st_(o_s["sel_k"][l, b], k[:8, 0:128], [bk])
                        st_(o_s["sel_v"][l, b], z[:8, 384:512], [bz])
                        st_(o_s["diff_k"][l, b], k[:8, 256:512], [bk])
                        st_(o_s["diff_v"][l, b], z[:8, 1024:1280], [bz])
                        st_(o_s["win_k"][l, b, 504:512, :], k[:8, 128:256], [bk])
                        st_(o_s["win_v"][l, b, 504:512, :], z[:8, 640:768], [bz])
                        st_(o_s["conv"][l, b, 22:30, :], z[:8, 1536:1792], [bz])
                        st_(u_d[T + 8 * b:T + 8 * b + 8, :], z[:8, 1536:1792], [bz])
                        st_(o_s["win_k"][l, b, 0:504, :], st_win_k[l, b, 8:512, :], [])
                        st_(o_s["win_v"][l, b, 0:504, :], st_win_v[l, b, 8:512, :], [])
                        st_(o_s["conv"][l, b, 0:22, :], st_conv[l, b, 8:30, :], [])

                for n in range(NT):
                    do_tile(n, 128, n * 128, None)
                for b in range(NSB):
                    do_tile(NT + b, 8, T + 8 * b, b)

        def sbt(es, name, shape, dt):
            return es.enter_context(nc.sbuf_tensor(U(name), list(shape), dt))

        def cload(es, name, shape, q="pool", dt=BF16):
            t_ = sbt(es, "c_" + name, shape, dt)
            b_ = Buf(name)
            S.dma(q if dt != F32 else "sp", t_[:], P.din[name], writes=[b_])
            return t_, b_

        def rope6(rows, x1, x2, o1, o2, cb, sb_, t, rd, bt, bo):
            S.op("dve", lambda e: e.tensor_tensor(out=t[0], in0=x1, in1=cb, op=ALU.mult), reads=rd, writes=[bt])
            S.op("pool", lambda e: e.tensor_tensor(out=t[1], in0=x2, in1=sb_, op=ALU.mult), reads=rd, writes=[bt])
            S.op("dve", lambda e: e.tensor_tensor(out=t[2], in0=x2, in1=cb, op=ALU.mult), reads=rd, writes=[bt])
            S.op("pool", lambda e: e.tensor_tensor(out=t[3], in0=x1, in1=sb_, op=ALU.mult), reads=rd, writes=[bt])
            S.op("dve", lambda e: e.tensor_tensor(out=o1, in0=t[0], in1=t[1], op=ALU.subtract), reads=[bt], writes=[bo])
            S.op("pool", lambda e: e.tensor_tensor(out=o2, in0=t[2], in1=t[3], op=ALU.add), reads=[bt], writes=[bo])

        def attn(units, kts, rows, depth=2):
            pts = attn.pts
            steps = [(ki, ui) for ki in range(len(kts)) for ui in range(len(units))]
            live = {}

            def stage_a(s_):
                ki, ui = steps[s_]
                kt, u = kts[ki], units[ui]
                nk, nco = kt["nk"], u["ncols"]
                ps, bp = PS.next()
                ms = kt["masks"][ui]
                S.op("pe", lambda e: e.matmul(ps[:nk, :nco], lhsT=kt["K"][ui], rhs=u["Q"], start=True, stop=(len(ms) == 0)),
                     reads=kt["rd"] + u["rdQ"], writes=[bp], inc=(len(ms) == 0))
                for mi, (ml, mr) in enumerate(ms):
                    S.op("pe", lambda e: e.matmul(ps[:nk, :nco], lhsT=ml, rhs=mr, start=False, stop=(mi == len(ms) - 1)),
                         reads=kt["rd"] + u["rdQ"], writes=[bp], inc=(mi == len(ms) - 1))
                pt_, bpt = pts.next()
                S.op("act", lambda e: e.activation(out=pt_[:nk, :nco], in_=ps[:nk, :nco], func=AF.Exp, scale=u["scale"]),
                     reads=[bp], writes=[bpt])
                live[s_] = (pt_, bpt)

            def stage_c(s_):
                ki, ui = steps[s_]
                kt, u = kts[ki], units[ui]
                nk = kt["nk"]
                pt_, bpt = live.pop(s_)
                acc, bacc = u["acc"]
                nb = u["nblk"]
                for blk in range(nb):
                    first = (ki == 0 and blk == 0)
                    last = (ki == len(kts) - 1)
                    v_ = kt["V"][ui]
                    v_ = v_[blk] if isinstance(v_, (list, tuple)) else v_
                    S.op("pe", lambda e: e.matmul(acc[:rows, blk * 65:blk * 65 + 65], lhsT=pt_[:nk, blk * rows:(blk + 1) * rows],
                                                  rhs=v_, start=first, stop=last, skip_group_check=True),
                         reads=[bpt] + kt["rd"], writes=[bacc], inc=(blk == nb - 1))

            for s_ in range(len(steps) + depth):
                if s_ < len(steps):
                    stage_a(s_)
                if s_ - depth >= 0:
                    stage_c(s_ - depth)

        def phase_conv(l):
            with ExitStack() as es:
                uT = sbt(es, "uT", [128, 2, 30 + T], BF16)
                uTs = sbt(es, "uTs", [128, 2, NSB, 38], BF16)
                B_u = Buf("uT")
                Dg = sbt(es, "Dg", [128, 2, 31, 128], BF16)
                B_Dg = Buf("Dg")
                dwT = sbt(es, "dwT", [128, 2, 32], F32)
                dws = sbt(es, "dws", [32, 256], F32)
                prm = sbt(es, "cprm", [128, 3, 2], F32)
                B_prm = Buf("cprm")
                onesF = sbt(es, "onesF", [128, 128], F32)
                B_ones = Buf("ones")
                S.op("pool", lambda e: e.memset(onesF[:], 1.0 / 256.0), writes=[B_ones])
                S.op("pool", lambda e: e.memset(uT[:, :, 0:30], 0.0), writes=[B_u])
                for i, src in enumerate((P.din["conv_dw_b"], P.din["conv_ln_g"], P.din["conv_ln_b"])):
                    S.dma("sp", prm[:, i, :], src[l].rearrange("(h p) -> p h", p=128), writes=[B_prm], allow_slow_non_contiguous=True)
                B_dws = Buf("dws")
                S.dma("sp", dws[:31, :], P.din["conv_dw_w"][l], writes=[B_dws])
                ps, bp = PS.next()
                for h in range(2):
                    S.op("pe", lambda e: e.transpose(ps[:, h * 32:h * 32 + 31], dws[:31, h * 128:(h + 1) * 128], ident_f[:31, :31]),
                         reads=[B_dws, B_id], writes=[bp])
                B_dwT = Buf("dwT")
                S.op("act", lambda e: e.copy(out=dwT[:, :, :], in_=ps[:, 0:64].rearrange("p (h w) -> p h w", h=2)), reads=[bp], writes=[B_dwT])
                for h in range(2):
                    for w in range(31):
                        S.op("dve" if (w % 2 == 0) else "pool",
                             lambda e: e.tensor_scalar(out=Dg[:, h, w, :], in0=ident_f[:], scalar1=dwT[:, h, w:w + 1], scalar2=None, op0=ALU.mult),
                             reads=[B_dwT, B_id], writes=[B_Dg])
                uin = Rot(es, nc, "uin", [128, 256], F32, 3)
                for n in range(NT):
                    ut, bu = uin.next()
                    S.dma("sp", ut[:, :], u_d[n * 128:(n + 1) * 128, :], reads=[B_kvd[n]], writes=[bu])
                    ps, bp = PS.next()
                    for h in range(2):
                        S.op("pe", lambda e: e.transpose(ps[:, h * 128:(h + 1) * 128], ut[:, h * 128:(h + 1) * 128], ident_f[:, :]),
                             reads=[bu, B_id], writes=[bp], inc=(h == 1))
                    evac(n, uT[:, :, 30 + n * 128:30 + (n + 1) * 128], ps[:, 0:256].rearrange("p (h c) -> p h c", h=2), [bp], [B_u])
                for b in range(NSB):
                    ut, bu = uin.next()
                    S.dma("sp", ut[:30, :], st_conv[l, b], writes=[bu])
                    ps, bp = PS.next()
                    for h in range(2):
                        S.op("pe", lambda e: e.transpose(ps[:, h * 32:h * 32 + 30], ut[:30, h * 128:(h + 1) * 128], ident_f[:30, :30]),
                             reads=[bu, B_id], writes=[bp], inc=(h == 1))
                    evac(b, uTs[:, :, b, 0:30], ps[:, 0:64].rearrange("p (h c) -> p h c", h=2)[:, :, 0:30], [bp], [B_u])
                    ut, bu = uin.next()
                    S.dma("sp", ut[:8, :], u_d[T + 8 * b:T + 8 * b + 8, :], reads=[B_kvd[NT + b]], writes=[bu])
                    ps, bp = PS.next()
                    for h in range(2):
                        S.op("pe", lambda e: e.transpose(ps[:, h * 32:h * 32 + 8], ut[:8, h * 128:(h + 1) * 128], ident_f[:8, :8]),
                             reads=[bu, B_id], writes=[bp], inc=(h == 1))
                    evac(b + 1, uTs[:, :, b, 30:38], ps[:, 0:64].rearrange("p (h c) -> p h c", h=2)[:, :, 0:8], [bp], [B_u])
                yv = Rot(es, nc, "cyv", [128, 2, 512], F32, 2)
                ysq = Rot(es, nc, "cysq", [128, 2, 512], F32, 2)
                stt = Rot(es, nc, "cst", [128, 4, 512], F32, 2)
                yo = Rot(es, nc, "cyo", [128, 2, 512], BF16, 2)

                def conv_chunk(rhs_fn, N, dst0):
                    y, by = yv.next()
                    q2, bq2 = ysq.next()
                    for h in range(2):
                        ps, bp = PS.next()
                        for w in range(31):
                            S.op("pe", lambda e: e.matmul(ps[:, :N], lhsT=Dg[:, h, w, :], rhs=rhs_fn(h, w), start=(w == 0), stop=(w == 30)),
                                 reads=[B_Dg, B_u], writes=[bp], inc=(w == 30))
                        S.op("act", lambda e: e.activation(out=y[:, h, :N], in_=ps[:, :N], func=AF.Identity, bias=prm[:, 0, h:h + 1], scale=1.0),
                             reads=[bp, B_prm], writes=[by])
                        S.op("act", lambda e: e.activation(out=q2[:, h, :N], in_=ps[:, :N], func=AF.Square, bias=prm[:, 0, h:h + 1], scale=1.0),
                             reads=[bp, B_prm], writes=[bq2])
                    psm, bpm = PS.next()
                    for h in range(2):
                        S.op("pe", lambda e: e.matmul(psm[:, :N], lhsT=onesF[:], rhs=y[:, h, :N], start=(h == 0), stop=(h == 1)),
                             reads=[B_ones, by], writes=[bpm], inc=(h == 1))
                    pss, bps_ = PS.next()
                    for h in range(2):
                        S.op("pe", lambda e: e.matmul(pss[:, :N], lhsT=onesF[:], rhs=q2[:, h, :N], start=(h == 0), stop=(h == 1)),
                             reads=[B_ones, bq2], writes=[bps_], inc=(h == 1))
                    st_, bst = stt.next()
                    S.op("act", lambda e: e.copy(out=st_[:, 0, :N], in_=psm[:, :N]), reads=[bpm], writes=[bst])
                    S.op("pool", lambda e: e.tensor_tensor(out=st_[:, 1, :N], in0=st_[:, 0, :N], in1=st_[:, 0, :N], op=ALU.mult), reads=[bst], writes=[bst])
                    S.op("dve", lambda e: e.tensor_tensor(out=st_[:, 2, :N], in0=pss[:, :N], in1=st_[:, 1, :N], op=ALU.subtract), reads=[bps_, bst], writes=[bst])
                    S.op("dve", lambda e: e.tensor_scalar(out=st_[:, 2, :N], in0=st_[:, 2, :N], scalar1=0.0, scalar2=EPS, op0=ALU.max, op1=ALU.add), reads=[bst], writes=[bst])
                    S.op("act", lambda e: e.activation(out=st_[:, 3, :N], in_=st_[:, 2, :N], func=AF.Sqrt), reads=[bst], writes=[bst])
                    S.op("dve", lambda e: e.reciprocal(out=st_[:, 3, :N], in_=st_[:, 3, :N]), reads=[bst], writes=[bst])
                    o_, bo = yo.next()
                    for h in range(2):
                        S.op("dve", lambda e: e.tensor_tensor(out=y[:, h, :N], in0=y[:, h, :N], in1=st_[:, 0, :N], op=ALU.subtract), reads=[bst, by], writes=[by])
                        S.op("pool", lambda e: e.tensor_tensor(out=y[:, h, :N], in0=y[:, h, :N], in1=st_[:, 3, :N], op=ALU.mult), reads=[bst, by], writes=[by])
                        S.op("act", lambda e: e.activation(out=o_[:, h, :N], in_=y[:, h, :N], func=AF.Silu, bias=prm[:, 2, h:h + 1], scale=prm[:, 1, h:h + 1]),
                             reads=[by, B_prm], writes=[bo])
                    S.dma("sp", mixT_d[:, 0:2, dst0:dst0 + N], o_[:, :, :N], reads=[bo], writes=[B_mix[0]])

                for c in range(T // 512):
                    conv_chunk(lambda h, w, c=c: uT[:, h, c * 512 + w:c * 512 + w + 512], 512, c * 512)
                for b in range(NSB):
                    conv_chunk(lambda h, w, b=b: uTs[:, h, b, w:w + 8], 8, T + 8 * b)

        def phase_nsa(l):
            with ExitStack() as es:
                KT = sbt(es, "KT", [128, 2, 8208], BF16)
                KT4 = KT[:].rearrange("p a c -> p (a c)").rearrange("p (a c) -> p a c", a=4)
                B_K = Bufs("KT")
                svA = sbt(es, "svA", [128, 65, 2, 65], BF16)
                wvA = sbt(es, "wvA", [128, 32, 2, 65], BF16)
                B_V = Bufs("VA")
                S.op("pool", lambda e: e.memset(svA[:, :, :, 64:65], 1.0), writes=[B_V["s"]])
                S.op("pool", lambda e: e.memset(wvA[:, :, :, 64:65], 1.0), writes=[B_V["w"]])
                Em = sbt(es, "Em", [128, 64, 128], BF16)
                B_E = Buf("E")
                S.dma("pool", Em[:, 0:32, :], P.din["Ep"], writes=[B_E])
                M0, B_M0 = cload(es, "M0", [128, 503], dt=F32)
                Ft, B_Ft = cload(es, "Ftab", [128, 32, 64], dt=F32)
                Fs, B_Fs = cload(es, "Fs", [128, 129], dt=F32)
                Cm4, B_Cm4 = cload(es, "Cm4", [128, 4, 128])
                Bu4, B_Bu4 = cload(es, "Bu4", [128, 4, 128])
                CsN, B_CsN = cload(es, "CsN", [8, 4, 8])
                WsN, B_WsN = cload(es, "WsN", [128, 4, 8])
                cosN, B_cos = cload(es, "cosN", [128, NT + 1, 32], dt=F32)
                sinN, B_sin = cload(es, "sinN", [128, NT + 1, 32], dt=F32)
                B_tab = [B_cos, B_sin]
                wq, bwq = load_w_cols(es, "wq", l, C_NQ, 512)
                wg, bwg = load_w_cols(es, "wg", l, C_GT, 24)
                W1 = [sbt(es, "W1k", [128, 32, 128], BF16), sbt(es, "W1v", [128, 32, 128], BF16)]
                B_W1 = Buf("W1")
                for kv, nm in enumerate(("cmp_w1_k", "cmp_w1_v")):
                    src = P.din[nm][l].rearrange("(r d) h -> d r h", d=64)
                    for hf in range(2):
                        S.dma("pool", W1[kv][64 * hf:64 * hf + 64, :, :], src, writes=[B_W1])
                W2kz = sbt(es, "W2kz", [128, 2, 128], BF16)
                W2v = sbt(es, "W2v", [128, 64], BF16)
                B_W2 = Buf("W2")
                S.op("pool", lambda e: e.memset(W2kz[:], 0.0), writes=[B_W2])
                for g in range(2):
                    S.dma("pool", W2kz[:, g, 64 * g:64 * g + 64], P.din["cmp_w2_k"][l], writes=[B_W2])
                S.dma("pool", W2v[:, :], P.din["cmp_w2_v"][l], writes=[B_W2])
                b1t = sbt(es, "b1t", [128, 2], F32)
                c1 = sbt(es, "c1", [128, 2], F32)
                B_c1 = Buf("c1")
                pes = sbt(es, "pes", [32, 2, 64], F32)
                peT = sbt(es, "peT", [64, 2, 32], BF16)
                for kv, (nb_, npe) in enumerate((("cmp_b1_k", "cmp_pe_k"), ("cmp_b1_v", "cmp_pe_v"))):
                    S.dma("sp", b1t[:, kv:kv + 1], P.din[nb_][l].rearrange("(p o) -> p o", o=1), writes=[B_c1])
                    S.dma("sp", pes[:, kv, :], P.din[npe][l], writes=[B_c1])
                ps, bp = PS.next()
                for kv in range(2):
                    S.op("pe", lambda e: e.transpose(ps[:64, kv * 32:kv * 32 + 32], pes[:32, kv, :], ident_f[:32, :32]), reads=[B_c1, B_id], writes=[bp], inc=(kv == 1))
                S.op("act", lambda e: e.copy(out=peT[:, :, :], in_=ps[:64, 0:64].rearrange("p (k r) -> p k r", k=2)), reads=[bp], writes=[B_c1])
                for kv in range(2):
                    ps, bp = PS.next()
                    for r in range(32):
                        S.op("pe", lambda e: e.matmul(ps[:, 0:1], lhsT=W1[kv][0:64, r, :], rhs=peT[0:64, kv, r:r + 1], start=(r == 0), stop=(r == 31)),
                             reads=[B_W1, B_c1], writes=[bp], inc=(r == 31))
                    S.op("dve", lambda e: e.tensor_tensor(out=c1[:, kv:kv + 1], in0=ps[:, 0:1], in1=b1t[:, kv:kv + 1], op=ALU.add), reads=[bp, B_c1], writes=[B_c1])
                kccT = sbt(es, "kccT", [128, 512], BF16)
                vcc = sbt(es, "vcc", [128, 4, 2, 64], BF16)
                B_cc = Buf("cc")
                kin = Rot(es, nc, "kin", [128, 128], F32, 4)
                gel = Rot(es, nc, "gel", [128, 512], BF16, 2)
                xtr = Rot(es, nc, "xtq", [128, 8, 128], BF16, 2)
                zq = Rot(es, nc, "zq", [128, 512], F32, 2)
                qrr = Rot(es, nc, "qr", [128, 512], F32, 2)
                rtmp = Rot(es, nc, "qtmp", [128, 1024], F32, 2)
                gts = Rot(es, nc, "gts", [128, 24], F32, 2)
                qbs = Rot(es, nc, "qb", [128, 2, 4, 128], BF16, 2)
                qTp = Rot(es, nc, "qTp", [128, 2, 4, 128], BF16, 2)
                qTs = Rot(es, nc, "qTs", [128, 2, 4, 8], BF16, 2)
                smr = Rot(es, nc, "sm", [128, 512], F32, 2)
                pfr = Rot(es, nc, "pf", [128, 512], F32, 2)
                pnr = Rot(es, nc, "pn", [128, 512], BF16, 2)
                pnT = Rot(es, nc, "pnT", [128, 4, 128], BF16, 2)
                sml = Rot(es, nc, "sml", [128, 8], F32, 8)
                ppad = sbt(es, "ppad", [128, 2, 528], F32)
                B_pp = Buf("ppad")
                scr = Rot(es, nc, "scr", [128, 3, 136], F32, 2)
                m8 = Rot(es, nc, "m8", [128, 16], F32, 2)
                selm = Rot(es, nc, "selm", [128, 2, 128], BF16, 2)
                selTp = Rot(es, nc, "selTp", [128, 4, 128], BF16, 2)
                selTs = Rot(es, nc, "selTs", [128, 2, 4, 8], BF16, 2)
                onsa = Rot(es, nc, "onsa", [128, 512], F32, 2)
                onb = Rot(es, nc, "onb", [128, 512], BF16, 2)
                ost = Rot(es, nc, "ost", [128, 4, 128], BF16, 2)
                attn.pts = Rot(es, nc, "ptn", [128, 512], BF16, 4)

                def load_kT(dst_fn, src_fn, ntiles, rows_fn, rd_fn, wb, gather_idx=None):
                    kt = 0
                    while kt < ntiles:
                        nb = min(4, ntiles - kt)
                        ps, bp = PS.next()
                        tot = 0
                        for j in range(nb):
                            rows = rows_fn(kt + j)
                            kt_, bk_ = kin.next()
                            if gather_idx is None:
                                S.dma("sp", kt_[:rows, :], src_fn(kt + j), reads=rd_fn(kt + j), writes=[bk_])
                            else:
                                S.gather(kt_[:rows, :], src_fn(kt + j), gather_idx(kt + j), reads=rd_fn(kt + j), writes=[bk_])
                            S.op("pe", lambda e: e.transpose(ps[:, j * 128:j * 128 + rows], kt_[:rows, :], ident_f[:rows, :rows]),
                                 reads=[bk_, B_id], writes=[bp], inc=(j == nb - 1))
                            tot = j * 128 + rows
                        evac(kt // 4, dst_fn(kt, tot), ps[:, :tot], [bp], [wb])
                        kt += nb

                def load_v(dstA, src_fn, ntiles, rows_fn, rd_fn, wb, gather_idx=None):
                    for kt in range(ntiles):
                        rows = rows_fn(kt)
                        kt_, bk_ = kin.next()
                        if gather_idx is None:
                            S.dma("sp", kt_[:rows, :], src_fn(kt), reads=rd_fn(kt), writes=[bk_])
                        else:
                            S.gather(kt_[:rows, :], src_fn(kt), gather_idx(kt), reads=rd_fn(kt), writes=[bk_])
                        S.op("pool", lambda e: e.tensor_copy(out=dstA[:rows, kt, :, 0:64], in_=kt_[:rows, :].rearrange("p (g d) -> p g d", g=2)),
                             reads=[bk_], writes=[wb])

                def compress(nblk, srcK, srcV, rdb):
                    taken, ks = PS.take(1)
                    (psK, bpK) = taken[0]
                    nch = (nblk + 127) // 128
                    for kv, src in enumerate((srcK, srcV)):
                        for g in range(2):
                            ps, bp = PS.next()
                            for r in range(32):
                                S.op("pe", lambda e: e.matmul(ps[:, :nblk], lhsT=W1[kv][64 * g:64 * g + 64, r, :],
                                                              rhs=src[64 * g:64 * g + 64, r:r + 16 * (nblk - 1) + 1:16], start=(r == 0), stop=(r == 31)),
                                     reads=[B_W1] + rdb, writes=[bp], inc=(r == 31))
                            ge, bge = gel.next()
                            S.op("act", lambda e: e.activation(out=ge[:, :nblk], in_=ps[:, :nblk], func=AF.Gelu, bias=c1[:, kv:kv + 1], scale=1.0),
                                 reads=[bp, B_c1], writes=[bge])
                            if kv == 0:
                                S.op("pe", lambda e: e.matmul(psK[:, :nblk], lhsT=W2kz[:, g, :], rhs=ge[:, :nblk], start=(g == 0), stop=(g == 1)),
                                     reads=[bge, B_W2], writes=[bpK])
                            else:
                                ps2, bp2 = PS.next()
                                for ch in range(nch):
                                    ncr = min(128, nblk - ch * 128)
                                    S.op("pe", lambda e: e.matmul(ps2[:ncr, ch * 64:ch * 64 + 64], lhsT=ge[:, ch * 128:ch * 128 + ncr], rhs=W2v[:, :],
                                                                  start=True, stop=True), reads=[bge, B_W2], writes=[bp2], inc=(ch == nch - 1))
                                for ch in range(nch):
                                    ncr = min(128, nblk - ch * 128)
                                    evac(ch, vcc[:ncr, ch, g, :], ps2[:ncr, ch * 64:ch * 64 + 64], [bp2], [B_cc])
                        if kv == 0:
                            evac(0, kccT[:, :nblk], psK[:, :nblk], [bpK], [B_cc])
                    PS.give(ks)

                def qtile(n, rows, c0, seq, sel_kts, win_kts, nvis, dst0):
                    samp = seq is not None
                    ti = NT if samp else n
                    xt, bx = xtr.next()
                    S.dma("sp", xt[:, :, :rows], xT_d[:, :, c0:c0 + rows], reads=[B_xT[min(n, NT)]], writes=[bx])
                    ps, bp = PS.next()
                    for kc in range(8):
                        S.op("pe", lambda e: e.matmul(ps[:rows, :512], lhsT=xt[:, kc, :rows], rhs=wq[:, kc, :], start=(kc == 0), stop=(kc == 7)),
                             reads=[bx, bwq], writes=[bp], inc=(kc == 7))
                    z, bz = zq.next()
                    S.op("act", lambda e: e.copy(out=z[:rows, :], in_=ps[:rows, :512]), reads=[bp], writes=[bz])
                    ps, bp = PS.next()
                    for kc in range(8):
                        S.op("pe", lambda e: e.matmul(ps[:rows, :24], lhsT=xt[:, kc, :rows], rhs=wg[:, kc, :], start=(kc == 0), stop=(kc == 7)),
                             reads=[bx, bwg], writes=[bp], inc=(kc == 7))
                    gt, bg = gts.next()
                    S.op("act", lambda e: e.activation(out=gt[:rows, :], in_=ps[:rows, :24], func=AF.Sigmoid), reads=[bp], writes=[bg])
                    qr, bqr = qrr.next()
                    tt, bt = rtmp.next()
                    zin = z[:rows, :].rearrange("p (h s d) -> p h s d", h=8, s=2)
                    qo = qr[:rows, :].rearrange("p (h s d) -> p h s d", h=8, s=2)
                    cb = cosN[:rows, ti, :].unsqueeze(1).broadcast_to([rows, 8, 32])
                    sb_ = sinN[:rows, ti, :].unsqueeze(1).broadcast_to([rows, 8, 32])
                    t4 = [tt[:rows, j * 256:(j + 1) * 256].rearrange("p (h d) -> p h d", h=8) for j in range(4)]
                    rope6(rows, zin[:, :, 0, :], zin[:, :, 1, :], qo[:, :, 0, :], qo[:, :, 1, :], cb, sb_, t4, [bz] + B_tab, bt, bqr)
                    qb, bqb = qbs.next()
                    for v, (src, bsrc) in enumerate(((z, bz), (qr, bqr))):
                        S.op("pool", lambda e: e.tensor_copy(out=qb[:rows, v].rearrange("t p (g d) -> t g p d", g=2),
                                                             in_=src[:rows, :].rearrange("t (g p d) -> t g p d", g=2, p=4)), reads=[bsrc], writes=[bqb])
                    ps, bp = PS.next()
                    psb = ps[:].bitcast(BF16)
                    for v in range(2):
                        for p in range(4):
                            j = v * 4 + p
                            S.op("pe", lambda e: e.transpose(psb[:, j * 128:j * 128 + rows], qb[:rows, v, p, :], ident_b[:rows, :rows]),
                                 reads=[bqb, B_id], writes=[bp], inc=(j == 7))
                    if samp:
                        qT, bqT = qTs.next()
                    else:
                        qT, bqT = qTp.next()
                    evac(0, qT[:, :, :, :rows], psb[:, :].rearrange("p (v q c) -> p v q c", v=2, q=4)[:, :, :, :rows], [bp], [bqT])
                    on, bon = onsa.next()
                    S.op("pool", lambda e: e.memset(ppad[:rows, :, :], 0.0), writes=[B_pp])
                    nch = (nvis + 127) // 128
                    for g in range(2):
                        for p in range(4):
                            h = 4 * g + p
                            if nvis == 0:
                                S.op("pool", lambda e: e.memset(on[:rows, h * 64:(h + 1) * 64], 0.0), writes=[bon])
                                continue
                            ps, bp = PS.next()
                            S.op("pe", lambda e: e.matmul(ps[:rows, :nvis], lhsT=qT[64 * g:64 * g + 64, 0, p, :rows], rhs=kccT[64 * g:64 * g + 64, :nvis],
                                                          start=True, stop=True), reads=[bqT, B_cc], writes=[bp])
                            pf, bpf = pfr.next()
                            rs, brs = sml.next()
                            if not samp:
                                sm, bsm = smr.next()
                                off = 248 - 8 * n
                                S.op("dve", lambda e: e.tensor_tensor(out=sm[:rows, :nvis], in0=ps[:rows, :nvis], in1=M0[:rows, off:off + nvis], op=ALU.add),
                                     reads=[bp, B_M0], writes=[bsm])
                                S.op("act", lambda e: e.activation(out=pf[:rows, :nvis], in_=sm[:rows, :nvis], func=AF.Exp, scale=0.125, accum_out=rs[:rows, 0:1]),
                                     reads=[bsm], writes=[bpf, brs])
                            else:
                                S.op("act", lambda e: e.activation(out=pf[:rows, :nvis], in_=ps[:rows, :nvis], func=AF.Exp, scale=0.125, accum_out=rs[:rows, 0:1]),
                                     reads=[bp], writes=[bpf, brs])
                            S.op("dve", lambda e: e.tensor_scalar(out=rs[:rows, 1:2], in0=rs[:rows, 0:1], scalar1=1e-30, scalar2=None, op0=ALU.max), reads=[brs], writes=[brs])
                            S.op("dve", lambda e: e.reciprocal(out=rs[:rows, 2:3], in_=rs[:rows, 1:2]), reads=[brs], writes=[brs])
                            pn, bpn = pnr.next()
                            S.op("dve", lambda e: e.tensor_scalar(out=pn[:rows, :nvis], in0=pf[:rows, :nvis], scalar1=rs[:rows, 2:3], scalar2=None, op0=ALU.mult),
                                 reads=[bpf, brs], writes=[bpn])
                            if p == 0:
                                S.op("pool", lambda e: e.tensor_scalar(out=ppad[:rows, g, 1:1 + nvis], in0=pf[:rows, :nvis], scalar1=rs[:rows, 2:3], scalar2=None, op0=ALU.mult),
                                     reads=[bpf, brs], writes=[B_pp])
                            else:
                                S.op("dve", lambda e: e.scalar_tensor_tensor(out=ppad[:rows, g, 1:1 + nvis], in0=pf[:rows, :nvis], scalar=rs[:rows, 2:3],
                                                                             in1=ppad[:rows, g, 1:1 + nvis], op0=ALU.mult, op1=ALU.add),
                                     reads=[bpf, brs], writes=[B_pp])
                            ps2, bp2 = PS.next()
                            ps2b = ps2[:].bitcast(BF16)
                            for ch in range(nch):
                                ncr = min(128, nvis - ch * 128)
                                S.op("pe", lambda e: e.transpose(ps2b[:ncr, ch * 128:ch * 128 + rows], pn[:rows, ch * 128:ch * 128 + ncr], ident_b[:rows, :rows]),
                                     reads=[bpn, B_id], writes=[bp2], inc=(ch == nch - 1))
                            pT, bpT = pnT.next()
                            for ch in range(nch):
                                ncr = min(128, nvis - ch * 128)
                                evac(ch, pT[:ncr, ch, :rows], ps2b[:ncr, ch * 128:ch * 128 + rows], [bp2], [bpT])
                            ps3, bp3 = PS.next()
                            for ch in range(nch):
                                ncr = min(128, nvis - ch * 128)
                                S.op("pe", lambda e: e.matmul(ps3[:rows, 0:64], lhsT=pT[:ncr, ch, :rows], rhs=vcc[:ncr, ch, g, :], start=(ch == 0), stop=(ch == nch - 1)),
                                     reads=[bpT, B_cc], writes=[bp3], inc=(ch == nch - 1))
                            S.op("dve", lambda e: e.tensor_scalar(out=on[:rows, h * 64:(h + 1) * 64], in0=ps3[:rows, 0:64], scalar1=gt[:rows, h * 3:h * 3 + 1], scalar2=None, op0=ALU.mult),
                                 reads=[bp3, bg], writes=[bon])
                    nsel = 129 if samp else 64
                    nselm = 128 if samp else 64
                    sm_, bsm_ = selm.next()
                    for g in range(2):
                        sc, bsc = scr.next()
                        sl = [ppad[:rows, g, i:i + 4 * (nsel - 1) + 1:4] for i in range(5)]
                        S.op("dve", lambda e: e.tensor_tensor(out=sc[:rows, 0, :nsel], in0=sl[0], in1=sl[1], op=ALU.add), reads=[B_pp], writes=[bsc])
                        S.op("pool", lambda e: e.tensor_tensor(out=sc[:rows, 1, :nsel], in0=sl[2], in1=sl[3], op=ALU.add), reads=[B_pp], writes=[bsc])
                        S.op("dve", lambda e: e.tensor_tensor(out=sc[:rows, 0, :nsel], in0=sc[:rows, 0, :nsel], in1=sl[4], op=ALU.add), reads=[B_pp, bsc], writes=[bsc])
                        S.op("dve", lambda e: e.tensor_tensor(out=sc[:rows, 0, :nsel], in0=sc[:rows, 0, :nsel], in1=sc[:rows, 1, :nsel], op=ALU.add), reads=[bsc], writes=[bsc])
                        ftab = Fs[:rows, :nsel] if samp else Ft[:rows, n, :]
                        S.op("dve", lambda e: e.tensor_tensor(out=sc[:rows, 0, :nsel], in0=sc[:rows, 0, :nsel], in1=ftab, op=ALU.add), reads=[bsc, B_Fs, B_Ft], writes=[bsc])
                        mm_, bmm = m8.next()
                        S.op("dve", lambda e: e.max(out=mm_[:rows, 0:8], in_=sc[:rows, 0, :nsel]), reads=[bsc], writes=[bmm])
                        S.op("dve", lambda e: e.match_replace(out=sc[:rows, 2, :nsel], in_to_replace=mm_[:rows, 0:8], in_values=sc[:rows, 0, :nsel], imm_value=-3e9),
                             reads=[bsc, bmm], writes=[bsc])
                        S.op("dve", lambda e: e.max(out=mm_[:rows, 8:16], in_=sc[:rows, 2, :nsel]), reads=[bsc], writes=[bmm])
                        smo = sm_[:rows, g, :nselm] if samp else sm_[:rows, :, :].rearrange("p g j -> p (g j)")[:, g * 64:g * 64 + 64]
                        S.op("dve", lambda e: e.tensor_scalar(out=smo, in0=sc[:rows, 0, :nselm], scalar1=mm_[:rows, 15:16], scalar2=1.0,
                                                              op0=ALU.is_ge, op1=ALU.subtract), reads=[bsc, bmm], writes=[bsm_])
                    ps, bp = PS.next()
                    psb = ps[:].bitcast(BF16)
                    if not samp:
                        S.op("pe", lambda e: e.transpose(psb[:, 0:rows], sm_[:rows, :, :].rearrange("p g j -> p (g j)")[:, 0:128], ident_b[:rows, :rows]), reads=[bsm_, B_id], writes=[bp])
                        sT, bsT = selTp.next()
                        for p in range(4):
                            evac(p, sT[:, p, :rows], psb[:, 0:rows], [bp], [bsT])
                    else:
                        for g in range(2):
                            S.op("pe", lambda e: e.transpose(psb[:, g * 8:g * 8 + rows], sm_[:rows, g, :], ident_b[:rows, :rows]), reads=[bsm_, B_id], writes=[bp], inc=(g == 1))
                        sT, bsT = selTs.next()
                        for p in range(4):
                            evac(p, sT[:, :, p, :], psb[:, 0:16].rearrange("p (g t) -> p g t", g=2), [bp], [bsT])
                    ncols = 4 * rows
                    for br, kts_desc in ((1, sel_kts), (2, win_kts)):
                        taken, ks = PS.take(2)
                        units = []
                        for g in range(2):
                            Q = qT[64 * g:64 * g + 64, 1, :, :rows].rearrange("p q c -> p (q c)")
                            units.append(dict(Q=Q, ncols=ncols, scale=0.125, nblk=4, acc=taken[g], rdQ=[bqT, bsT, B_E, B_Cm4, B_Bu4, B_CsN, B_WsN, B_id]))
                        kts = []
                        for kd in kts_desc:
                            nk = kd["nk"]
                            c_ = kd["col"]
                            arr = kd["arr"]
                            K = [arr[64 * g:64 * g + 64, c_:c_ + nk] for g in range(2)]
                            V = [kd["VA"][:nk, kd["vt"], g, :] for g in range(2)]
                            masks = []
                            for g in range(2):
                                ml = []
                                for mk in kd["masks"]:
                                    if mk[0] == "E":
                                        ml.append((Em[64 * g:64 * g + 64, mk[1], :nk], sT[64 * g:64 * g + 64, :, :rows].rearrange("p q c -> p (q c)")))
                                    elif mk[0] == "Es":
                                        ml.append((Em[:, mk[1], :nk], sT[:, g, :, :].rearrange("p q c -> p (q c)")))
                                    elif mk[0] == "C":
                                        ml.append((ident_b[:nk, :nk], Cm4[:nk, :, :].rearrange("p q c -> p (q c)")))
                                    elif mk[0] == "B":
                                        ml.append((ident_b[:nk, :nk], Bu4[:nk, :, :].rearrange("p q c -> p (q c)")))
                                    elif mk[0] == "Cs":
                                        ml.append((ident_b[:nk, :nk], CsN[:nk, :, :].rearrange("p q c -> p (q c)")))
                                    elif mk[0] == "Ws":
                                        ml.append((ident_b[:nk, :nk], WsN[:nk, :, :].rearrange("p q c -> p (q c)")))
                                masks.append(ml)
                            kts.append(dict(nk=nk, K=K, V=V, masks=masks, rd=kd["rd"]))
                        attn(units, kts, rows)
                        for g in range(2):
                            acc, bacc = taken[g]
                            for p in range(4):
                                h = 4 * g + p
                                rs, brs = sml.next()
                                S.op("dve", lambda e: e.tensor_scalar(out=rs[:rows, 0:1], in0=acc[:rows, p * 65 + 64:p * 65 + 65], scalar1=1e-30, scalar2=None, op0=ALU.max),
                                     reads=[bacc], writes=[brs])
                                S.op("dve", lambda e: e.reciprocal(out=rs[:rows, 1:2], in_=rs[:rows, 0:1]), reads=[brs], writes=[brs])
                                S.op("dve", lambda e: e.tensor_tensor(out=rs[:rows, 2:3], in0=rs[:rows, 1:2], in1=gt[:rows, h * 3 + br:h * 3 + br + 1], op=ALU.mult),
                                     reads=[brs, bg], writes=[brs])
                                S.op("dve", lambda e: e.scalar_tensor_tensor(out=on[:rows, h * 64:(h + 1) * 64], in0=acc[:rows, p * 65:p * 65 + 64], scalar=rs[:rows, 2:3],
                                                                             in1=on[:rows, h * 64:(h + 1) * 64], op0=ALU.mult, op1=ALU.add),
                                     reads=[bacc, brs, bon], writes=[bon])
                        PS.give(ks)
                    ob, bob = onb.next()
                    S.op("pool", lambda e: e.tensor_copy(out=ob[:rows, :], in_=on[:rows, :]), reads=[bon], writes=[bob])
                    ps, bp = PS.next()
                    psb = ps[:].bitcast(BF16)
                    for j in range(4):
                        S.op("pe", lambda e: e.transpose(psb[:, j * 128:j * 128 + rows], ob[:rows, j * 128:(j + 1) * 128], ident_b[:rows, :rows]),
                             reads=[bob, B_id], writes=[bp], inc=(j == 3))
                    os_, bos = ost.next()
                    evac(1, os_[:, :, :rows], psb[:, 0:512].rearrange("p (j c) -> p j c", j=4)[:, :, :rows], [bp], [bos])
                    S.dma("sp", mixT_d[:, 2:6, dst0:dst0 + rows], os_[:, :, :rows], reads=[bos], writes=[B_mix[1]])

                full = lambda kt: 128
                srcs = ((o_p["sel_k"][l], 0), (wk_d, 1), (o_p["cmp_k"][l], 2), (o_p["cmp_v"][l], 3))
                for src, a in srcs:
                    load_kT(lambda kt0, tot, a=a: KT4[:, a, kt0 * 128:kt0 * 128 + tot], lambda kt, src=src: src[kt * 128:(kt + 1) * 128, :],
                            NT, full, lambda kt: [B_kvd[kt]], B_K[a])
                load_v(svA, lambda kt: o_p["sel_v"][l, kt * 128:(kt + 1) * 128, :], NT, full, lambda kt: [B_kvd[kt]], B_V["s"])
                load_v(wvA, lambda kt: wv_d[kt * 128:(kt + 1) * 128, :], NT, full, lambda kt: [B_kvd[kt]], B_V["w"])
                compress(255, KT4[:, 2, :], KT4[:, 3, :], [B_K[2], B_K[3]])
                for n in range(NT):
                    sel_kts = []
                    for kt in range(n + 1):
                        mk = [("E", kt)] + ([("C",)] if kt == n else [])
                        sel_kts.append(dict(nk=128, arr=KT4[:, 0, :], col=kt * 128, VA=svA, vt=kt, masks=mk, rd=[B_K[0], B_V["s"]]))
                    win_kts = []
                    for kt in range(max(0, n - 4), n + 1):
                        mk = []
                        if kt == n - 4:
                            mk.append(("B",))
                        if kt == n:
                            mk.append(("C",))
                        win_kts.append(dict(nk=128, arr=KT4[:, 1, :], col=kt * 128, VA=wvA, vt=kt, masks=mk, rd=[B_K[1], B_V["w"]]))
                    nvis = min(255, max(0, 8 * n + 7))
                    qtile(n, 128, n * 128, None, sel_kts, win_kts, nvis, n * 128)
                S.barrier()
                S.dma("pool", Em[:, :, :], P.din["Es"], writes=[B_E])
                ptb = sbt(es, "ptb", [128, NSB, NPG], I32)
                idx = sbt(es, "idx", [128, NSB, NPG], I32)
                iot = sbt(es, "iot", [128, 1], I32)
                B_idx = Buf("idx")
                S.dma("sp", ptb[:].rearrange("p b j -> p (b j)"), pt.rearrange("b j -> (b j)").unsqueeze(0).broadcast_to([128, NSB * NPG]), writes=[B_idx])
                S.op("pool", lambda e: e.iota(iot[:], pattern=[[0, 1]], base=l * NPOOL * 128, channel_multiplier=1), writes=[B_idx])
                S.op("dve", lambda e: e.tensor_scalar(out=idx[:].rearrange("p b j -> p (b j)"), in0=ptb[:].rearrange("p b j -> p (b j)"), scalar1=128.0,
                                                      scalar2=iot[:, 0:1], op0=ALU.mult, op1=ALU.add), reads=[B_idx], writes=[B_idx])
                for b in range(NSB):
                    gi = lambda kt, b=b: idx[:, b, kt:kt + 1]
                    load_kT(lambda kt0, tot: KT[:, 0, kt0 * 128:kt0 * 128 + tot], lambda kt: c_cmp_k, NPG, full, lambda kt: [B_idx], B_K["a0"], gather_idx=gi)
                    load_kT(lambda kt0, tot: KT[:, 1, kt0 * 128:kt0 * 128 + tot], lambda kt: c_cmp_v, NPG, full, lambda kt: [B_idx], B_K["a1"], gather_idx=gi)
                    compress(511, KT[:, 0, :], KT[:, 1, :], [B_K["a0"], B_K["a1"]])
                    S.barrier()
                    load_kT(lambda kt0, tot: KT[:, 0, kt0 * 128:kt0 * 128 + tot], lambda kt: c_sel_k, NPG, full, lambda kt: [B_idx], B_K["b0"], gather_idx=gi)
                    load_kT(lambda kt0, tot: KT[:, 0, PAST:PAST + 8], lambda kt: o_s["sel_k"][l, b], 1, lambda kt: 8, lambda kt: [B_kvd[NT + b]], B_K["b0"])
                    load_kT(lambda kt0, tot: KT[:, 1, kt0 * 128:kt0 * 128 + tot], lambda kt: st_win_k[l, b, kt * 128:(kt + 1) * 128, :], 4, full, lambda kt: [], B_K["b1"])
                    load_kT(lambda kt0, tot: KT[:, 1, 512:520], lambda kt: o_s["win_k"][l, b, 504:512, :], 1, lambda kt: 8, lambda kt: [B_kvd[NT + b]], B_K["b1"])
                    load_v(svA, lambda kt: c_sel_v, NPG, full, lambda kt: [B_idx], B_V["s"], gather_idx=gi)
                    kt_, bk_ = kin.next()
                    S.dma("sp", kt_[:8, :], o_s["sel_v"][l, b], reads=[B_kvd[NT + b]], writes=[bk_])
                    S.op("pool", lambda e: e.tensor_copy(out=svA[:8, 64, :, 0:64], in_=kt_[:8, :].rearrange("p (g d) -> p g d", g=2)), reads=[bk_], writes=[B_V["s"]])
                    load_v(wvA, lambda kt: st_win_v[l, b, kt * 128:(kt + 1) * 128, :], 4, full, lambda kt: [], B_V["w"])
                    kt_, bk_ = kin.next()
                    S.dma("sp", kt_[:8, :], o_s["win_v"][l, b, 504:512, :], reads=[B_kvd[NT + b]], writes=[bk_])
                    S.op("pool", lambda e: e.tensor_copy(out=wvA[:8, 4, :, 0:64], in_=kt_[:8, :].rearrange("p (g d) -> p g d", g=2)), reads=[bk_], writes=[B_V["w"]])
                    sel_kts = [dict(nk=128, arr=KT[:, 0, :], col=kt * 128, VA=svA, vt=kt, masks=[("Es", kt)], rd=[B_K["b0"], B_V["s"]]) for kt in range(NPG)]
                    sel_kts.append(dict(nk=8, arr=KT[:, 0, :], col=PAST, VA=svA, vt=64, masks=[("Cs",)], rd=[B_K["b0"], B_V["s"]]))
                    win_kts = [dict(nk=128, arr=KT[:, 1, :], col=kt * 128, VA=wvA, vt=kt, masks=([("Ws",)] if kt == 0 else []), rd=[B_K["b1"], B_V["w"]]) for kt in range(4)]
                    win_kts.append(dict(nk=8, arr=KT[:, 1, :], col=512, VA=wvA, vt=4, masks=[("Cs",)], rd=[B_K["b1"], B_V["w"]]))
                    qtile(NT + b, 8, T + 8 * b, b, sel_kts, win_kts, 511, T + 8 * b)
                    S.barrier()

        def phase_diff(l):
            lam_init = 0.8 - 0.6 * math.exp(-0.3 * l)
            with ExitStack() as es:
                dkT = sbt(es, "dkT", [128, 2, 8208], BF16)
                dvA = sbt(es, "dvA", [128, 65, 4, 65], BF16)
                B_K = Buf("dkT")
                B_V = Buf("dvA")
                S.op("pool", lambda e: e.memset(dvA[:, :, :, 64:65], 1.0), writes=[B_V])
                CmD, B_CmD = cload(es, "CmD", [128, 4, 4, 128])
                CsN, B_CsN = cload(es, "CsN", [8, 4, 8])
                cosD, B_cos = cload(es, "cosD", [128, NT + 1, 16], dt=F32)
                sinD, B_sin = cload(es, "sinD", [128, NT + 1, 16], dt=F32)
                wdq, bwdq = load_w_cols(es, "wdq", l, C_DQ, 256)
                lin = sbt(es, "lin", [128, 4, 32], F32)
                lt = sbt(es, "lt", [128, 2, 32], F32)
                lam = sbt(es, "lam", [128, 8], F32)
                sgl = sbt(es, "sgl", [128, 64], F32)
                B_l = Buf("lam")
                for i, nm in enumerate(("diff_lq1", "diff_lk1", "diff_lq2", "diff_lk2")):
                    S.dma("sp", lin[:, i, :], P.din[nm][l:l + 1, :].broadcast_to([128, 32]), writes=[B_l])
                S.dma("sp", sgl[:, :], P.din["diff_subln_g"][l:l + 1, :].broadcast_to([128, 64]), writes=[B_l])
                for j in range(2):
                    S.op("dve", lambda e: e.tensor_tensor(out=lt[:, j, :], in0=lin[:, 2 * j, :], in1=lin[:, 2 * j + 1, :], op=ALU.mult), reads=[B_l], writes=[B_l])
                    S.op("dve", lambda e: e.tensor_reduce(out=lam[:, j:j + 1], in_=lt[:, j, :], axis=mybir.AxisListType.X, op=ALU.add), reads=[B_l], writes=[B_l])
                S.op("act", lambda e: e.activation(out=lam[:, 2:4], in_=lam[:, 0:2], func=AF.Exp), reads=[B_l], writes=[B_l])
                S.op("dve", lambda e: e.tensor_tensor(out=lam[:, 4:5], in0=lam[:, 2:3], in1=lam[:, 3:4], op=ALU.subtract), reads=[B_l], writes=[B_l])
                S.op("dve", lambda e: e.tensor_scalar(out=lam[:, 5:6], in0=lam[:, 4:5], scalar1=lam_init, scalar2=-1.0, op0=ALU.add, op1=ALU.mult), reads=[B_l], writes=[B_l])
                S.op("dve", lambda e: e.tensor_scalar(out=sgl[:, :], in0=sgl[:, :], scalar1=1.0 - lam_init, scalar2=None, op0=ALU.mult), reads=[B_l], writes=[B_l])
                kin = Rot(es, nc, "dkin", [128, 256], F32, 4)
                xtr = Rot(es, nc, "xtd", [128, 8, 128], BF16, 2)
                zq = Rot(es, nc, "zdq", [128, 256], F32, 2)
                qrr = Rot(es, nc, "dqr", [128, 256], F32, 2)
                rtmp = Rot(es, nc, "dtmp", [128, 512], F32, 2)
                qds = Rot(es, nc, "qd", [128, 2, 256], BF16, 2)
                QTp = Rot(es, nc, "QTp", [128, 2, 2, 4, 128], BF16, 2)
                QTs = Rot(es, nc, "QTs", [128, 2, 2, 1, 8], BF16, 2)
                Qbs = Rot(es, nc, "Qbs", [128, 2, 4, 8], BF16, 2)
                odr = Rot(es, nc, "od", [128, 4, 256], F32, 2)
                odb = Rot(es, nc, "odb", [128, 256], BF16, 2)
                ost = Rot(es, nc, "dost", [128, 2, 128], BF16, 2)
                sml = Rot(es, nc, "dsml", [128, 8], F32, 8)
                junk = Rot(es, nc, "djunk", [128, 64], F32, 2)
                attn.pts = Rot(es, nc, "ptd", [128, 512], BF16, 4)

                def load_k(ntiles, src_fn, rows_fn, rd_fn, col_fn, gather_idx=None):
                    for kt in range(ntiles):
                        rows = rows_fn(kt)
                        kt_, bk_ = kin.next()
                        if gather_idx is None:
                            S.dma("sp", kt_[:rows, :], src_fn(kt), reads=rd_fn(kt), writes=[bk_])
                        else:
                            S.gather(kt_[:rows, :], src_fn(kt), gather_idx(kt), reads=rd_fn(kt), writes=[bk_])
                        ps, bp = PS.next()
                        for hb in range(2):
                            S.op("pe", lambda e: e.transpose(ps[:, hb * 128:hb * 128 + rows], kt_[:rows, hb * 128:(hb + 1) * 128], ident_f[:rows, :rows]),
                                 reads=[bk_, B_id], writes=[bp], inc=(hb == 1))
                        c_ = col_fn(kt)
                        evac(kt, dkT[:, :, c_:c_ + rows], ps[:, 0:256].rearrange("p (h c) -> p h c", h=2)[:, :, :rows], [bp], [B_K])

                def load_v(ntiles, src_fn, rows_fn, rd_fn, vt_fn, gather_idx=None):
                    for kt in range(ntiles):
                        rows = rows_fn(kt)
                        kt_, bk_ = kin.next()
                        if gather_idx is None:
                            S.dma("sp", kt_[:rows, :], src_fn(kt), reads=rd_fn(kt), writes=[bk_])
                        else:
                            S.gather(kt_[:rows, :], src_fn(kt), gather_idx(kt), reads=rd_fn(kt), writes=[bk_])
                        S.op("pool", lambda e: e.tensor_copy(out=dvA[:rows, vt_fn(kt), :, 0:64], in_=kt_[:rows, :].rearrange("p (h d) -> p h d", h=4)),
                             reads=[bk_], writes=[B_V])

                def qchunk(tiles, rows, samp, kts_desc, dst_cols):
                    nj = len(tiles)
                    QT, bQT = (QTs.next() if samp else QTp.next())
                    for j, (n, c0) in enumerate(tiles):
                        ti = NT if samp else n
                        xt, bx = xtr.next()
                        S.dma("sp", xt[:, :, :rows], xT_d[:, :, c0:c0 + rows], reads=[B_xT[min(n, NT)]], writes=[bx])
                        ps, bp = PS.next()
                        for kc in range(8):
                            S.op("pe", lambda e: e.matmul(ps[:rows, :256], lhsT=xt[:, kc, :rows], rhs=wdq[:, kc, :], start=(kc == 0), stop=(kc == 7)),
                                 reads=[bx, bwdq], writes=[bp], inc=(kc == 7))
                        z, bz = zq.next()
                        S.op("act", lambda e: e.copy(out=z[:rows, :], in_=ps[:rows, :256]), reads=[bp], writes=[bz])
                        qr, bqr = qrr.next()
                        tt, bt = rtmp.next()
                        zin = z[:rows, :].rearrange("p (h s d) -> p h s d", h=8, s=2)
                        qo = qr[:rows, :].rearrange("p (h s d) -> p h s d", h=8, s=2)
                        cb = cosD[:rows, ti, :].unsqueeze(1).broadcast_to([rows, 8, 16])
                        sb_ = sinD[:rows, ti, :].unsqueeze(1).broadcast_to([rows, 8, 16])
                        t4 = [tt[:rows, k * 128:(k + 1) * 128].rearrange("p (h d) -> p h d", h=8) for k in range(4)]
                        rope6(rows, zin[:, :, 0, :], zin[:, :, 1, :], qo[:, :, 0, :], qo[:, :, 1, :], cb, sb_, t4, [bz, B_cos, B_sin], bt, bqr)
                        qd, bqd = qds.next()
                        S.op("pool", lambda e: e.memset(qd[:rows, :, :], 0.0), writes=[bqd])
                        for i in range(2):
                            S.op("pool", lambda e: e.tensor_copy(out=qd[:rows, i, :].rearrange("p (h s d) -> p h s d", h=4, s=2)[:, :, i, :],
                                                                 in_=qr[:rows, :].rearrange("p (h s d) -> p h s d", h=4, s=2)[:, :, i, :]), reads=[bqr], writes=[bqd])
                        ps, bp = PS.next()
                        psb = ps[:].bitcast(BF16)
                        for i in range(2):
                            for hb in range(2):
                                k_ = i * 2 + hb
                                S.op("pe", lambda e: e.transpose(psb[:, k_ * 128:k_ * 128 + rows], qd[:rows, i, hb * 128:(hb + 1) * 128], ident_b[:rows, :rows]),
                                     reads=[bqd, B_id], writes=[bp], inc=(k_ == 3))
                        evac(j, QT[:, :, :, j, :rows], psb[:, 0:512].rearrange("p (i h c) -> p i h c", i=2, h=2)[:, :, :, :rows], [bp], [bQT])
                    od, bod = odr.next()
                    ncols = nj * rows
                    if samp:
                        Qb, bQb = Qbs.next()
                        S.op("pool", lambda e: e.memset(Qb[:, :, :, :], 0.0), writes=[bQb])
                        for hb in range(2):
                            for hh in range(2):
                                for i in range(2):
                                    S.op("dve" if i == 0 else "pool",
                                         lambda e: e.tensor_copy(out=Qb[64 * hh:64 * hh + 64, hb, hh * 2 + i, :], in_=QT[64 * hh:64 * hh + 64, i, hb, 0, :rows]),
                                         reads=[bQT], writes=[bQb])
                    for hb in range(2):
                        taken, ks = PS.take(1 if samp else 4)
                        units = []
                        if samp:
                            units.append(dict(Q=Qb[:, hb, :, :].rearrange("p u c -> p (u c)"), ncols=4 * rows, scale=32 ** -0.5, nblk=4, acc=taken[0],
                                              rdQ=[bQb, B_CmD, B_CsN, B_id]))
                        else:
                            for hh in range(2):
                                for i in range(2):
                                    Q = QT[64 * hh:64 * hh + 64, i, hb, :, :rows].rearrange("p j c -> p (j c)")
                                    units.append(dict(Q=Q, ncols=ncols, scale=32 ** -0.5, nblk=nj, acc=taken[hh * 2 + i], rdQ=[bQT, B_CmD, B_CsN, B_id]))
                        kts = []
                        for kd in kts_desc:
                            nk, c_ = kd["nk"], kd["col"]
                            K, V, masks = [], [], []
                            if samp:
                                K.append(dkT[:, hb, c_:c_ + nk])
                                V.append([dvA[:nk, kd["vt"], hb * 2 + hh, :] for hh in range(2) for i in range(2)])
                                if kd["mask"] is None:
                                    masks.append([])
                                else:
                                    masks.append([(ident_b[:nk, :nk], CsN[:nk, :, :].rearrange("p q c -> p (q c)"))])
                            else:
                                for hh in range(2):
                                    for i in range(2):
                                        K.append(dkT[64 * hh:64 * hh + 64, hb, c_:c_ + nk])
                                        V.append(dvA[:nk, kd["vt"], hb * 2 + hh, :])
                                        if kd["mask"] is None:
                                            masks.append([])
                                        else:
                                            masks.append([(ident_b[:nk, :nk], CmD[:nk, kd["mask"][1], :, :].rearrange("p j c -> p (j c)"))])
                            kts.append(dict(nk=nk, K=K, V=V, masks=masks, rd=[B_K, B_V]))
                        attn(units, kts, rows)
                        for hh in range(2):
                            h = hb * 2 + hh
                            if samp:
                                a0, b0 = taken[0]
                                a1, b1 = taken[0]
                                o0, o1 = (hh * 2) * 65, (hh * 2 + 1) * 65
                            else:
                                a0, b0 = taken[hh * 2]
                                a1, b1 = taken[hh * 2 + 1]
                                o0, o1 = 0, 0
                            for j in range(nj):
                                c0_, c1_ = o0 + j * 65, o1 + j * 65
                                rs, brs = sml.next()
                                S.op("dve", lambda e: e.tensor_scalar(out=rs[:rows, 0:1], in0=a0[:rows, c0_ + 64:c0_ + 65], scalar1=1e-30, scalar2=None, op0=ALU.max), reads=[b0], writes=[brs])
                                S.op("dve", lambda e: e.tensor_scalar(out=rs[:rows, 1:2], in0=a1[:rows, c1_ + 64:c1_ + 65], scalar1=1e-30, scalar2=None, op0=ALU.max), reads=[b1], writes=[brs])
                                S.op("dve", lambda e: e.reciprocal(out=rs[:rows, 2:4], in_=rs[:rows, 0:2]), reads=[brs], writes=[brs])
                                S.op("dve", lambda e: e.tensor_tensor(out=rs[:rows, 3:4], in0=rs[:rows, 3:4], in1=lam[:rows, 5:6], op=ALU.mult), reads=[brs, B_l], writes=[brs])
                                o_ = od[:rows, j, h * 64:(h + 1) * 64]
                                S.op("dve", lambda e: e.tensor_scalar(out=o_, in0=a0[:rows, c0_:c0_ + 64], scalar1=rs[:rows, 2:3], scalar2=None, op0=ALU.mult), reads=[b0, brs], writes=[bod])
                                S.op("dve", lambda e: e.scalar_tensor_tensor(out=o_, in0=a1[:rows, c1_:c1_ + 64], scalar=rs[:rows, 3:4], in1=o_, op0=ALU.mult, op1=ALU.add),
                                     reads=[b1, brs, bod], writes=[bod])
                                jk, bjk = junk.next()
                                S.op("act", lambda e: e.activation(out=jk[:rows, :], in_=o_, func=AF.Square, accum_out=rs[:rows, 4:5]), reads=[bod], writes=[bjk, brs])
                                S.op("dve", lambda e: e.tensor_scalar(out=rs[:rows, 5:6], in0=rs[:rows, 4:5], scalar1=1.0 / 64.0, scalar2=EPS, op0=ALU.mult, op1=ALU.add), reads=[brs], writes=[brs])
                                S.op("act", lambda e: e.activation(out=rs[:rows, 6:7], in_=rs[:rows, 5:6], func=AF.Sqrt), reads=[brs], writes=[brs])
                                S.op("dve", lambda e: e.reciprocal(out=rs[:rows, 7:8], in_=rs[:rows, 6:7]), reads=[brs], writes=[brs])
                                S.op("dve", lambda e: e.scalar_tensor_tensor(out=o_, in0=o_, scalar=rs[:rows, 7:8], in1=sgl[:rows, :], op0=ALU.mult, op1=ALU.mult),
                                     reads=[bod, brs, B_l], writes=[bod])
                        PS.give(ks)
                    for j in range(nj):
                        ob, bob = odb.next()
                        S.op("pool", lambda e: e.tensor_copy(out=ob[:rows, :], in_=od[:rows, j, :]), reads=[bod], writes=[bob])
                        ps, bp = PS.next()
                        psb = ps[:].bitcast(BF16)
                        for k_ in range(2):
                            S.op("pe", lambda e: e.transpose(psb[:, k_ * 128:k_ * 128 + rows], ob[:rows, k_ * 128:(k_ + 1) * 128], ident_b[:rows, :rows]),
                                 reads=[bob, B_id], writes=[bp], inc=(k_ == 1))
                        os_, bos = ost.next()
                        evac(j, os_[:, :, :rows], psb[:, 0:256].rearrange("p (k c) -> p k c", k=2)[:, :, :rows], [bp], [bos])
                        S.dma("sp", mixT_d[:, 6:8, dst_cols[j]:dst_cols[j] + rows], os_[:, :, :rows], reads=[bos], writes=[B_mix[2]])

                full = lambda kt: 128
                load_k(NT, lambda kt: o_p["diff_k"][l, kt * 128:(kt + 1) * 128, :], full, lambda kt: [B_kvd[kt]], lambda kt: kt * 128)
                load_v(NT, lambda kt: o_p["diff_v"][l, kt * 128:(kt + 1) * 128, :], full, lambda kt: [B_kvd[kt]], lambda kt: kt)
                for qc in range(NT // 4):
                    kts_desc = [dict(nk=128, col=kt * 128, vt=kt, mask=(("D", kt - 4 * qc) if kt >= 4 * qc else None)) for kt in range(4 * qc + 4)]
                    qchunk([(4 * qc + j, (4 * qc + j) * 128) for j in range(4)], 128, False, kts_desc, [(4 * qc + j) * 128 for j in range(4)])
                S.barrier()
                ptb = sbt(es, "dptb", [128, NSB, NPG], I32)
                idx = sbt(es, "didx", [128, NSB, NPG], I32)
                iot = sbt(es, "diot", [128, 1], I32)
                B_idx = Buf("didx")
                S.dma("sp", ptb[:].rearrange("p b j -> p (b j)"), pt.rearrange("b j -> (b j)").unsqueeze(0).broadcast_to([128, NSB * NPG]), writes=[B_idx])
                S.op("pool", lambda e: e.iota(iot[:], pattern=[[0, 1]], base=l * NPOOL * 128, channel_multiplier=1), writes=[B_idx])
                S.op("dve", lambda e: e.tensor_scalar(out=idx[:].rearrange("p b j -> p (b j)"), in0=ptb[:].rearrange("p b j -> p (b j)"), scalar1=128.0,
                                                      scalar2=iot[:, 0:1], op0=ALU.mult, op1=ALU.add), reads=[B_idx], writes=[B_idx])
                for b in range(NSB):
                    gi = lambda kt, b=b: idx[:, b, kt:kt + 1]
                    load_k(NPG, lambda kt: c_diff_k, full, lambda kt: [B_idx], lambda kt: kt * 128, gather_idx=gi)
                    load_k(1, lambda kt: o_s["diff_k"][l, b], lambda kt: 8, lambda kt: [B_kvd[NT + b]], lambda kt: PAST)
                    load_v(NPG, lambda kt: c_diff_v, full, lambda kt: [B_idx], lambda kt: kt, gather_idx=gi)
                    load_v(1, lambda kt: o_s["diff_v"][l, b], lambda kt: 8, lambda kt: [B_kvd[NT + b]], lambda kt: 64)
                    kts_desc = [dict(nk=128, col=kt * 128, vt=kt, mask=None) for kt in range(NPG)]
                    kts_desc.append(dict(nk=8, col=PAST, vt=64, mask=("S",)))
                    qchunk([(NT + b, T + 8 * b)], 8, True, kts_desc, [T + 8 * b])
                    S.barrier()

        def phase_c(l):
            with ExitStack() as es:
                wo = sbt(es, "wo", [128, 8, D], BF16)
                W1 = sbt(es, "fW1", [128, 8, DFF], BF16)
                W3 = sbt(es, "fW3", [128, 8, DFF], BF16)
                W2 = sbt(es, "fW2", [128, 22, D], BF16)
                B_w = Buf("fw")
                src = P.din["w_out"][l].rearrange("(kc p) c -> p kc c", p=128)
                for kc in range(8):
                    S.dma("pool", wo[:, kc, :], src[:, kc, :], writes=[B_w])
                for wt, nm in ((W1, "ffn_w1"), (W3, "ffn_w3")):
                    src = P.din[nm][l].rearrange("(kc p) c -> p kc c", p=128)
                    for kc in range(8):
                        for hf in range(2):
                            S.dma("pool", wt[:, kc, hf * 1408:(hf + 1) * 1408], src[:, kc, hf * 1408:(hf + 1) * 1408], writes=[B_w])
                src = P.din["ffn_w2"][l].rearrange("(fc p) c -> p fc c", p=128)
                for fc in range(22):
                    S.dma("pool", W2[:, fc, :], src[:, fc, :], writes=[B_w])
                lnp = sbt(es, "lnp", [128, 4, D], F32)
                B_ln = Buf("lnp")
                for i, nm in enumerate(("ln1_g", "ln1_b", "ln2_g", "ln2_b")):
                    S.dma("sp", lnp[:, i, :], P.din[nm][l:l + 1, :].broadcast_to([128, D]), writes=[B_ln])
                mxr = Rot(es, nc, "mx", [128, 8, 128], BF16, 2)
                xin = Rot(es, nc, "cx", [128, D], F32, 2)
                xar = Rot(es, nc, "cxa", [128, D], F32, 2)
                xnr = Rot(es, nc, "cxn", [128, D], F32, 2)
                xnT = Rot(es, nc, "cxnT", [128, 8, 128], BF16, 1)
                hTr = Rot(es, nc, "chT", [128, 22, 128], BF16, 1)
                sgr = Rot(es, nc, "csg", [128, 512], F32, 2)
                str_ = Rot(es, nc, "cstat", [128, 24], F32, 4)

                def layer_norm(rows, src_, bsrc, dst, bdst, gi):
                    st, bst = str_.next()
                    for c in range(2):
                        S.op("dve", lambda e: e.bn_stats(out=st[:rows, c * 6:(c + 1) * 6], in_=src_[:rows, c * 512:(c + 1) * 512]), reads=[bsrc], writes=[bst])
                    S.op("dve", lambda e: e.bn_aggr(out=st[:rows, 12:14], in_=st[:rows, 0:12]), reads=[bst], writes=[bst])
                    S.op("dve", lambda e: e.tensor_scalar(out=st[:rows, 14:15], in0=st[:rows, 13:14], scalar1=EPS, scalar2=None, op0=ALU.add), reads=[bst], writes=[bst])
                    S.op("act", lambda e: e.activation(out=st[:rows, 15:16], in_=st[:rows, 14:15], func=AF.Sqrt), reads=[bst], writes=[bst])
                    S.op("dve", lambda e: e.reciprocal(out=st[:rows, 16:17], in_=st[:rows, 15:16]), reads=[bst], writes=[bst])
                    S.op("dve", lambda e: e.tensor_scalar(out=dst[:rows, :], in0=src_[:rows, :], scalar1=st[:rows, 12:13], scalar2=st[:rows, 16:17], op0=ALU.subtract, op1=ALU.mult),
                         reads=[bsrc, bst], writes=[bdst])
                    S.op("pool", lambda e: e.tensor_tensor(out=dst[:rows, :], in0=dst[:rows, :], in1=lnp[:rows, gi, :], op=ALU.mult), reads=[bdst, B_ln], writes=[bdst])
                    S.op("pool", lambda e: e.tensor_tensor(out=dst[:rows, :], in0=dst[:rows, :], in1=lnp[:rows, gi + 1, :], op=ALU.add), reads=[bdst, B_ln], writes=[bdst])

                for n in range(NT + 1):
                    rows = 128 if n < NT else NS
                    t0 = n * 128
                    mx, bmx = mxr.next()
                    S.dma("sp", mx[:, :, :rows], mixT_d[:, :, t0:t0 + rows], reads=[B_mix[0], B_mix[1], B_mix[2]], writes=[bmx])
                    x_, bx = xin.next()
                    if l == 0:
                        S.dma("sp", x_[:rows, :], (xp[t0:t0 + rows, :] if n < NT else xs), writes=[bx])
                    else:
                        S.dma("sp", x_[:rows, :], xmid[t0:t0 + rows, :], reads=[B_xmid[n]], writes=[bx])
                    xa, bxa = xar.next()
                    for hf in range(2):
                        ps, bp = PS.next()
                        for kc in range(8):
                            S.op("pe", lambda e: e.matmul(ps[:rows, :], lhsT=mx[:, kc, :rows], rhs=wo[:, kc, hf * 512:(hf + 1) * 512], start=(kc == 0), stop=(kc == 7)),
                                 reads=[bmx, B_w], writes=[bp], inc=(kc == 7))
                        S.op("dve", lambda e: e.scalar_tensor_tensor(out=xa[:rows, hf * 512:(hf + 1) * 512], in0=x_[:rows, hf * 512:(hf + 1) * 512], scalar=DN_ALPHA,
                                                                     in1=ps[:rows, :], op0=ALU.mult, op1=ALU.add), reads=[bx, bp], writes=[bxa])
                    xn, bxn = xnr.next()
                    layer_norm(rows, xa, bxa, xn, bxn, 0)
                    xT_, bxT = xnT.next()
                    for hf in range(2):
                        ps, bp = PS.next()
                        for j in range(4):
                            kc = hf * 4 + j
                            S.op("pe", lambda e: e.transpose(ps[:, j * 128:j * 128 + rows], xn[:rows, kc * 128:(kc + 1) * 128], ident_f[:rows, :rows]),
                                 reads=[bxn, B_id], writes=[bp], inc=(j == 3))
                        evac(hf, xT_[:, hf * 4:hf * 4 + 4, :rows], ps[:].rearrange("p (j c) -> p j c", j=4)[:, :, :rows], [bp], [bxT])
                    hT, bhT = hTr.next()
                    for f0 in range(0, 22, 4):
                        nf = min(4, 22 - f0)
                        ps1, bp1 = PS.next()
                        ps3, bp3 = PS.next()
                        for (ps_, bp_, wt) in ((ps1, bp1, W1), (ps3, bp3, W3)):
                            for f in range(nf):
                                for kc in range(8):
                                    S.op("pe", lambda e: e.matmul(ps_[:, f * 128:f * 128 + rows], lhsT=wt[:, kc, (f0 + f) * 128:(f0 + f + 1) * 128], rhs=xT_[:, kc, :rows],
                                                                  start=(kc == 0), stop=(kc == 7)), reads=[bxT, B_w], writes=[bp_], inc=(kc == 7 and f == nf - 1))
                        sg, bsg = sgr.next()
                        v1 = ps1[:, 0:nf * 128].rearrange("p (f c) -> p f c", f=nf)[:, :, :rows]
                        v3 = ps3[:, 0:nf * 128].rearrange("p (f c) -> p f c", f=nf)[:, :, :rows]
                        sv = sg[:, 0:nf * 128].rearrange("p (f c) -> p f c", f=nf)[:, :, :rows]
                        S.op("act", lambda e: e.activation(out=sv, in_=v1, func=AF.Silu), reads=[bp1], writes=[bsg])
                        S.op("dve", lambda e: e.tensor_tensor(out=hT[:, f0:f0 + nf, :rows], in0=sv, in1=v3, op=ALU.mult), reads=[bsg, bp3], writes=[bhT])
                    xb, bxb = xar.next()
                    for hf in range(2):
                        ps, bp = PS.next()
                        for fc in range(22):
                            S.op("pe", lambda e: e.matmul(ps[:rows, :], lhsT=hT[:, fc, :rows], rhs=W2[:, fc, hf * 512:(hf + 1) * 512], start=(fc == 0), stop=(fc == 21)),
                                 reads=[bhT, B_w], writes=[bp], inc=(fc == 21))
                        S.op("dve", lambda e: e.scalar_tensor_tensor(out=xb[:rows, hf * 512:(hf + 1) * 512], in0=xn[:rows, hf * 512:(hf + 1) * 512], scalar=DN_ALPHA,
                                                                     in1=ps[:rows, :], op0=ALU.mult, op1=ALU.add), reads=[bxn, bp], writes=[bxb])
                    y_, by = xin.next()
                    layer_norm(rows, xb, bxb, y_, by, 2)
                    if l == DEPTH - 1:
                        P.store((y_p[t0:t0 + rows, :] if n < NT else y_s), y_[:rows, :], [by])
                    else:
                        S.dma("sp", xmid[t0:t0 + rows, :], y_[:rows, :], reads=[by], writes=[B_xmid[n]])

        for l in range(DEPTH):
            phase_xT(l)
            S.barrier()
            phase_kv_outputs(l)
            S.barrier()
            phase_conv(l)
            S.barrier()
            phase_nsa(l)
            S.barrier()
            phase_diff(l)
            S.barrier()
            phase_c(l)
            S.barrier()
        S.barrier()
    S.close()
    return P


_PROG = None


def kernel(**inputs):
    global _PROG
    if _PROG is None:
        _PROG = build_program()
    P = _PROG
    f32 = np.float32
    consts = _consts()
    g = lambda k: np.asarray(inputs[k])
    cnames = (("c_cmp_k", "cache_nsa_cmp_k", 128), ("c_cmp_v", "cache_nsa_cmp_v", 128), ("c_sel_k", "cache_nsa_sel_k", 128),
              ("c_sel_v", "cache_nsa_sel_v", 128), ("c_diff_k", "cache_diff_k", 256), ("c_diff_v", "cache_diff_v", 256))
    shared = {}
    if not KDEV:
        for dn, sn, w_ in cnames:
            shared[dn] = g(sn).reshape(DEPTH * NPOOL * 128, w_)
    for nm in P.din:
        if nm in inputs and nm not in shared:
            shared[nm] = g(nm)
    shared.update(consts)
    in_maps = []
    for c in range(8):
        m = dict(shared)
        m["xp"] = g("x_prompt")[c % 4]
        sl = slice(4 * c, 4 * c + 4)
        m["xs"] = g("x_sample")[sl].reshape(NS, D)
        m["st_win_k"] = np.ascontiguousarray(g("state_nsa_win_k")[:, sl].reshape(DEPTH, NSB, 512, 128))
        m["st_win_v"] = np.ascontiguousarray(g("state_nsa_win_v")[:, sl].reshape(DEPTH, NSB, 512, 128))
        m["st_conv"] = np.ascontiguousarray(g("state_conv")[:, sl])
        ptc = np.ascontiguousarray(g("page_table")[sl]).astype(np.int32)
        if KDEV:
            flat = ptc.reshape(-1)
            for dn, sn, w_ in cnames:
                m[dn] = np.ascontiguousarray(g(sn)[:, flat]).reshape(DEPTH * NPOOL * 128, w_)
            ptc = np.arange(NSB * NPG, dtype=np.int32).reshape(NSB, NPG)
        m["pt"] = ptc
        in_maps.append({k: m[k] for k in P.din})
    res = run_bass_kernel_spmd(P.nc, in_maps, core_ids=list(range(8))).results

    def pgather(name, shape):
        return np.stack([res[c][name] for c in range(4)], axis=1).reshape(shape)

    def sgather(name, shape):
        return np.concatenate([res[c][name] for c in range(8)], axis=1).reshape(shape)

    y_prompt = np.stack([res[c]["y_p"] for c in range(4)], axis=0)
    y_sample = np.concatenate([res[c]["y_s"].reshape(NSB, 8, D) for c in range(8)], axis=0)
    outs = [y_prompt, y_sample]
    for nm, shp in (("cmp_k", (DEPTH, 4, T, 2, 64)), ("cmp_v", (DEPTH, 4, T, 2, 64)), ("sel_k", (DEPTH, 4, T, 2, 64)),
                    ("sel_v", (DEPTH, 4, T, 2, 64)), ("diff_k", (DEPTH, 4, T, 4, 64)), ("diff_v", (DEPTH, 4, T, 4, 64)),
                    ("win_k", (DEPTH, 4, 512, 2, 64)), ("win_v", (DEPTH, 4, 512, 2, 64)), ("conv", (DEPTH, 4, 30, 256))):
        outs.append(pgather("p_" + nm, shp))
    for nm, shp in (("cmp_k", (DEPTH, 32, 8, 2, 64)), ("cmp_v", (DEPTH, 32, 8, 2, 64)), ("sel_k", (DEPTH, 32, 8, 2, 64)),
                    ("sel_v", (DEPTH, 32, 8, 2, 64)), ("diff_k", (DEPTH, 32, 8, 4, 64)), ("diff_v", (DEPTH, 32, 8, 4, 64)),
                    ("win_k", (DEPTH, 32, 512, 2, 64)), ("win_v", (DEPTH, 32, 512, 2, 64)), ("conv", (DEPTH, 32, 30, 256))):
        outs.append(sgather("s_" + nm, shp))
    return tuple(np.ascontiguousarray(o, dtype=f32) for o in outs)
```

```python
import math
from contextlib import ExitStack

import numpy as np
import ml_dtypes
import concourse.bass as bass
import concourse.mybir as mybir
from concourse.bass_utils import run_bass_kernel_spmd

F32 = mybir.dt.float32
BF16 = mybir.dt.bfloat16
I32 = mybir.dt.int32
AF = mybir.ActivationFunctionType
ALU = mybir.AluOpType

D = 1024
T = 4096
NT = 32
NSB = 4
NS = 32
XC = T + NS
DEPTH = 2
import os
KDEV = os.environ.get('KDEV', '') == '1'
NPOOL = 256 if KDEV else 2560
PAST = 8192
NPG = 64
NIN = 2584
DFF = 2816
NEG = -30000.0
EPS = 1e-5
DN_ALPHA = (2 * DEPTH) ** 0.25
THETA = 10000.0


class Buf:
    __slots__ = ("w", "r", "name")

    def __init__(self, name=""):
        self.w = None
        self.r = {}
        self.name = name


class Bufs(dict):
    def __init__(self, name):
        super().__init__()
        self.name = name

    def __missing__(self, k):
        b = Buf(f"{self.name}{k}")
        self[k] = b
        return b


class Sched:
    def __init__(self, nc, ndma=16):
        self.nc = nc
        self.eng = {"pe": nc.tensor, "act": nc.scalar, "dve": nc.vector, "pool": nc.gpsimd, "sp": nc.sync}
        self.sem, self.cnt = {}, {}
        self.waited = {k: {} for k in self.eng}
        self._ctx = []
        for k in ("pe", "act", "dve", "pool"):
            cm = nc.semaphore("s_" + k)
            self.sem[k] = cm.__enter__()
            self._ctx.append(cm)
            self.cnt[k] = 0
        self.dq = {}
        for q in ("sp", "pool"):
            ring = []
            for i in range(ndma):
                cm = nc.semaphore(f"d_{q}{i}")
                ring.append(cm.__enter__())
                self._ctx.append(cm)
            self.dq[q] = dict(ring=ring, n=0)

    def close(self):
        for cm in reversed(self._ctx):
            cm.__exit__(None, None, None)

    def _wait(self, e, tok):
        sem, val, key = tok
        w = self.waited[e]
        if w.get(key, 0) >= val:
            return
        self.eng[e].wait_ge(sem, val)
        w[key] = val

    @staticmethod
    def _flat(bs):
        out = []
        for b in bs:
            if isinstance(b, (list, tuple)):
                out.extend(Sched._flat(b))
            else:
                out.append(b)
        return out

    def _deps(self, e, reads, writes):
        reads, writes = self._flat(reads), self._flat(writes)
        for b in reads:
            t = b.w
            if t is not None and not (e == "pe" and t[2] == "pe"):
                self._wait(e, t)
        for b in writes:
            t = b.w
            if t is not None and not (e == "pe" and t[2] == "pe"):
                self._wait(e, t)
            for t in b.r.values():
                if not (e == "pe" and t[2] == "pe"):
                    self._wait(e, t)

    def _commit(self, tok, reads, writes):
        reads, writes = self._flat(reads), self._flat(writes)
        for b in reads:
            o = b.r.get(tok[2])
            if o is None or o[1] < tok[1]:
                b.r[tok[2]] = tok
        for b in writes:
            b.w = tok
            b.r = {}

    def op(self, e, fn, reads=(), writes=(), inc=True):
        self._deps(e, reads, writes)
        ins = fn(self.eng[e])
        if inc:
            self.cnt[e] += 1
            ins.then_inc(self.sem[e], 1)
            tok = (self.sem[e], self.cnt[e], e)
        else:
            tok = (self.sem[e], self.cnt[e] + 1, e)
        self._commit(tok, reads, writes)
        return tok

    def _dma_tok(self, q):
        d = self.dq[q]
        R = len(d["ring"])
        i = d["n"]
        slot = i % R
        sem = d["ring"][slot]
        key = f"{q}{slot}"
        if i >= R:
            self._wait(q, (sem, 16 * (i // R), key))
        return d, sem, (sem, 16 * (i // R + 1), key)

    def dma(self, q, out, in_, reads=(), writes=(), **kw):
        d, sem, tok = self._dma_tok(q)
        self._deps(q, reads, writes)
        ins = self.eng[q].dma_start(out=out, in_=in_, **kw)
        ins.then_inc(sem, 16)
        d["n"] += 1
        self._commit(tok, reads, writes)
        return tok

    def gather(self, out, src2d, idx_col, reads=(), writes=()):
        q = "pool"
        d, sem, tok = self._dma_tok(q)
        self._deps(q, reads, writes)
        ins = self.nc.gpsimd.indirect_dma_start(out=out, out_offset=None, in_=src2d,
                                                 in_offset=bass.IndirectOffsetOnAxis(ap=idx_col, axis=0))
        ins.then_inc(sem, 16)
        d["n"] += 1
        self._commit(tok, reads, writes)
        return tok

    def barrier(self):
        toks = [(self.sem[k], self.cnt[k], k) for k in ("pe", "act", "dve", "pool") if self.cnt[k] > 0]
        for q, d in self.dq.items():
            R = len(d["ring"])
            for slot in range(min(R, d["n"])):
                n_on = (d["n"] - 1 - slot) // R + 1
                toks.append((d["ring"][slot], 16 * n_on, f"{q}{slot}"))
        for e in self.eng:
            for t in toks:
                if t[2] != e:
                    self._wait(e, t)


def _rope_tab(half):
    inv = (np.float32(THETA) ** (-np.arange(half, dtype=np.float32) / np.float32(half))).astype(np.float32)
    pos = np.zeros((128, NT + 1), np.float32)
    for n in range(NT):
        pos[:, n] = 128 * n + np.arange(128)
    pos[:, NT] = PAST + (np.arange(128) % 8)
    ang = (pos[:, :, None] * inv[None, None, :]).astype(np.float32)
    return np.cos(ang).astype(np.float32), np.sin(ang).astype(np.float32)


def _consts():
    c = {}
    c["ident_f"] = np.eye(128, dtype=np.float32)
    c["cosN"], c["sinN"] = _rope_tab(32)
    c["cosD"], c["sinD"] = _rope_tab(16)
    BIG = -NEG
    k = np.arange(128)
    Ep = np.zeros((128, 32, 128), np.float32)
    for kt in range(32):
        for kk in range(128):
            j = 2 * kt + kk // 64
            Ep[j, kt, kk] = BIG
            Ep[64 + j, kt, kk] = BIG
    c["Ep"] = Ep
    Es = np.zeros((128, 64, 128), np.float32)
    for kt in range(64):
        for kk in range(128):
            Es[2 * kt + kk // 64, kt, kk] = BIG
    c["Es"] = Es
    cp = np.arange(503)
    c["M0"] = np.where(16 * (cp[None, :] - 248) + 31 <= k[:, None], 0.0, NEG).astype(np.float32)
    Ft = np.zeros((128, 32, 64), np.float32)
    jj = np.arange(64)
    for n in range(32):
        t = 128 * n + k
        cur = (t // 64)[:, None]
        forced = (jj[None, :] == 0) | (jj[None, :] == cur) | (jj[None, :] == cur - 1)
        Ft[:, n, :] = np.where(jj[None, :] > cur, -2e9, np.where(forced, 1e9, 0.0))
    c["Ftab"] = Ft
    Fs = np.zeros((128, 129), np.float32)
    Fs[:, [0, 127, 128]] = 1e9
    c["Fs"] = Fs
    q = np.arange(128)
    cm = np.where(k[:, None] <= q[None, :], 0.0, NEG).astype(np.float32)
    c["Cm4"] = np.ascontiguousarray(np.broadcast_to(cm[:, None, :], (128, 4, 128)))
    bu = np.where(k[:, None] > q[None, :], 0.0, NEG).astype(np.float32)
    c["Bu4"] = np.ascontiguousarray(np.broadcast_to(bu[:, None, :], (128, 4, 128)))
    t8 = np.arange(8)
    cs = np.where(t8[:, None] <= t8[None, :], 0.0, NEG).astype(np.float32)
    c["CsN"] = np.ascontiguousarray(np.broadcast_to(cs[:, None, :], (8, 4, 8)))
    ws = np.where(k[:, None] > t8[None, :], 0.0, NEG).astype(np.float32)
    c["WsN"] = np.ascontiguousarray(np.broadcast_to(ws[:, None, :], (128, 4, 8)))
    CmD = np.zeros((128, 4, 4, 128), np.float32)
    for r in range(4):
        for j in range(4):
            CmD[:, r, j, :] = np.where(128 * r + k[:, None] <= 128 * j + q[None, :], 0.0, NEG)
    c["CmD"] = CmD
    return c


CONST_SPECS = {
    "ident_f": ([128, 128], F32),
    "Ep": ([128, 32, 128], F32), "Es": ([128, 64, 128], F32), "M0": ([128, 503], F32), "Ftab": ([128, 32, 64], F32), "Fs": ([128, 129], F32),
    "Cm4": ([128, 4, 128], F32), "Bu4": ([128, 4, 128], F32), "CsN": ([8, 4, 8], F32), "WsN": ([128, 4, 8], F32), "CmD": ([128, 4, 4, 128], F32),
    "cosN": ([128, NT + 1, 32], F32), "sinN": ([128, NT + 1, 32], F32),
    "cosD": ([128, NT + 1, 16], F32), "sinD": ([128, NT + 1, 16], F32),
}

C_CA, C_CG, C_NQ, C_CK, C_CV, C_SK, C_SV, C_WK, C_WV, C_GT, C_DQ, C_DK, C_DV = (
    0, 256, 512, 1024, 1152, 1280, 1408, 1536, 1664, 1792, 1816, 2072, 2328)


class Prog:
    def __init__(self):
        self.nc = nc = bass.Bass("TRN2", target_bir_lowering=False)
        self.S = Sched(nc)
        self.din, self.dout = {}, {}
        self.outbufs = []

    def inp(self, name, shape, dt=F32):
        self.din[name] = self.nc.dram_tensor(name, list(shape), dt, kind="ExternalInput").ap()
        return self.din[name]

    def outp(self, name, shape, dt=F32):
        self.dout[name] = self.nc.dram_tensor(name, list(shape), dt, kind="ExternalOutput").ap()
        return self.dout[name]

    def scratch(self, name, shape, dt):
        return self.nc.dram_tensor(name, list(shape), dt, kind="Internal").ap()

    def store(self, out_ap, in_ap, reads, q="sp"):
        b = Buf("o")
        self.S.dma(q, out_ap, in_ap, reads=reads, writes=[b])
        return b


_UC = [0]


def U(name):
    _UC[0] += 1
    return f"{name}_{_UC[0]}"


class Rot:
    def __init__(self, es, nc, name, shape, dt, n):
        self.t = [es.enter_context(nc.sbuf_tensor(U(f"{name}{i}"), list(shape), dt)) for i in range(n)]
        self.b = [Buf(f"{name}{i}") for i in range(n)]
        self.i = 0

    def next(self):
        k = self.i % len(self.t)
        self.i += 1
        return self.t[k], self.b[k]


class PsumRot:
    def __init__(self, es, nc, n=8):
        self.t = [es.enter_context(nc.psum_tensor(f"psb{i}", [128, 512], F32)) for i in range(n)]
        self.b = [Buf(f"psb{i}") for i in range(n)]
        self.free = list(range(n))
        self.i = 0

    def next(self):
        k = self.free[self.i % len(self.free)]
        self.i += 1
        return self.t[k], self.b[k]

    def take(self, n):
        ks = self.free[-n:]
        self.free = self.free[:-n]
        return [(self.t[k], self.b[k]) for k in ks], ks

    def give(self, ks):
        self.free = self.free + list(ks)


def build_program():
    P = Prog()
    nc, S = P.nc, P.S
    xp = P.inp("xp", [T, D])
    xs = P.inp("xs", [NS, D])
    c_cmp_k = P.inp("c_cmp_k", [DEPTH * NPOOL * 128, 128])
    c_cmp_v = P.inp("c_cmp_v", [DEPTH * NPOOL * 128, 128])
    c_sel_k = P.inp("c_sel_k", [DEPTH * NPOOL * 128, 128])
    c_sel_v = P.inp("c_sel_v", [DEPTH * NPOOL * 128, 128])
    c_diff_k = P.inp("c_diff_k", [DEPTH * NPOOL * 128, 256])
    c_diff_v = P.inp("c_diff_v", [DEPTH * NPOOL * 128, 256])
    st_win_k = P.inp("st_win_k", [DEPTH, NSB, 512, 128])
    st_win_v = P.inp("st_win_v", [DEPTH, NSB, 512, 128])
    st_conv = P.inp("st_conv", [DEPTH, NSB, 30, 256])
    pt = P.inp("pt", [NSB, NPG], I32)
    w_in = P.inp("w_in", [DEPTH, D, NIN])

    for nm, shp in (("conv_dw_w", [DEPTH, 31, 256]), ("conv_dw_b", [DEPTH, 256]), ("conv_ln_g", [DEPTH, 256]), ("conv_ln_b", [DEPTH, 256]),
                    ("cmp_pe_k", [DEPTH, 32, 64]), ("cmp_w1_k", [DEPTH, 2048, 128]), ("cmp_b1_k", [DEPTH, 128]), ("cmp_w2_k", [DEPTH, 128, 64]),
                    ("cmp_pe_v", [DEPTH, 32, 64]), ("cmp_w1_v", [DEPTH, 2048, 128]), ("cmp_b1_v", [DEPTH, 128]), ("cmp_w2_v", [DEPTH, 128, 64]),
                    ("diff_lq1", [DEPTH, 32]), ("diff_lk1", [DEPTH, 32]), ("diff_lq2", [DEPTH, 32]), ("diff_lk2", [DEPTH, 32]),
                    ("diff_subln_g", [DEPTH, 64]), ("w_out", [DEPTH, D, D]), ("ln1_g", [DEPTH, D]), ("ln1_b", [DEPTH, D]),
                    ("ln2_g", [DEPTH, D]), ("ln2_b", [DEPTH, D]), ("ffn_w1", [DEPTH, D, DFF]), ("ffn_w3", [DEPTH, D, DFF]), ("ffn_w2", [DEPTH, DFF, D])):
        P.inp(nm, shp)
    for nm, shp in CONST_SPECS.items():
        P.inp(nm, shp[0], shp[1])

    y_p = P.outp("y_p", [T, D])
    y_s = P.outp("y_s", [NS, D])
    o_p = {}
    for nm, w_ in (("cmp_k", 128), ("cmp_v", 128), ("sel_k", 128), ("sel_v", 128), ("diff_k", 256), ("diff_v", 256)):
        o_p[nm] = P.outp("p_" + nm, [DEPTH, T, w_])
    o_p["win_k"] = P.outp("p_win_k", [DEPTH, 512, 128])
    o_p["win_v"] = P.outp("p_win_v", [DEPTH, 512, 128])
    o_p["conv"] = P.outp("p_conv", [DEPTH, 30, 256])
    o_s = {}
    for nm, w_ in (("cmp_k", 128), ("cmp_v", 128), ("sel_k", 128), ("sel_v", 128), ("diff_k", 256), ("diff_v", 256)):
        o_s[nm] = P.outp("s_" + nm, [DEPTH, NSB, 8, w_])
    o_s["win_k"] = P.outp("s_win_k", [DEPTH, NSB, 512, 128])
    o_s["win_v"] = P.outp("s_win_v", [DEPTH, NSB, 512, 128])
    o_s["conv"] = P.outp("s_conv", [DEPTH, NSB, 30, 256])

    xT_d = P.scratch("xT_d", [128, 8, XC], BF16)
    xmid = P.scratch("xmid", [XC, D], F32)
    wk_d = P.scratch("wk_d", [T, 128], F32)
    wv_d = P.scratch("wv_d", [T, 128], F32)
    u_d = P.scratch("u_d", [XC, 256], F32)
    mixT_d = P.scratch("mixT_d", [128, 8, XC], BF16)
    B_kvd = {k: [] for k in range(NT + NSB)}
    B_mix = Bufs("mix")
    B_xT = Bufs("xTd")
    B_xmid = Bufs("xmid")

    with ExitStack() as gs:
        PS = PsumRot(gs, nc)
        ident_f = gs.enter_context(nc.sbuf_tensor(U("sb_ident_f"), [128, 128], F32))
        ident_b = gs.enter_context(nc.sbuf_tensor(U("sb_ident_b"), [128, 128], BF16))
        B_id = Buf("ident")
        S.dma("sp", ident_f[:], P.din["ident_f"], writes=[B_id])
        S.op("dve", lambda e: e.tensor_copy(out=ident_b[:], in_=ident_f[:]), reads=[B_id], writes=[B_id])

        def evac(i, out, in_, reads, writes):
            if i % 2 == 0:
                return S.op("act", lambda e: e.copy(out=out, in_=in_), reads=reads, writes=writes)
            return S.op("dve", lambda e: e.tensor_copy(out=out, in_=in_), reads=reads, writes=writes)

        def phase_xT(l):
            with ExitStack() as es:
                xin = Rot(es, nc, "xin", [128, D], F32, 3)
                xts = Rot(es, nc, "xts", [128, 8, 128], BF16, 3)
                for n in range(NT + 1):
                    rows = 128 if n < NT else NS
                    t0 = n * 128
                    if l == 0:
                        src = xp[t0:t0 + rows, :] if n < NT else xs
                        rd = []
                    else:
                        src = xmid[t0:t0 + rows, :]
                        rd = [B_xmid[n]]
                    xt, bx = xin.next()
                    S.dma("sp", xt[:rows, :], src, reads=rd, writes=[bx])
                    st, bs = xts.next()
                    for hf in range(2):
                        ps, bp = PS.next()
                        for j in range(4):
                            kc = hf * 4 + j
                            S.op("pe", lambda e: e.transpose(ps[:, j * 128:j * 128 + rows], xt[:rows, kc * 128:(kc + 1) * 128],
                                                             ident_f[:rows, :rows]),
                                 reads=[bx, B_id], writes=[bp], inc=(j == 3))
                        evac(hf, st[:, hf * 4:hf * 4 + 4, :rows],
                             ps[:].rearrange("p (j c) -> p j c", j=4)[:, :, :rows], [bp], [bs])
                    S.dma("sp", xT_d[:, :, t0:t0 + rows], st[:, :, :rows], reads=[bs], writes=[B_xT[n]])

        def rope_tm(rows, out4, in4, cos, sin, nh, half, tmp):
            tt, bt = tmp
            n = nh * half
            cb = cos.unsqueeze(1).broadcast_to([rows, nh, half])
            sb_ = sin.unsqueeze(1).broadcast_to([rows, nh, half])
            x1, x2 = in4[:, :, 0, :], in4[:, :, 1, :]
            tv = [tt[:rows, k * n:(k + 1) * n].rearrange("p (h d) -> p h d", h=nh) for k in range(4)]
            return x1, x2, cb, sb_, tv

        def load_w_cols(es, name, l, c0, ncols):
            wt = es.enter_context(nc.sbuf_tensor(U(name), [128, 8, ncols], BF16))
            bw = Buf(name)
            src = w_in[l].rearrange("(kc p) c -> p kc c", p=128)
            for kc in range(8):
                S.dma("pool", wt[:, kc, :], src[:, kc, c0:c0 + ncols], writes=[bw])
            return wt, bw

        def phase_kv_outputs(l):
            with ExitStack() as es:
                wkv, bwkv = load_w_cols(es, "wkv", l, C_CK, C_GT - C_CK)
                wdf, bwdf = load_w_cols(es, "wdf", l, C_DK, 512)
                wcv, bwcv = load_w_cols(es, "wcv", l, C_CA, 512)
                cosN = es.enter_context(nc.sbuf_tensor(U("sb_cosN"), [128, NT + 1, 32], F32))
                sinN = es.enter_context(nc.sbuf_tensor(U("sb_sinN"), [128, NT + 1, 32], F32))
                cosD = es.enter_context(nc.sbuf_tensor(U("sb_cosD"), [128, NT + 1, 16], F32))
                sinD = es.enter_context(nc.sbuf_tensor(U("sb_sinD"), [128, NT + 1, 16], F32))
                B_tab = Buf("tabs")
                for t_, nm in ((cosN, "cosN"), (sinN, "sinN"), (cosD, "cosD"), (sinD, "sinD")):
                    S.dma("sp", t_[:], P.din[nm], writes=[B_tab])
                xtr = Rot(es, nc, "xtl", [128, 8, 128], BF16, 3)
                zs = Rot(es, nc, "zs", [128, 768 + 512 + 512], F32, 3)
                kr = Rot(es, nc, "kr", [128, 2 * 128 + 256], F32, 3)
                tmp = Rot(es, nc, "rtmp", [128, 4 * 256], F32, 2)

                def do_tile(n, rows, c0, seq):
                    xt, bx = xtr.next()
                    S.dma("sp", xt[:, :, :rows], xT_d[:, :, c0:c0 + rows], reads=[B_xT[min(n, NT)]], writes=[bx])
                    z, bz = zs.next()
                    pieces = [(wkv, bwkv, 0, 512, 0), (wkv, bwkv, 512, 256, 512), (wdf, bwdf, 0, 512, 768), (wcv, bwcv, 0, 512, 1280)]
                    for i, (wt, bw, wc0, wn, zc0) in enumerate(pieces):
                        ps, bp = PS.next()
                        for kc in range(8):
                            S.op("pe", lambda e: e.matmul(ps[:rows, :wn], lhsT=xt[:, kc, :rows], rhs=wt[:, kc, wc0:wc0 + wn],
                                                          start=(kc == 0), stop=(kc == 7)),
                                 reads=[bx, bw], writes=[bp], inc=(kc == 7))
                        evac(i, z[:rows, zc0:zc0 + wn], ps[:rows, :wn], [bp], [bz])
                    k, bk = kr.next()
                    tt, bt = tmp.next()
                    ti = NT if seq is not None else n
                    zin = z[:rows, 256:768].rearrange("p (a g h d) -> p a g h d", a=4, g=2, h=2)[:, 0:4:2]
                    kout = k[:rows, 0:256].rearrange("p (a g h d) -> p a g h d", a=2, g=2, h=2)
                    cb = cosN[:rows, ti, :].unsqueeze(1).broadcast_to([rows, 2, 32])
                    sb_ = sinN[:rows, ti, :].unsqueeze(1).broadcast_to([rows, 2, 32])
                    for a in range(2):
                        x1, x2 = zin[:, a, :, 0, :], zin[:, a, :, 1, :]
                        o1, o2 = kout[:, a, :, 0, :], kout[:, a, :, 1, :]
                        t = [tt[:rows, (a * 4 + j) * 64:(a * 4 + j + 1) * 64].rearrange("p (g d) -> p g d", g=2) for j in range(4)]
                        S.op("dve", lambda e: e.tensor_tensor(out=t[0], in0=x1, in1=cb, op=ALU.mult), reads=[bz, B_tab], writes=[bt])
                        S.op("pool", lambda e: e.tensor_tensor(out=t[1], in0=x2, in1=sb_, op=ALU.mult), reads=[bz, B_tab], writes=[bt])
                        S.op("dve", lambda e: e.tensor_tensor(out=t[2], in0=x2, in1=cb, op=ALU.mult), reads=[bz, B_tab], writes=[bt])
                        S.op("pool", lambda e: e.tensor_tensor(out=t[3], in0=x1, in1=sb_, op=ALU.mult), reads=[bz, B_tab], writes=[bt])
                        S.op("dve", lambda e: e.tensor_tensor(out=o1, in0=t[0], in1=t[1], op=ALU.subtract), reads=[bt], writes=[bk])
                        S.op("pool", lambda e: e.tensor_tensor(out=o2, in0=t[2], in1=t[3], op=ALU.add), reads=[bt], writes=[bk])
                    zd = z[:rows, 768:1024].rearrange("p (h s d) -> p h s d", h=8, s=2)
                    kd = k[:rows, 256:512].rearrange("p (h s d) -> p h s d", h=8, s=2)
                    cbd = cosD[:rows, ti, :].unsqueeze(1).broadcast_to([rows, 8, 16])
                    sbd = sinD[:rows, ti, :].unsqueeze(1).broadcast_to([rows, 8, 16])
                    x1, x2 = zd[:, :, 0, :], zd[:, :, 1, :]
                    t = [tt[:rows, 512 + j * 128:512 + (j + 1) * 128].rearrange("p (h d) -> p h d", h=8) for j in range(4)]
                    S.op("dve", lambda e: e.tensor_tensor(out=t[0], in0=x1, in1=cbd, op=ALU.mult), reads=[bz, B_tab], writes=[bt])
                    S.op("pool", lambda e: e.tensor_tensor(out=t[1], in0=x2, in1=sbd, op=ALU.mult), reads=[bz, B_tab], writes=[bt])
                    S.op("dve", lambda e: e.tensor_tensor(out=t[2], in0=x2, in1=cbd, op=ALU.mult), reads=[bz, B_tab], writes=[bt])
                    S.op("pool", lambda e: e.tensor_tensor(out=t[3], in0=x1, in1=sbd, op=ALU.mult), reads=[bz, B_tab], writes=[bt])
                    S.op("dve", lambda e: e.tensor_tensor(out=kd[:, :, 0, :], in0=t[0], in1=t[1], op=ALU.subtract), reads=[bt], writes=[bk])
                    S.op("pool", lambda e: e.tensor_tensor(out=kd[:, :, 1, :], in0=t[2], in1=t[3], op=ALU.add), reads=[bt], writes=[bk])
                    S.op("act", lambda e: e.activation(out=z[:rows, 1536:1792], in_=z[:rows, 1536:1792], func=AF.Sigmoid), reads=[bz], writes=[bz])
                    S.op("pool", lambda e: e.tensor_tensor(out=z[:rows, 1536:1792], in0=z[:rows, 1280:1536], in1=z[:rows, 1536:1792], op=ALU.mult),
                         reads=[bz], writes=[bz])
                    B_kvd[n].clear()

                    def st_(o_, i_, rd_):
                        B_kvd[n].append(P.store(o_, i_, rd_))
                    if seq is None:
                        r0 = n * 128
                        st_(o_p["cmp_k"][l, r0:r0 + 128, :], z[:, 0:128], [bz])
                        st_(o_p["cmp_v"][l, r0:r0 + 128, :], z[:, 128:256], [bz])
                        st_(o_p["sel_k"][l, r0:r0 + 128, :], k[:, 0:128], [bk])
                        st_(o_p["sel_v"][l, r0:r0 + 128, :], z[:, 384:512], [bz])
                        st_(o_p["diff_k"][l, r0:r0 + 128, :], k[:, 256:512], [bk])
                        st_(o_p["diff_v"][l, r0:r0 + 128, :], z[:, 1024:1280], [bz])
                        if n >= NT - 4:
                            w0 = (n - (NT - 4)) * 128
                            st_(o_p["win_k"][l, w0:w0 + 128, :], k[:, 128:256], [bk])
                            st_(o_p["win_v"][l, w0:w0 + 128, :], z[:, 640:768], [bz])
                        if n == NT - 1:
                            st_(o_p["conv"][l, :, :], z[98:128, 1536:1792], [bz])
                        st_(wk_d[r0:r0 + 128, :], k[:, 128:256], [bk])
                        st_(wv_d[r0:r0 + 128, :], z[:, 640:768], [bz])
                        st_(u_d[r0:r0 + 128, :], z[:, 1536:1792], [bz])
                    else:
                        b = seq
                        st_(o_s["cmp_k"][l, b], z[:8, 0:128], [bz])
                        st_(o_s["cmp_v"][l, b], z[:8, 128:256], [bz])
                        st_(o_s["sel_k"][l, b], k[:8, 0:128], [bk])
                        st_(o_s["sel_v"][l, b], z[:8, 384:512], [bz])
                        st_(o_s["diff_k"][l, b], k[:8, 256:512], [bk])
                        st_(o_s["diff_v"][l, b], z[:8, 1024:1280], [bz])
                        st_(o_s["win_k"][l, b, 504:512, :], k[:8, 128:256], [bk])
                        st_(o_s["win_v"][l, b, 504:512, :], z[:8, 640:768], [bz])
                        st_(o_s["conv"][l, b, 22:30, :], z[:8, 1536:1792], [bz])
                        st_(u_d[T + 8 * b:T + 8 * b + 8, :], z[:8, 1536:1792], [bz])
                        st_(o_s["win_k"][l, b, 0:504, :], st_win_k[l, b, 8:512, :], [])
                        st_(o_s["win_v"][l, b, 0:504, :], st_win_v[l, b, 8:512, :], [])
                        st_(o_s["conv"][l, b, 0:22, :], st_conv[l, b, 8:30, :], [])

                for n in range(NT):
                    do_tile(n, 128, n * 128, None)
                for b in range(NSB):
                    do_tile(NT + b, 8, T + 8 * b, b)

        def sbt(es, name, shape, dt):
            return es.enter_context(nc.sbuf_tensor(U(name), list(shape), dt))

        def cload(es, name, shape, q="pool", dt=BF16):
            t_ = sbt(es, "c_" + name, shape, dt)
            b_ = Buf(name)
            S.dma(q if dt != F32 else "sp", t_[:], P.din[name], writes=[b_])
            return t_, b_

        def rope6(rows, x1, x2, o1, o2, cb, sb_, t, rd, bt, bo):
            S.op("dve", lambda e: e.tensor_tensor(out=t[0], in0=x1, in1=cb, op=ALU.mult), reads=rd, writes=[bt])
            S.op("pool", lambda e: e.tensor_tensor(out=t[1], in0=x2, in1=sb_, op=ALU.mult), reads=rd, writes=[bt])
            S.op("dve", lambda e: e.tensor_tensor(out=t[2], in0=x2, in1=cb, op=ALU.mult), reads=rd, writes=[bt])
            S.op("pool", lambda e: e.tensor_tensor(out=t[3], in0=x1, in1=sb_, op=ALU.mult), reads=rd, writes=[bt])
            S.op("dve", lambda e: e.tensor_tensor(out=o1, in0=t[0], in1=t[1], op=ALU.subtract), reads=[bt], writes=[bo])
            S.op("pool", lambda e: e.tensor_tensor(out=o2, in0=t[2], in1=t[3], op=ALU.add), reads=[bt], writes=[bo])

        def attn(units, kts, rows, depth=3):
            pts = attn.pts
            steps = [(ki, ui) for ki in range(len(kts)) for ui in range(len(units))]
            live = {}

            def stage_a(s_):
                ki, ui = steps[s_]
                kt, u = kts[ki], units[ui]
                nk, nco = kt["nk"], u["ncols"]
                ps, bp = PS.next()
                ms = kt["masks"][ui]
                S.op("pe", lambda e: e.matmul(ps[:nk, :nco], lhsT=kt["K"][ui], rhs=u["Q"], start=True, stop=(len(ms) == 0)),
                     reads=kt["rd"] + u["rdQ"], writes=[bp], inc=(len(ms) == 0))
                for mi, (ml, mr) in enumerate(ms):
                    S.op("pe", lambda e: e.matmul(ps[:nk, :nco], lhsT=ml, rhs=mr, start=False, stop=(mi == len(ms) - 1)),
                         reads=kt["rd"] + u["rdQ"], writes=[bp], inc=(mi == len(ms) - 1))
                pt_, bpt = pts.next()
                S.op("act", lambda e: e.activation(out=pt_[:nk, :nco], in_=ps[:nk, :nco], func=AF.Exp, scale=u["scale"]),
                     reads=[bp], writes=[bpt])
                live[s_] = (pt_, bpt)

            def stage_c(s_):
                ki, ui = steps[s_]
                kt, u = kts[ki], units[ui]
                nk = kt["nk"]
                pt_, bpt = live.pop(s_)
                acc, bacc = u["acc"]
                nb = u["nblk"]
                for blk in range(nb):
                    first = (ki == 0 and blk == 0)
                    last = (ki == len(kts) - 1)
                    v_ = kt["V"][ui]
                    v_ = v_[blk] if isinstance(v_, (list, tuple)) else v_
                    S.op("pe", lambda e: e.matmul(acc[:rows, blk * 65:blk * 65 + 65], lhsT=pt_[:nk, blk * rows:(blk + 1) * rows],
                                                  rhs=v_, start=first, stop=last, skip_group_check=True),
                         reads=[bpt] + kt["rd"], writes=[bacc], inc=(blk == nb - 1))

            for s_ in range(len(steps) + depth):
                if s_ < len(steps):
                    stage_a(s_)
                if s_ - depth >= 0:
                    stage_c(s_ - depth)

        def phase_conv(l):
            with ExitStack() as es:
                uT = sbt(es, "uT", [128, 2, 30 + T], BF16)
                uTs = sbt(es, "uTs", [128, 2, NSB, 38], BF16)
                B_u = Buf("uT")
                Dg = sbt(es, "Dg", [128, 2, 31, 128], BF16)
                B_Dg = Buf("Dg")
                dwT = sbt(es, "dwT", [128, 2, 32], F32)
                dws = sbt(es, "dws", [32, 256], F32)
                prm = sbt(es, "cprm", [128, 3, 2], F32)
                B_prm = Buf("cprm")
                onesF = sbt(es, "onesF", [128, 128], F32)
                B_ones = Buf("ones")
                S.op("pool", lambda e: e.memset(onesF[:], 1.0 / 256.0), writes=[B_ones])
                S.op("pool", lambda e: e.memset(uT[:, :, 0:30], 0.0), writes=[B_u])
                for i, src in enumerate((P.din["conv_dw_b"], P.din["conv_ln_g"], P.din["conv_ln_b"])):
                    S.dma("sp", prm[:, i, :], src[l].rearrange("(h p) -> p h", p=128), writes=[B_prm], allow_slow_non_contiguous=True)
                B_dws = Buf("dws")
                S.dma("sp", dws[:31, :], P.din["conv_dw_w"][l], writes=[B_dws])
                ps, bp = PS.next()
                for h in range(2):
                    S.op("pe", lambda e: e.transpose(ps[:, h * 32:h * 32 + 31], dws[:31, h * 128:(h + 1) * 128], ident_f[:31, :31]),
                         reads=[B_dws, B_id], writes=[bp])
                B_dwT = Buf("dwT")
                S.op("act", lambda e: e.copy(out=dwT[:, :, :], in_=ps[:, 0:64].rearrange("p (h w) -> p h w", h=2)), reads=[bp], writes=[B_dwT])
                for h in range(2):
                    for w in range(31):
                        S.op("dve" if (w % 2 == 0) else "pool",
                             lambda e: e.tensor_scalar(out=Dg[:, h, w, :], in0=ident_f[:], scalar1=dwT[:, h, w:w + 1], scalar2=None, op0=ALU.mult),
                             reads=[B_dwT, B_id], writes=[B_Dg])
                uin = Rot(es, nc, "uin", [128, 256], F32, 3)
                for n in range(NT):
                    ut, bu = uin.next()
                    S.dma("sp", ut[:, :], u_d[n * 128:(n + 1) * 128, :], reads=[B_kvd[n]], writes=[bu])
                    ps, bp = PS.next()
                    for h in range(2):
                        S.op("pe", lambda e: e.transpose(ps[:, h * 128:(h + 1) * 128], ut[:, h * 128:(h + 1) * 128], ident_f[:, :]),
                             reads=[bu, B_id], writes=[bp], inc=(h == 1))
                    evac(n, uT[:, :, 30 + n * 128:30 + (n + 1) * 128], ps[:, 0:256].rearrange("p (h c) -> p h c", h=2), [bp], [B_u])
                for b in range(NSB):
                    ut, bu = uin.next()
                    S.dma("sp", ut[:30, :], st_conv[l, b], writes=[bu])
                    ps, bp = PS.next()
                    for h in range(2):
                        S.op("pe", lambda e: e.transpose(ps[:, h * 32:h * 32 + 30], ut[:30, h * 128:(h + 1) * 128], ident_f[:30, :30]),
                             reads=[bu, B_id], writes=[bp], inc=(h == 1))
                    evac(b, uTs[:, :, b, 0:30], ps[:, 0:64].rearrange("p (h c) -> p h c", h=2)[:, :, 0:30], [bp], [B_u])
                    ut, bu = uin.next()
                    S.dma("sp", ut[:8, :], u_d[T + 8 * b:T + 8 * b + 8, :], reads=[B_kvd[NT + b]], writes=[bu])
                    ps, bp = PS.next()
                    for h in range(2):
                        S.op("pe", lambda e: e.transpose(ps[:, h * 32:h * 32 + 8], ut[:8, h * 128:(h + 1) * 128], ident_f[:8, :8]),
                             reads=[bu, B_id], writes=[bp], inc=(h == 1))
                    evac(b + 1, uTs[:, :, b, 30:38], ps[:, 0:64].rearrange("p (h c) -> p h c", h=2)[:, :, 0:8], [bp], [B_u])
                yv = Rot(es, nc, "cyv", [128, 2, 512], F32, 2)
                ysq = Rot(es, nc, "cysq", [128, 2, 512], F32, 2)
                stt = Rot(es, nc, "cst", [128, 4, 512], F32, 2)
                yo = Rot(es, nc, "cyo", [128, 2, 512], BF16, 2)

                def conv_chunk(rhs_fn, N, dst0):
                    y, by = yv.next()
                    q2, bq2 = ysq.next()
                    for h in range(2):
                        ps, bp = PS.next()
                        for w in range(31):
                            S.op("pe", lambda e: e.matmul(ps[:, :N], lhsT=Dg[:, h, w, :], rhs=rhs_fn(h, w), start=(w == 0), stop=(w == 30)),
                                 reads=[B_Dg, B_u], writes=[bp], inc=(w == 30))
                        S.op("act", lambda e: e.activation(out=y[:, h, :N], in_=ps[:, :N], func=AF.Identity, bias=prm[:, 0, h:h + 1], scale=1.0),
                             reads=[bp, B_prm], writes=[by])
                        S.op("act", lambda e: e.activation(out=q2[:, h, :N], in_=ps[:, :N], func=AF.Square, bias=prm[:, 0, h:h + 1], scale=1.0),
                             reads=[bp, B_prm], writes=[bq2])
                    psm, bpm = PS.next()
                    for h in range(2):
                        S.op("pe", lambda e: e.matmul(psm[:, :N], lhsT=onesF[:], rhs=y[:, h, :N], start=(h == 0), stop=(h == 1)),
                             reads=[B_ones, by], writes=[bpm], inc=(h == 1))
                    pss, bps_ = PS.next()
                    for h in range(2):
                        S.op("pe", lambda e: e.matmul(pss[:, :N], lhsT=onesF[:], rhs=q2[:, h, :N], start=(h == 0), stop=(h == 1)),
                             reads=[B_ones, bq2], writes=[bps_], inc=(h == 1))
                    st_, bst = stt.next()
                    S.op("act", lambda e: e.copy(out=st_[:, 0, :N], in_=psm[:, :N]), reads=[bpm], writes=[bst])
                    S.op("pool", lambda e: e.tensor_tensor(out=st_[:, 1, :N], in0=st_[:, 0, :N], in1=st_[:, 0, :N], op=ALU.mult), reads=[bst], writes=[bst])
                    S.op("dve", lambda e: e.tensor_tensor(out=st_[:, 2, :N], in0=pss[:, :N], in1=st_[:, 1, :N], op=ALU.subtract), reads=[bps_, bst], writes=[bst])
                    S.op("dve", lambda e: e.tensor_scalar(out=st_[:, 2, :N], in0=st_[:, 2, :N], scalar1=0.0, scalar2=EPS, op0=ALU.max, op1=ALU.add), reads=[bst], writes=[bst])
                    S.op("act", lambda e: e.activation(out=st_[:, 3, :N], in_=st_[:, 2, :N], func=AF.Sqrt), reads=[bst], writes=[bst])
                    S.op("dve", lambda e: e.reciprocal(out=st_[:, 3, :N], in_=st_[:, 3, :N]), reads=[bst], writes=[bst])
                    o_, bo = yo.next()
                    for h in range(2):
                        S.op("dve", lambda e: e.tensor_tensor(out=y[:, h, :N], in0=y[:, h, :N], in1=st_[:, 0, :N], op=ALU.subtract), reads=[bst, by], writes=[by])
                        S.op("pool", lambda e: e.tensor_tensor(out=y[:, h, :N], in0=y[:, h, :N], in1=st_[:, 3, :N], op=ALU.mult), reads=[bst, by], writes=[by])
                        S.op("act", lambda e: e.activation(out=o_[:, h, :N], in_=y[:, h, :N], func=AF.Silu, bias=prm[:, 2, h:h + 1], scale=prm[:, 1, h:h + 1]),
                             reads=[by, B_prm], writes=[bo])
                    S.dma("sp", mixT_d[:, 0:2, dst0:dst0 + N], o_[:, :, :N], reads=[bo], writes=[B_mix[0]])

                for c in range(T // 512):
                    conv_chunk(lambda h, w, c=c: uT[:, h, c * 512 + w:c * 512 + w + 512], 512, c * 512)
                for b in range(NSB):
                    conv_chunk(lambda h, w, b=b: uTs[:, h, b, w:w + 8], 8, T + 8 * b)

        def phase_nsa(l):
            with ExitStack() as es:
                KT = sbt(es, "KT", [128, 2, 8208], BF16)
                KT4 = KT[:].rearrange("p a c -> p (a c)").rearrange("p (a c) -> p a c", a=4)
                B_K = Bufs("KT")
                svA = sbt(es, "svA", [128, 65, 2, 65], BF16)
                wvA = sbt(es, "wvA", [128, 32, 2, 65], BF16)
                B_V = Bufs("VA")
                S.op("pool", lambda e: e.memset(svA[:, :, :, 64:65], 1.0), writes=[B_V["s"]])
                S.op("pool", lambda e: e.memset(wvA[:, :, :, 64:65], 1.0), writes=[B_V["w"]])
                Em = sbt(es, "Em", [128, 64, 128], BF16)
                B_E = Buf("E")
                S.dma("pool", Em[:, 0:32, :], P.din["Ep"], writes=[B_E])
                M0, B_M0 = cload(es, "M0", [128, 503], dt=F32)
                Ft, B_Ft = cload(es, "Ftab", [128, 32, 64], dt=F32)
                Fs, B_Fs = cload(es, "Fs", [128, 129], dt=F32)
                Cm4, B_Cm4 = cload(es, "Cm4", [128, 4, 128])
                Bu4, B_Bu4 = cload(es, "Bu4", [128, 4, 128])
                CsN, B_CsN = cload(es, "CsN", [8, 4, 8])
                WsN, B_WsN = cload(es, "WsN", [128, 4, 8])
                cosN, B_cos = cload(es, "cosN", [128, NT + 1, 32], dt=F32)
                sinN, B_sin = cload(es, "sinN", [128, NT + 1, 32], dt=F32)
                B_tab = [B_cos, B_sin]
                wq, bwq = load_w_cols(es, "wq", l, C_NQ, 512)
                wg, bwg = load_w_cols(es, "wg", l, C_GT, 24)
                W1 = [sbt(es, "W1k", [128, 32, 128], BF16), sbt(es, "W1v", [128, 32, 128], BF16)]
                B_W1 = Buf("W1")
                for kv, nm in enumerate(("cmp_w1_k", "cmp_w1_v")):
                    src = P.din[nm][l].rearrange("(r d) h -> d r h", d=64)
                    for hf in range(2):
                        S.dma("pool", W1[kv][64 * hf:64 * hf + 64, :, :], src, writes=[B_W1])
                W2kz = sbt(es, "W2kz", [128, 2, 128], BF16)
                W2v = sbt(es, "W2v", [128, 64], BF16)
                B_W2 = Buf("W2")
                S.op("pool", lambda e: e.memset(W2kz[:], 0.0), writes=[B_W2])
                for g in range(2):
                    S.dma("pool", W2kz[:, g, 64 * g:64 * g + 64], P.din["cmp_w2_k"][l], writes=[B_W2])
                S.dma("pool", W2v[:, :], P.din["cmp_w2_v"][l], writes=[B_W2])
                b1t = sbt(es, "b1t", [128, 2], F32)
                c1 = sbt(es, "c1", [128, 2], F32)
                B_c1 = Buf("c1")
                pes = sbt(es, "pes", [32, 2, 64], F32)
                peT = sbt(es, "peT", [64, 2, 32], BF16)
                for kv, (nb_, npe) in enumerate((("cmp_b1_k", "cmp_pe_k"), ("cmp_b1_v", "cmp_pe_v"))):
                    S.dma("sp", b1t[:, kv:kv + 1], P.din[nb_][l].rearrange("(p o) -> p o", o=1), writes=[B_c1])
                    S.dma("sp", pes[:, kv, :], P.din[npe][l], writes=[B_c1])
                ps, bp = PS.next()
                for kv in range(2):
                    S.op("pe", lambda e: e.transpose(ps[:64, kv * 32:kv * 32 + 32], pes[:32, kv, :], ident_f[:32, :32]), reads=[B_c1, B_id], writes=[bp], inc=(kv == 1))
                S.op("act", lambda e: e.copy(out=peT[:, :, :], in_=ps[:64, 0:64].rearrange("p (k r) -> p k r", k=2)), reads=[bp], writes=[B_c1])
                for kv in range(2):
                    ps, bp = PS.next()
                    for r in range(32):
                        S.op("pe", lambda e: e.matmul(ps[:, 0:1], lhsT=W1[kv][0:64, r, :], rhs=peT[0:64, kv, r:r + 1], start=(r == 0), stop=(r == 31)),
                             reads=[B_W1, B_c1], writes=[bp], inc=(r == 31))
                    S.op("dve", lambda e: e.tensor_tensor(out=c1[:, kv:kv + 1], in0=ps[:, 0:1], in1=b1t[:, kv:kv + 1], op=ALU.add), reads=[bp, B_c1], writes=[B_c1])
                kccT = sbt(es, "kccT", [128, 512], BF16)
                vcc = sbt(es, "vcc", [128, 4, 2, 64], BF16)
                B_cc = Buf("cc")
                kin = Rot(es, nc, "kin", [128, 128], F32, 10)
                gel = Rot(es, nc, "gel", [128, 512], BF16, 2)
                xtr = Rot(es, nc, "xtq", [128, 8, 128], BF16, 2)
                zq = Rot(es, nc, "zq", [128, 512], F32, 2)
                qrr = Rot(es, nc, "qr", [128, 512], F32, 1)
                rtmp = Rot(es, nc, "qtmp", [128, 1024], F32, 1)
                gts = Rot(es, nc, "gts", [128, 24], F32, 2)
                qbs = Rot(es, nc, "qb", [128, 2, 4, 128], BF16, 2)
                qTp = Rot(es, nc, "qTp", [128, 2, 4, 128], BF16, 2)
                qTs = Rot(es, nc, "qTs", [128, 2, 4, 8], BF16, 2)
                smr = Rot(es, nc, "sm", [128, 512], F32, 4)
                pfr = Rot(es, nc, "pf", [128, 512], F32, 4)
                pnr = Rot(es, nc, "pn", [128, 512], BF16, 4)
                pnT = Rot(es, nc, "pnT", [128, 4, 128], BF16, 4)
                sml = Rot(es, nc, "sml", [128, 8], F32, 8)
                ppad = sbt(es, "ppad", [128, 2, 528], F32)
                B_pp = Buf("ppad")
                scr = Rot(es, nc, "scr", [128, 3, 136], F32, 2)
                m8 = Rot(es, nc, "m8", [128, 16], F32, 2)
                selm = Rot(es, nc, "selm", [128, 2, 128], BF16, 2)
                selTp = Rot(es, nc, "selTp", [128, 4, 128], BF16, 2)
                selTs = Rot(es, nc, "selTs", [128, 2, 4, 8], BF16, 2)
                onsa = Rot(es, nc, "onsa", [128, 512], F32, 2)
                onb = Rot(es, nc, "onb", [128, 512], BF16, 2)
                ost = Rot(es, nc, "ost", [128, 4, 128], BF16, 2)
                attn.pts = Rot(es, nc, "ptn", [128, 512], BF16, 4)

                def load_kT(dst_fn, src_fn, ntiles, rows_fn, rd_fn, wb, gather_idx=None):
                    kt = 0
                    while kt < ntiles:
                        nb = min(4, ntiles - kt)
                        ps, bp = PS.next()
                        tot = 0
                        for j in range(nb):
                            rows = rows_fn(kt + j)
                            kt_, bk_ = kin.next()
                            if gather_idx is None:
                                S.dma("sp", kt_[:rows, :], src_fn(kt + j), reads=rd_fn(kt + j), writes=[bk_])
                            else:
                                S.gather(kt_[:rows, :], src_fn(kt + j), gather_idx(kt + j), reads=rd_fn(kt + j), writes=[bk_])
                            S.op("pe", lambda e: e.transpose(ps[:, j * 128:j * 128 + rows], kt_[:rows, :], ident_f[:rows, :rows]),
                                 reads=[bk_, B_id], writes=[bp], inc=(j == nb - 1))
                            tot = j * 128 + rows
                        evac(kt // 4, dst_fn(kt, tot), ps[:, :tot], [bp], [wb])
                        kt += nb

                def load_v(dstA, src_fn, ntiles, rows_fn, rd_fn, wb, gather_idx=None):
                    for kt in range(ntiles):
                        rows = rows_fn(kt)
                        kt_, bk_ = kin.next()
                        if gather_idx is None:
                            S.dma("sp", kt_[:rows, :], src_fn(kt), reads=rd_fn(kt), writes=[bk_])
                        else:
                            S.gather(kt_[:rows, :], src_fn(kt), gather_idx(kt), reads=rd_fn(kt), writes=[bk_])
                        S.op("dve" if kt % 2 == 0 else "act",
                             (lambda e: e.tensor_copy(out=dstA[:rows, kt, :, 0:64], in_=kt_[:rows, :].rearrange("p (g d) -> p g d", g=2))) if kt % 2 == 0 else
                             (lambda e: e.copy(out=dstA[:rows, kt, :, 0:64], in_=kt_[:rows, :].rearrange("p (g d) -> p g d", g=2))),
                             reads=[bk_], writes=[wb])

                def compress(nblk, srcK, srcV, rdb):
                    taken, ks = PS.take(1)
                    (psK, bpK) = taken[0]
                    nch = (nblk + 127) // 128
                    for kv, src in enumerate((srcK, srcV)):
                        for g in range(2):
                            ps, bp = PS.next()
                            for r in range(32):
                                S.op("pe", lambda e: e.matmul(ps[:, :nblk], lhsT=W1[kv][64 * g:64 * g + 64, r, :],
                                                              rhs=src[64 * g:64 * g + 64, r:r + 16 * (nblk - 1) + 1:16], start=(r == 0), stop=(r == 31)),
                                     reads=[B_W1] + rdb, writes=[bp], inc=(r == 31))
                            ge, bge = gel.next()
                            S.op("act", lambda e: e.activation(out=ge[:, :nblk], in_=ps[:, :nblk], func=AF.Gelu, bias=c1[:, kv:kv + 1], scale=1.0),
                                 reads=[bp, B_c1], writes=[bge])
                            if kv == 0:
                                S.op("pe", lambda e: e.matmul(psK[:, :nblk], lhsT=W2kz[:, g, :], rhs=ge[:, :nblk], start=(g == 0), stop=(g == 1)),
                                     reads=[bge, B_W2], writes=[bpK])
                            else:
                                ps2, bp2 = PS.next()
                                for ch in range(nch):
                                    ncr = min(128, nblk - ch * 128)
                                    S.op("pe", lambda e: e.matmul(ps2[:ncr, ch * 64:ch * 64 + 64], lhsT=ge[:, ch * 128:ch * 128 + ncr], rhs=W2v[:, :],
                                                                  start=True, stop=True), reads=[bge, B_W2], writes=[bp2], inc=(ch == nch - 1))
                                for ch in range(nch):
                                    ncr = min(128, nblk - ch * 128)
                                    evac(ch, vcc[:ncr, ch, g, :], ps2[:ncr, ch * 64:ch * 64 + 64], [bp2], [B_cc])
                        if kv == 0:
                            evac(0, kccT[:, :nblk], psK[:, :nblk], [bpK], [B_cc])
                    PS.give(ks)

                def qtile(n, rows, c0, seq, sel_kts, win_kts, nvis, dst0):
                    samp = seq is not None
                    ti = NT if samp else n
                    xt, bx = xtr.next()
                    S.dma("sp", xt[:, :, :rows], xT_d[:, :, c0:c0 + rows], reads=[B_xT[min(n, NT)]], writes=[bx])
                    ps, bp = PS.next()
                    for kc in range(8):
                        S.op("pe", lambda e: e.matmul(ps[:rows, :512], lhsT=xt[:, kc, :rows], rhs=wq[:, kc, :], start=(kc == 0), stop=(kc == 7)),
                             reads=[bx, bwq], writes=[bp], inc=(kc == 7))
                    z, bz = zq.next()
                    S.op("act", lambda e: e.copy(out=z[:rows, :], in_=ps[:rows, :512]), reads=[bp], writes=[bz])
                    ps, bp = PS.next()
                    for kc in range(8):
                        S.op("pe", lambda e: e.matmul(ps[:rows, :24], lhsT=xt[:, kc, :rows], rhs=wg[:, kc, :], start=(kc == 0), stop=(kc == 7)),
                             reads=[bx, bwg], writes=[bp], inc=(kc == 7))
                    gt, bg = gts.next()
                    S.op("act", lambda e: e.activation(out=gt[:rows, :], in_=ps[:rows, :24], func=AF.Sigmoid), reads=[bp], writes=[bg])
                    qr, bqr = qrr.next()
                    tt, bt = rtmp.next()
                    zin = z[:rows, :].rearrange("p (h s d) -> p h s d", h=8, s=2)
                    qo = qr[:rows, :].rearrange("p (h s d) -> p h s d", h=8, s=2)
                    cb = cosN[:rows, ti, :].unsqueeze(1).broadcast_to([rows, 8, 32])
                    sb_ = sinN[:rows, ti, :].unsqueeze(1).broadcast_to([rows, 8, 32])
                    t4 = [tt[:rows, j * 256:(j + 1) * 256].rearrange("p (h d) -> p h d", h=8) for j in range(4)]
                    rope6(rows, zin[:, :, 0, :], zin[:, :, 1, :], qo[:, :, 0, :], qo[:, :, 1, :], cb, sb_, t4, [bz] + B_tab, bt, bqr)
                    qb, bqb = qbs.next()
                    for v, (src, bsrc) in enumerate(((z, bz), (qr, bqr))):
                        S.op("pool", lambda e: e.tensor_copy(out=qb[:rows, v].rearrange("t p (g d) -> t g p d", g=2),
                                                             in_=src[:rows, :].rearrange("t (g p d) -> t g p d", g=2, p=4)), reads=[bsrc], writes=[bqb])
                    ps, bp = PS.next()
                    psb = ps[:].bitcast(BF16)
                    for v in range(2):
                        for p in range(4):
                            j = v * 4 + p
                            S.op("pe", lambda e: e.transpose(psb[:, j * 128:j * 128 + rows], qb[:rows, v, p, :], ident_b[:rows, :rows]),
                                 reads=[bqb, B_id], writes=[bp], inc=(j == 7))
                    if samp:
                        qT, bqT = qTs.next()
                    else:
                        qT, bqT = qTp.next()
                    evac(0, qT[:, :, :, :rows], psb[:, :].rearrange("p (v q c) -> p v q c", v=2, q=4)[:, :, :, :rows], [bp], [bqT])
                    on, bon = onsa.next()
                    S.op("pool", lambda e: e.memset(ppad[:rows, :, :], 0.0), writes=[B_pp])
                    nch = (nvis + 127) // 128
                    for g in range(2):
                        psl, pfl, rsl, pnl, ps2l, pTl, ps3l = [], [], [], [], [], [], []
                        for p in range(4):
                            ps, bp = PS.next()
                            S.op("pe", lambda e: e.matmul(ps[:rows, :nvis], lhsT=qT[64 * g:64 * g + 64, 0, p, :rows], rhs=kccT[64 * g:64 * g + 64, :nvis],
                                                          start=True, stop=True), reads=[bqT, B_cc], writes=[bp])
                            psl.append((ps, bp))
                        for p in range(4):
                            ps, bp = psl[p]
                            pf, bpf = pfr.next()
                            rs, brs = sml.next()
                            if not samp:
                                sm, bsm = smr.next()
                                off = 248 - 8 * n
                                S.op("dve", lambda e: e.tensor_tensor(out=sm[:rows, :nvis], in0=ps[:rows, :nvis], in1=M0[:rows, off:off + nvis], op=ALU.add),
                                     reads=[bp, B_M0], writes=[bsm])
                                S.op("act", lambda e: e.activation(out=pf[:rows, :nvis], in_=sm[:rows, :nvis], func=AF.Exp, scale=0.125, accum_out=rs[:rows, 0:1]),
                                     reads=[bsm], writes=[bpf, brs])
                            else:
                                S.op("act", lambda e: e.activation(out=pf[:rows, :nvis], in_=ps[:rows, :nvis], func=AF.Exp, scale=0.125, accum_out=rs[:rows, 0:1]),
                                     reads=[bp], writes=[bpf, brs])
                            pfl.append((pf, bpf))
                            rsl.append((rs, brs))
                        for p in range(4):
                            pf, bpf = pfl[p]
                            rs, brs = rsl[p]
                            S.op("dve", lambda e: e.tensor_scalar(out=rs[:rows, 1:2], in0=rs[:rows, 0:1], scalar1=1e-30, scalar2=None, op0=ALU.max), reads=[brs], writes=[brs])
                            S.op("dve", lambda e: e.reciprocal(out=rs[:rows, 2:3], in_=rs[:rows, 1:2]), reads=[brs], writes=[brs])
                            pn, bpn = pnr.next()
                            S.op("dve", lambda e: e.tensor_scalar(out=pn[:rows, :nvis], in0=pf[:rows, :nvis], scalar1=rs[:rows, 2:3], scalar2=None, op0=ALU.mult),
                                 reads=[bpf, brs], writes=[bpn])
                            if p == 0:
                                S.op("pool", lambda e: e.tensor_scalar(out=ppad[:rows, g, 1:1 + nvis], in0=pf[:rows, :nvis], scalar1=rs[:rows, 2:3], scalar2=None, op0=ALU.mult),
                                     reads=[bpf, brs], writes=[B_pp])
                            else:
                                S.op("dve", lambda e: e.scalar_tensor_tensor(out=ppad[:rows, g, 1:1 + nvis], in0=pf[:rows, :nvis], scalar=rs[:rows, 2:3],
                                                                             in1=ppad[:rows, g, 1:1 + nvis], op0=ALU.mult, op1=ALU.add),
                                     reads=[bpf, brs], writes=[B_pp])
                            pnl.append((pn, bpn))
                        for p in range(4):
                            pn, bpn = pnl[p]
                            ps2, bp2 = PS.next()
                            ps2b = ps2[:].bitcast(BF16)
                            for ch in range(nch):
                                ncr = min(128, nvis - ch * 128)
                                S.op("pe", lambda e: e.transpose(ps2b[:ncr, ch * 128:ch * 128 + rows], pn[:rows, ch * 128:ch * 128 + ncr], ident_b[:rows, :rows]),
                                     reads=[bpn, B_id], writes=[bp2], inc=(ch == nch - 1))
                            ps2l.append((ps2b, bp2))
                        for p in range(4):
                            ps2b, bp2 = ps2l[p]
                            pT, bpT = pnT.next()
                            for ch in range(nch):
                                ncr = min(128, nvis - ch * 128)
                                evac(ch + p, pT[:ncr, ch, :rows], ps2b[:ncr, ch * 128:ch * 128 + rows], [bp2], [bpT])
                            pTl.append((pT, bpT))
                        for p in range(4):
                            pT, bpT = pTl[p]
                            ps3, bp3 = PS.next()
                            for ch in range(nch):
                                ncr = min(128, nvis - ch * 128)
                                S.op("pe", lambda e: e.matmul(ps3[:rows, 0:64], lhsT=pT[:ncr, ch, :rows], rhs=vcc[:ncr, ch, g, :], start=(ch == 0), stop=(ch == nch - 1)),
                                     reads=[bpT, B_cc], writes=[bp3], inc=(ch == nch - 1))
                            ps3l.append((ps3, bp3))
                        for p in range(4):
                            h = 4 * g + p
                            ps3, bp3 = ps3l[p]
                            S.op("dve", lambda e: e.tensor_scalar(out=on[:rows, h * 64:(h + 1) * 64], in0=ps3[:rows, 0:64], scalar1=gt[:rows, h * 3:h * 3 + 1], scalar2=None, op0=ALU.mult),
                                 reads=[bp3, bg], writes=[bon])
                    nsel = 129 if samp else 64
                    nselm = 128 if samp else 64
                    sm_, bsm_ = selm.next()
                    for g in range(2):
                        sc, bsc = scr.next()
                        sl = [ppad[:rows, g, i:i + 4 * (nsel - 1) + 1:4] for i in range(5)]
                        S.op("dve", lambda e: e.tensor_tensor(out=sc[:rows, 0, :nsel], in0=sl[0], in1=sl[1], op=ALU.add), reads=[B_pp], writes=[bsc])
                        S.op("pool", lambda e: e.tensor_tensor(out=sc[:rows, 1, :nsel], in0=sl[2], in1=sl[3], op=ALU.add), reads=[B_pp], writes=[bsc])
                        S.op("dve", lambda e: e.tensor_tensor(out=sc[:rows, 0, :nsel], in0=sc[:rows, 0, :nsel], in1=sl[4], op=ALU.add), reads=[B_pp, bsc], writes=[bsc])
                        S.op("dve", lambda e: e.tensor_tensor(out=sc[:rows, 0, :nsel], in0=sc[:rows, 0, :nsel], in1=sc[:rows, 1, :nsel], op=ALU.add), reads=[bsc], writes=[bsc])
                        ftab = Fs[:rows, :nsel] if samp else Ft[:rows, n, :]
                        S.op("dve", lambda e: e.tensor_tensor(out=sc[:rows, 0, :nsel], in0=sc[:rows, 0, :nsel], in1=ftab, op=ALU.add), reads=[bsc, B_Fs, B_Ft], writes=[bsc])
                        mm_, bmm = m8.next()
                        S.op("dve", lambda e: e.max(out=mm_[:rows, 0:8], in_=sc[:rows, 0, :nsel]), reads=[bsc], writes=[bmm])
                        S.op("dve", lambda e: e.match_replace(out=sc[:rows, 2, :nsel], in_to_replace=mm_[:rows, 0:8], in_values=sc[:rows, 0, :nsel], imm_value=-3e9),
                             reads=[bsc, bmm], writes=[bsc])
                        S.op("dve", lambda e: e.max(out=mm_[:rows, 8:16], in_=sc[:rows, 2, :nsel]), reads=[bsc], writes=[bmm])
                        smo = sm_[:rows, g, :nselm] if samp else sm_[:rows, :, :].rearrange("p g j -> p (g j)")[:, g * 64:g * 64 + 64]
                        S.op("dve", lambda e: e.tensor_scalar(out=smo, in0=sc[:rows, 0, :nselm], scalar1=mm_[:rows, 15:16], scalar2=1.0,
                                                              op0=ALU.is_ge, op1=ALU.subtract), reads=[bsc, bmm], writes=[bsm_])
                    ps, bp = PS.next()
                    psb = ps[:].bitcast(BF16)
                    if not samp:
                        S.op("pe", lambda e: e.transpose(psb[:, 0:rows], sm_[:rows, :, :].rearrange("p g j -> p (g j)")[:, 0:128], ident_b[:rows, :rows]), reads=[bsm_, B_id], writes=[bp])
                        sT, bsT = selTp.next()
                        for p in range(4):
                            evac(p, sT[:, p, :rows], psb[:, 0:rows], [bp], [bsT])
                    else:
                        for g in range(2):
                            S.op("pe", lambda e: e.transpose(psb[:, g * 8:g * 8 + rows], sm_[:rows, g, :], ident_b[:rows, :rows]), reads=[bsm_, B_id], writes=[bp], inc=(g == 1))
                        sT, bsT = selTs.next()
                        for p in range(4):
                            evac(p, sT[:, :, p, :], psb[:, 0:16].rearrange("p (g t) -> p g t", g=2), [bp], [bsT])
                    ncols = 4 * rows
                    for br, kts_desc in ((1, sel_kts), (2, win_kts)):
                        taken, ks = PS.take(2)
                        units = []
                        for g in range(2):
                            Q = qT[64 * g:64 * g + 64, 1, :, :rows].rearrange("p q c -> p (q c)")
                            units.append(dict(Q=Q, ncols=ncols, scale=0.125, nblk=4, acc=taken[g], rdQ=[bqT, bsT, B_E, B_Cm4, B_Bu4, B_CsN, B_WsN, B_id]))
                        kts = []
                        for kd in kts_desc:
                            nk = kd["nk"]
                            c_ = kd["col"]
                            arr = kd["arr"]
                            K = [arr[64 * g:64 * g + 64, c_:c_ + nk] for g in range(2)]
                            V = [kd["VA"][:nk, kd["vt"], g, :] for g in range(2)]
                            masks = []
                            for g in range(2):
                                ml = []
                                for mk in kd["masks"]:
                                    if mk[0] == "E":
                                        ml.append((Em[64 * g:64 * g + 64, mk[1], :nk], sT[64 * g:64 * g + 64, :, :rows].rearrange("p q c -> p (q c)")))
                                    elif mk[0] == "Es":
                                        ml.append((Em[:, mk[1], :nk], sT[:, g, :, :].rearrange("p q c -> p (q c)")))
                                    elif mk[0] == "C":
                                        ml.append((ident_b[:nk, :nk], Cm4[:nk, :, :].rearrange("p q c -> p (q c)")))
                                    elif mk[0] == "B":
                                        ml.append((ident_b[:nk, :nk], Bu4[:nk, :, :].rearrange("p q c -> p (q c)")))
                                    elif mk[0] == "Cs":
                                        ml.append((ident_b[:nk, :nk], CsN[:nk, :, :].rearrange("p q c -> p (q c)")))
                                    elif mk[0] == "Ws":
                                        ml.append((ident_b[:nk, :nk], WsN[:nk, :, :].rearrange("p q c -> p (q c)")))
                                masks.append(ml)
                            kts.append(dict(nk=nk, K=K, V=V, masks=masks, rd=kd["rd"]))
                        attn(units, kts, rows)
                        for g in range(2):
                            acc, bacc = taken[g]
                            for p in range(4):
                                h = 4 * g + p
                                rs, brs = sml.next()
                                S.op("dve", lambda e: e.tensor_scalar(out=rs[:rows, 0:1], in0=acc[:rows, p * 65 + 64:p * 65 + 65], scalar1=1e-30, scalar2=None, op0=ALU.max),
                                     reads=[bacc], writes=[brs])
                                S.op("dve", lambda e: e.reciprocal(out=rs[:rows, 1:2], in_=rs[:rows, 0:1]), reads=[brs], writes=[brs])
                                S.op("dve", lambda e: e.tensor_tensor(out=rs[:rows, 2:3], in0=rs[:rows, 1:2], in1=gt[:rows, h * 3 + br:h * 3 + br + 1], op=ALU.mult),
                                     reads=[brs, bg], writes=[brs])
                                S.op("dve", lambda e: e.scalar_tensor_tensor(out=on[:rows, h * 64:(h + 1) * 64], in0=acc[:rows, p * 65:p * 65 + 64], scalar=rs[:rows, 2:3],
                                                                             in1=on[:rows, h * 64:(h + 1) * 64], op0=ALU.mult, op1=ALU.add),
                                     reads=[bacc, brs, bon], writes=[bon])
                        PS.give(ks)
                    ob, bob = onb.next()
                    S.op("pool", lambda e: e.tensor_copy(out=ob[:rows, :], in_=on[:rows, :]), reads=[bon], writes=[bob])
                    ps, bp = PS.next()
                    psb = ps[:].bitcast(BF16)
                    for j in range(4):
                        S.op("pe", lambda e: e.transpose(psb[:, j * 128:j * 128 + rows], ob[:rows, j * 128:(j + 1) * 128], ident_b[:rows, :rows]),
                             reads=[bob, B_id], writes=[bp], inc=(j == 3))
                    os_, bos = ost.next()
                    evac(1, os_[:, :, :rows], psb[:, 0:512].rearrange("p (j c) -> p j c", j=4)[:, :, :rows], [bp], [bos])
                    S.dma("sp", mixT_d[:, 2:6, dst0:dst0 + rows], os_[:, :, :rows], reads=[bos], writes=[B_mix[1]])

                full = lambda kt: 128
                srcs = ((o_p["sel_k"][l], 0), (wk_d, 1), (o_p["cmp_k"][l], 2), (o_p["cmp_v"][l], 3))
                for src, a in srcs:
                    load_kT(lambda kt0, tot, a=a: KT4[:, a, kt0 * 128:kt0 * 128 + tot], lambda kt, src=src: src[kt * 128:(kt + 1) * 128, :],
                            NT, full, lambda kt: [B_kvd[kt]], B_K[a])
                load_v(svA, lambda kt: o_p["sel_v"][l, kt * 128:(kt + 1) * 128, :], NT, full, lambda kt: [B_kvd[kt]], B_V["s"])
                load_v(wvA, lambda kt: wv_d[kt * 128:(kt + 1) * 128, :], NT, full, lambda kt: [B_kvd[kt]], B_V["w"])
                compress(255, KT4[:, 2, :], KT4[:, 3, :], [B_K[2], B_K[3]])
                for n in range(NT):
                    sel_kts = []
                    for kt in range(n + 1):
                        mk = [("E", kt)] + ([("C",)] if kt == n else [])
                        sel_kts.append(dict(nk=128, arr=KT4[:, 0, :], col=kt * 128, VA=svA, vt=kt, masks=mk, rd=[B_K[0], B_V["s"]]))
                    win_kts = []
                    for kt in range(max(0, n - 4), n + 1):
                        mk = []
                        if kt == n - 4:
                            mk.append(("B",))
                        if kt == n:
                            mk.append(("C",))
                        win_kts.append(dict(nk=128, arr=KT4[:, 1, :], col=kt * 128, VA=wvA, vt=kt, masks=mk, rd=[B_K[1], B_V["w"]]))
                    nvis = min(255, max(0, 8 * n + 7))
                    qtile(n, 128, n * 128, None, sel_kts, win_kts, nvis, n * 128)
                S.barrier()
                S.dma("pool", Em[:, :, :], P.din["Es"], writes=[B_E])
                ptb = sbt(es, "ptb", [128, NSB, NPG], I32)
                idx = sbt(es, "idx", [128, NSB, NPG], I32)
                iot = sbt(es, "iot", [128, 1], I32)
                B_idx = Buf("idx")
                S.dma("sp", ptb[:].rearrange("p b j -> p (b j)"), pt.rearrange("b j -> (b j)").unsqueeze(0).broadcast_to([128, NSB * NPG]), writes=[B_idx])
                S.op("pool", lambda e: e.iota(iot[:], pattern=[[0, 1]], base=l * NPOOL * 128, channel_multiplier=1), writes=[B_idx])
                S.op("dve", lambda e: e.tensor_scalar(out=idx[:].rearrange("p b j -> p (b j)"), in0=ptb[:].rearrange("p b j -> p (b j)"), scalar1=128.0,
                                                      scalar2=iot[:, 0:1], op0=ALU.mult, op1=ALU.add), reads=[B_idx], writes=[B_idx])
                for b in range(NSB):
                    gi = lambda kt, b=b: idx[:, b, kt:kt + 1]
                    load_kT(lambda kt0, tot: KT[:, 0, kt0 * 128:kt0 * 128 + tot], lambda kt: c_cmp_k, NPG, full, lambda kt: [B_idx], B_K["a0"], gather_idx=gi)
                    load_kT(lambda kt0, tot: KT[:, 1, kt0 * 128:kt0 * 128 + tot], lambda kt: c_cmp_v, NPG, full, lambda kt: [B_idx], B_K["a1"], gather_idx=gi)
                    compress(511, KT[:, 0, :], KT[:, 1, :], [B_K["a0"], B_K["a1"]])
                    S.barrier()
                    load_kT(lambda kt0, tot: KT[:, 0, kt0 * 128:kt0 * 128 + tot], lambda kt: c_sel_k, NPG, full, lambda kt: [B_idx], B_K["b0"], gather_idx=gi)
                    load_kT(lambda kt0, tot: KT[:, 0, PAST:PAST + 8], lambda kt: o_s["sel_k"][l, b], 1, lambda kt: 8, lambda kt: [B_kvd[NT + b]], B_K["b0"])
                    load_kT(lambda kt0, tot: KT[:, 1, kt0 * 128:kt0 * 128 + tot], lambda kt: st_win_k[l, b, kt * 128:(kt + 1) * 128, :], 4, full, lambda kt: [], B_K["b1"])
                    load_kT(lambda kt0, tot: KT[:, 1, 512:520], lambda kt: o_s["win_k"][l, b, 504:512, :], 1, lambda kt: 8, lambda kt: [B_kvd[NT + b]], B_K["b1"])
                    load_v(svA, lambda kt: c_sel_v, NPG, full, lambda kt: [B_idx], B_V["s"], gather_idx=gi)
                    kt_, bk_ = kin.next()
                    S.dma("sp", kt_[:8, :], o_s["sel_v"][l, b], reads=[B_kvd[NT + b]], writes=[bk_])
                    S.op("pool", lambda e: e.tensor_copy(out=svA[:8, 64, :, 0:64], in_=kt_[:8, :].rearrange("p (g d) -> p g d", g=2)), reads=[bk_], writes=[B_V["s"]])
                    load_v(wvA, lambda kt: st_win_v[l, b, kt * 128:(kt + 1) * 128, :], 4, full, lambda kt: [], B_V["w"])
                    kt_, bk_ = kin.next()
                    S.dma("sp", kt_[:8, :], o_s["win_v"][l, b, 504:512, :], reads=[B_kvd[NT + b]], writes=[bk_])
                    S.op("pool", lambda e: e.tensor_copy(out=wvA[:8, 4, :, 0:64], in_=kt_[:8, :].rearrange("p (g d) -> p g d", g=2)), reads=[bk_], writes=[B_V["w"]])
                    sel_kts = [dict(nk=128, arr=KT[:, 0, :], col=kt * 128, VA=svA, vt=kt, masks=[("Es", kt)], rd=[B_K["b0"], B_V["s"]]) for kt in range(NPG)]
                    sel_kts.append(dict(nk=8, arr=KT[:, 0, :], col=PAST, VA=svA, vt=64, masks=[("Cs",)], rd=[B_K["b0"], B_V["s"]]))
                    win_kts = [dict(nk=128, arr=KT[:, 1, :], col=kt * 128, VA=wvA, vt=kt, masks=([("Ws",)] if kt == 0 else []), rd=[B_K["b1"], B_V["w"]]) for kt in range(4)]
                    win_kts.append(dict(nk=8, arr=KT[:, 1, :], col=512, VA=wvA, vt=4, masks=[("Cs",)], rd=[B_K["b1"], B_V["w"]]))
                    qtile(NT + b, 8, T + 8 * b, b, sel_kts, win_kts, 511, T + 8 * b)
                    S.barrier()

        def phase_diff(l):
            lam_init = 0.8 - 0.6 * math.exp(-0.3 * l)
            with ExitStack() as es:
                dkT = sbt(es, "dkT", [128, 2, 8208], BF16)
                dvA = sbt(es, "dvA", [128, 65, 4, 65], BF16)
                B_K = Buf("dkT")
                B_V = Buf("dvA")
                S.op("pool", lambda e: e.memset(dvA[:, :, :, 64:65], 1.0), writes=[B_V])
                CmD, B_CmD = cload(es, "CmD", [128, 4, 4, 128])
                CsN, B_CsN = cload(es, "CsN", [8, 4, 8])
                cosD, B_cos = cload(es, "cosD", [128, NT + 1, 16], dt=F32)
                sinD, B_sin = cload(es, "sinD", [128, NT + 1, 16], dt=F32)
                wdq, bwdq = load_w_cols(es, "wdq", l, C_DQ, 256)
                lin = sbt(es, "lin", [128, 4, 32], F32)
                lt = sbt(es, "lt", [128, 2, 32], F32)
                lam = sbt(es, "lam", [128, 8], F32)
                sgl = sbt(es, "sgl", [128, 64], F32)
                B_l = Buf("lam")
                for i, nm in enumerate(("diff_lq1", "diff_lk1", "diff_lq2", "diff_lk2")):
                    S.dma("sp", lin[:, i, :], P.din[nm][l:l + 1, :].broadcast_to([128, 32]), writes=[B_l])
                S.dma("sp", sgl[:, :], P.din["diff_subln_g"][l:l + 1, :].broadcast_to([128, 64]), writes=[B_l])
                for j in range(2):
                    S.op("dve", lambda e: e.tensor_tensor(out=lt[:, j, :], in0=lin[:, 2 * j, :], in1=lin[:, 2 * j + 1, :], op=ALU.mult), reads=[B_l], writes=[B_l])
                    S.op("dve", lambda e: e.tensor_reduce(out=lam[:, j:j + 1], in_=lt[:, j, :], axis=mybir.AxisListType.X, op=ALU.add), reads=[B_l], writes=[B_l])
                S.op("act", lambda e: e.activation(out=lam[:, 2:4], in_=lam[:, 0:2], func=AF.Exp), reads=[B_l], writes=[B_l])
                S.op("dve", lambda e: e.tensor_tensor(out=lam[:, 4:5], in0=lam[:, 2:3], in1=lam[:, 3:4], op=ALU.subtract), reads=[B_l], writes=[B_l])
                S.op("dve", lambda e: e.tensor_scalar(out=lam[:, 5:6], in0=lam[:, 4:5], scalar1=lam_init, scalar2=-1.0, op0=ALU.add, op1=ALU.mult), reads=[B_l], writes=[B_l])
                S.op("dve", lambda e: e.tensor_scalar(out=sgl[:, :], in0=sgl[:, :], scalar1=1.0 - lam_init, scalar2=None, op0=ALU.mult), reads=[B_l], writes=[B_l])
                kin = Rot(es, nc, "dkin", [128, 256], F32, 10)
                xtr = Rot(es, nc, "xtd", [128, 8, 128], BF16, 2)
                zq = Rot(es, nc, "zdq", [128, 256], F32, 2)
                qrr = Rot(es, nc, "dqr", [128, 256], F32, 2)
                rtmp = Rot(es, nc, "dtmp", [128, 512], F32, 2)
                qds = Rot(es, nc, "qd", [128, 2, 256], BF16, 2)
                QTp = Rot(es, nc, "QTp", [128, 2, 2, 4, 128], BF16, 2)
                QTs = Rot(es, nc, "QTs", [128, 2, 2, 1, 8], BF16, 2)
                Qbs = Rot(es, nc, "Qbs", [128, 2, 4, 8], BF16, 2)
                odr = Rot(es, nc, "od", [128, 4, 256], F32, 2)
                odb = Rot(es, nc, "odb", [128, 256], BF16, 2)
                ost = Rot(es, nc, "dost", [128, 2, 128], BF16, 2)
                sml = Rot(es, nc, "dsml", [128, 8], F32, 8)
                junk = Rot(es, nc, "djunk", [128, 64], F32, 2)
                attn.pts = Rot(es, nc, "ptd", [128, 512], BF16, 4)

                def load_k(ntiles, src_fn, rows_fn, rd_fn, col_fn, gather_idx=None):
                    for kt in range(ntiles):
                        rows = rows_fn(kt)
                        kt_, bk_ = kin.next()
                        if gather_idx is None:
                            S.dma("sp", kt_[:rows, :], src_fn(kt), reads=rd_fn(kt), writes=[bk_])
                        else:
                            S.gather(kt_[:rows, :], src_fn(kt), gather_idx(kt), reads=rd_fn(kt), writes=[bk_])
                        ps, bp = PS.next()
                        for hb in range(2):
                            S.op("pe", lambda e: e.transpose(ps[:, hb * 128:hb * 128 + rows], kt_[:rows, hb * 128:(hb + 1) * 128], ident_f[:rows, :rows]),
                                 reads=[bk_, B_id], writes=[bp], inc=(hb == 1))
                        c_ = col_fn(kt)
                        evac(kt, dkT[:, :, c_:c_ + rows], ps[:, 0:256].rearrange("p (h c) -> p h c", h=2)[:, :, :rows], [bp], [B_K])

                def load_v(ntiles, src_fn, rows_fn, rd_fn, vt_fn, gather_idx=None):
                    for kt in range(ntiles):
                        rows = rows_fn(kt)
                        kt_, bk_ = kin.next()
                        if gather_idx is None:
                            S.dma("sp", kt_[:rows, :], src_fn(kt), reads=rd_fn(kt), writes=[bk_])
                        else:
                            S.gather(kt_[:rows, :], src_fn(kt), gather_idx(kt), reads=rd_fn(kt), writes=[bk_])
                        S.op("dve" if kt % 2 == 0 else "act",
                             (lambda e: e.tensor_copy(out=dvA[:rows, vt_fn(kt), :, 0:64], in_=kt_[:rows, :].rearrange("p (h d) -> p h d", h=4))) if kt % 2 == 0 else
                             (lambda e: e.copy(out=dvA[:rows, vt_fn(kt), :, 0:64], in_=kt_[:rows, :].rearrange("p (h d) -> p h d", h=4))),
                             reads=[bk_], writes=[B_V])

                def qchunk(tiles, rows, samp, kts_desc, dst_cols):
                    nj = len(tiles)
                    QT, bQT = (QTs.next() if samp else QTp.next())
                    for j, (n, c0) in enumerate(tiles):
                        ti = NT if samp else n
                        xt, bx = xtr.next()
                        S.dma("sp", xt[:, :, :rows], xT_d[:, :, c0:c0 + rows], reads=[B_xT[min(n, NT)]], writes=[bx])
                        ps, bp = PS.next()
                        for kc in range(8):
                            S.op("pe", lambda e: e.matmul(ps[:rows, :256], lhsT=xt[:, kc, :rows], rhs=wdq[:, kc, :], start=(kc == 0), stop=(kc == 7)),
                                 reads=[bx, bwdq], writes=[bp], inc=(kc == 7))
                        z, bz = zq.next()
                        S.op("act", lambda e: e.copy(out=z[:rows, :], in_=ps[:rows, :256]), reads=[bp], writes=[bz])
                        qr, bqr = qrr.next()
                        tt, bt = rtmp.next()
                        zin = z[:rows, :].rearrange("p (h s d) -> p h s d", h=8, s=2)
                        qo = qr[:rows, :].rearrange("p (h s d) -> p h s d", h=8, s=2)
                        cb = cosD[:rows, ti, :].unsqueeze(1).broadcast_to([rows, 8, 16])
                        sb_ = sinD[:rows, ti, :].unsqueeze(1).broadcast_to([rows, 8, 16])
                        t4 = [tt[:rows, k * 128:(k + 1) * 128].rearrange("p (h d) -> p h d", h=8) for k in range(4)]
                        rope6(rows, zin[:, :, 0, :], zin[:, :, 1, :], qo[:, :, 0, :], qo[:, :, 1, :], cb, sb_, t4, [bz, B_cos, B_sin], bt, bqr)
                        qd, bqd = qds.next()
                        S.op("pool", lambda e: e.memset(qd[:rows, :, :], 0.0), writes=[bqd])
                        for i in range(2):
                            S.op("pool", lambda e: e.tensor_copy(out=qd[:rows, i, :].rearrange("p (h s d) -> p h s d", h=4, s=2)[:, :, i, :],
                                                                 in_=qr[:rows, :].rearrange("p (h s d) -> p h s d", h=4, s=2)[:, :, i, :]), reads=[bqr], writes=[bqd])
                        ps, bp = PS.next()
                        psb = ps[:].bitcast(BF16)
                        for i in range(2):
                            for hb in range(2):
                                k_ = i * 2 + hb
                                S.op("pe", lambda e: e.transpose(psb[:, k_ * 128:k_ * 128 + rows], qd[:rows, i, hb * 128:(hb + 1) * 128], ident_b[:rows, :rows]),
                                     reads=[bqd, B_id], writes=[bp], inc=(k_ == 3))
                        evac(j, QT[:, :, :, j, :rows], psb[:, 0:512].rearrange("p (i h c) -> p i h c", i=2, h=2)[:, :, :, :rows], [bp], [bQT])
                    od, bod = odr.next()
                    ncols = nj * rows
                    if samp:
                        Qb, bQb = Qbs.next()
                        S.op("pool", lambda e: e.memset(Qb[:, :, :, :], 0.0), writes=[bQb])
                        for hb in range(2):
                            for hh in range(2):
                                for i in range(2):
                                    S.op("dve" if i == 0 else "pool",
                                         lambda e: e.tensor_copy(out=Qb[64 * hh:64 * hh + 64, hb, hh * 2 + i, :], in_=QT[64 * hh:64 * hh + 64, i, hb, 0, :rows]),
                                         reads=[bQT], writes=[bQb])
                    for hb in range(2):
                        taken, ks = PS.take(1 if samp else 4)
                        units = []
                        if samp:
                            units.append(dict(Q=Qb[:, hb, :, :].rearrange("p u c -> p (u c)"), ncols=4 * rows, scale=32 ** -0.5, nblk=4, acc=taken[0],
                                              rdQ=[bQb, B_CmD, B_CsN, B_id]))
                        else:
                            for hh in range(2):
                                for i in range(2):
                                    Q = QT[64 * hh:64 * hh + 64, i, hb, :, :rows].rearrange("p j c -> p (j c)")
                                    units.append(dict(Q=Q, ncols=ncols, scale=32 ** -0.5, nblk=nj, acc=taken[hh * 2 + i], rdQ=[bQT, B_CmD, B_CsN, B_id]))
                        kts = []
                        for kd in kts_desc:
                            nk, c_ = kd["nk"], kd["col"]
                            K, V, masks = [], [], []
                            if samp:
                                K.append(dkT[:, hb, c_:c_ + nk])
                                V.append([dvA[:nk, kd["vt"], hb * 2 + hh, :] for hh in range(2) for i in range(2)])
                                if kd["mask"] is None:
                                    masks.append([])
                                else:
                                    masks.append([(ident_b[:nk, :nk], CsN[:nk, :, :].rearrange("p q c -> p (q c)"))])
                            else:
                                for hh in range(2):
                                    for i in range(2):
                                        K.append(dkT[64 * hh:64 * hh + 64, hb, c_:c_ + nk])
                                        V.append(dvA[:nk, kd["vt"], hb * 2 + hh, :])
                                        if kd["mask"] is None:
                                            masks.append([])
                                        else:
                                            masks.append([(ident_b[:nk, :nk], CmD[:nk, kd["mask"][1], :, :].rearrange("p j c -> p (j c)"))])
                            kts.append(dict(nk=nk, K=K, V=V, masks=masks, rd=[B_K, B_V]))
                        attn(units, kts, rows)
                        for hh in range(2):
                            h = hb * 2 + hh
                            if samp:
                                a0, b0 = taken[0]
                                a1, b1 = taken[0]
                                o0, o1 = (hh * 2) * 65, (hh * 2 + 1) * 65
                            else:
                                a0, b0 = taken[hh * 2]
                                a1, b1 = taken[hh * 2 + 1]
                                o0, o1 = 0, 0
                            for j in range(nj):
                                c0_, c1_ = o0 + j * 65, o1 + j * 65
                                rs, brs = sml.next()
                                S.op("dve", lambda e: e.tensor_scalar(out=rs[:rows, 0:1], in0=a0[:rows, c0_ + 64:c0_ + 65], scalar1=1e-30, scalar2=None, op0=ALU.max), reads=[b0], writes=[brs])
                                S.op("dve", lambda e: e.tensor_scalar(out=rs[:rows, 1:2], in0=a1[:rows, c1_ + 64:c1_ + 65], scalar1=1e-30, scalar2=None, op0=ALU.max), reads=[b1], writes=[brs])
                                S.op("dve", lambda e: e.reciprocal(out=rs[:rows, 2:4], in_=rs[:rows, 0:2]), reads=[brs], writes=[brs])
                                S.op("dve", lambda e: e.tensor_tensor(out=rs[:rows, 3:4], in0=rs[:rows, 3:4], in1=lam[:rows, 5:6], op=ALU.mult), reads=[brs, B_l], writes=[brs])
                                o_ = od[:rows, j, h * 64:(h + 1) * 64]
                                S.op("dve", lambda e: e.tensor_scalar(out=o_, in0=a0[:rows, c0_:c0_ + 64], scalar1=rs[:rows, 2:3], scalar2=None, op0=ALU.mult), reads=[b0, brs], writes=[bod])
                                S.op("dve", lambda e: e.scalar_tensor_tensor(out=o_, in0=a1[:rows, c1_:c1_ + 64], scalar=rs[:rows, 3:4], in1=o_, op0=ALU.mult, op1=ALU.add),
                                     reads=[b1, brs, bod], writes=[bod])
                                jk, bjk = junk.next()
                                S.op("act", lambda e: e.activation(out=jk[:rows, :], in_=o_, func=AF.Square, accum_out=rs[:rows, 4:5]), reads=[bod], writes=[bjk, brs])
                                S.op("dve", lambda e: e.tensor_scalar(out=rs[:rows, 5:6], in0=rs[:rows, 4:5], scalar1=1.0 / 64.0, scalar2=EPS, op0=ALU.mult, op1=ALU.add), reads=[brs], writes=[brs])
                                S.op("act", lambda e: e.activation(out=rs[:rows, 6:7], in_=rs[:rows, 5:6], func=AF.Sqrt), reads=[brs], writes=[brs])
                                S.op("dve", lambda e: e.reciprocal(out=rs[:rows, 7:8], in_=rs[:rows, 6:7]), reads=[brs], writes=[brs])
                                S.op("dve", lambda e: e.scalar_tensor_tensor(out=o_, in0=o_, scalar=rs[:rows, 7:8], in1=sgl[:rows, :], op0=ALU.mult, op1=ALU.mult),
                                     reads=[bod, brs, B_l], writes=[bod])
                        PS.give(ks)
                    for j in range(nj):
                        ob, bob = odb.next()
                        S.op("pool", lambda e: e.tensor_copy(out=ob[:rows, :], in_=od[:rows, j, :]), reads=[bod], writes=[bob])
                        ps, bp = PS.next()
                        psb = ps[:].bitcast(BF16)
                        for k_ in range(2):
                            S.op("pe", lambda e: e.transpose(psb[:, k_ * 128:k_ * 128 + rows], ob[:rows, k_ * 128:(k_ + 1) * 128], ident_b[:rows, :rows]),
                                 reads=[bob, B_id], writes=[bp], inc=(k_ == 1))
                        os_, bos = ost.next()
                        evac(j, os_[:, :, :rows], psb[:, 0:256].rearrange("p (k c) -> p k c", k=2)[:, :, :rows], [bp], [bos])
                        S.dma("sp", mixT_d[:, 6:8, dst_cols[j]:dst_cols[j] + rows], os_[:, :, :rows], reads=[bos], writes=[B_mix[2]])

                full = lambda kt: 128
                load_k(NT, lambda kt: o_p["diff_k"][l, kt * 128:(kt + 1) * 128, :], full, lambda kt: [B_kvd[kt]], lambda kt: kt * 128)
                load_v(NT, lambda kt: o_p["diff_v"][l, kt * 128:(kt + 1) * 128, :], full, lambda kt: [B_kvd[kt]], lambda kt: kt)
                for qc in range(NT // 4):
                    kts_desc = [dict(nk=128, col=kt * 128, vt=kt, mask=(("D", kt - 4 * qc) if kt >= 4 * qc else None)) for kt in range(4 * qc + 4)]
                    qchunk([(4 * qc + j, (4 * qc + j) * 128) for j in range(4)], 128, False, kts_desc, [(4 * qc + j) * 128 for j in range(4)])
                S.barrier()
                ptb = sbt(es, "dptb", [128, NSB, NPG], I32)
                idx = sbt(es, "didx", [128, NSB, NPG], I32)
                iot = sbt(es, "diot", [128, 1], I32)
                B_idx = Buf("didx")
                S.dma("sp", ptb[:].rearrange("p b j -> p (b j)"), pt.rearrange("b j -> (b j)").unsqueeze(0).broadcast_to([128, NSB * NPG]), writes=[B_idx])
                S.op("pool", lambda e: e.iota(iot[:], pattern=[[0, 1]], base=l * NPOOL * 128, channel_multiplier=1), writes=[B_idx])
                S.op("dve", lambda e: e.tensor_scalar(out=idx[:].rearrange("p b j -> p (b j)"), in0=ptb[:].rearrange("p b j -> p (b j)"), scalar1=128.0,
                                                      scalar2=iot[:, 0:1], op0=ALU.mult, op1=ALU.add), reads=[B_idx], writes=[B_idx])
                for b in range(NSB):
                    gi = lambda kt, b=b: idx[:, b, kt:kt + 1]
                    load_k(NPG, lambda kt: c_diff_k, full, lambda kt: [B_idx], lambda kt: kt * 128, gather_idx=gi)
                    load_k(1, lambda kt: o_s["diff_k"][l, b], lambda kt: 8, lambda kt: [B_kvd[NT + b]], lambda kt: PAST)
                    load_v(NPG, lambda kt: c_diff_v, full, lambda kt: [B_idx], lambda kt: kt, gather_idx=gi)
                    load_v(1, lambda kt: o_s["diff_v"][l, b], lambda kt: 8, lambda kt: [B_kvd[NT + b]], lambda kt: 64)
                    kts_desc = [dict(nk=128, col=kt * 128, vt=kt, mask=None) for kt in range(NPG)]
                    kts_desc.append(dict(nk=8, col=PAST, vt=64, mask=("S",)))
                    qchunk([(NT + b, T + 8 * b)], 8, True, kts_desc, [T + 8 * b])
                    S.barrier()

        def phase_c(l):
            with ExitStack() as es:
                wo = sbt(es, "wo", [128, 8, D], BF16)
                W1 = sbt(es, "fW1", [128, 8, DFF], BF16)
                W3 = sbt(es, "fW3", [128, 8, DFF], BF16)
                W2 = sbt(es, "fW2", [128, 22, D], BF16)
                B_w = Buf("fw")
                src = P.din["w_out"][l].rearrange("(kc p) c -> p kc c", p=128)
                for kc in range(8):
                    S.dma("pool", wo[:, kc, :], src[:, kc, :], writes=[B_w])
                for wt, nm in ((W1, "ffn_w1"), (W3, "ffn_w3")):
                    src = P.din[nm][l].rearrange("(kc p) c -> p kc c", p=128)
                    for kc in range(8):
                        for hf in range(2):
                            S.dma("pool", wt[:, kc, hf * 1408:(hf + 1) * 1408], src[:, kc, hf * 1408:(hf + 1) * 1408], writes=[B_w])
                src = P.din["ffn_w2"][l].rearrange("(fc p) c -> p fc c", p=128)
                for fc in range(22):
                    S.dma("pool", W2[:, fc, :], src[:, fc, :], writes=[B_w])
                lnp = sbt(es, "lnp", [128, 4, D], F32)
                B_ln = Buf("lnp")
                for i, nm in enumerate(("ln1_g", "ln1_b", "ln2_g", "ln2_b")):
                    S.dma("sp", lnp[:, i, :], P.din[nm][l:l + 1, :].broadcast_to([128, D]), writes=[B_ln])
                mxr = Rot(es, nc, "mx", [128, 8, 128], BF16, 2)
                xin = Rot(es, nc, "cx", [128, D], F32, 2)
                xar = Rot(es, nc, "cxa", [128, D], F32, 2)
                xnr = Rot(es, nc, "cxn", [128, D], F32, 2)
                xnT = Rot(es, nc, "cxnT", [128, 8, 128], BF16, 1)
                hTr = Rot(es, nc, "chT", [128, 22, 128], BF16, 1)
                sgr = Rot(es, nc, "csg", [128, 512], F32, 2)
                str_ = Rot(es, nc, "cstat", [128, 24], F32, 4)

                def layer_norm(rows, src_, bsrc, dst, bdst, gi):
                    st, bst = str_.next()
                    for c in range(2):
                        S.op("dve", lambda e: e.bn_stats(out=st[:rows, c * 6:(c + 1) * 6], in_=src_[:rows, c * 512:(c + 1) * 512]), reads=[bsrc], writes=[bst])
                    S.op("dve", lambda e: e.bn_aggr(out=st[:rows, 12:14], in_=st[:rows, 0:12]), reads=[bst], writes=[bst])
                    S.op("dve", lambda e: e.tensor_scalar(out=st[:rows, 14:15], in0=st[:rows, 13:14], scalar1=EPS, scalar2=None, op0=ALU.add), reads=[bst], writes=[bst])
                    S.op("act", lambda e: e.activation(out=st[:rows, 15:16], in_=st[:rows, 14:15], func=AF.Sqrt), reads=[bst], writes=[bst])
                    S.op("dve", lambda e: e.reciprocal(out=st[:rows, 16:17], in_=st[:rows, 15:16]), reads=[bst], writes=[bst])
                    S.op("dve", lambda e: e.tensor_scalar(out=dst[:rows, :], in0=src_[:rows, :], scalar1=st[:rows, 12:13], scalar2=st[:rows, 16:17], op0=ALU.subtract, op1=ALU.mult),
                         reads=[bsrc, bst], writes=[bdst])
                    S.op("pool", lambda e: e.tensor_tensor(out=dst[:rows, :], in0=dst[:rows, :], in1=lnp[:rows, gi, :], op=ALU.mult), reads=[bdst, B_ln], writes=[bdst])
                    S.op("pool", lambda e: e.tensor_tensor(out=dst[:rows, :], in0=dst[:rows, :], in1=lnp[:rows, gi + 1, :], op=ALU.add), reads=[bdst, B_ln], writes=[bdst])

                for n in range(NT + 1):
                    rows = 128 if n < NT else NS
                    t0 = n * 128
                    mx, bmx = mxr.next()
                    S.dma("sp", mx[:, :, :rows], mixT_d[:, :, t0:t0 + rows], reads=[B_mix[0], B_mix[1], B_mix[2]], writes=[bmx])
                    x_, bx = xin.next()
                    if l == 0:
                        S.dma("sp", x_[:rows, :], (xp[t0:t0 + rows, :] if n < NT else xs), writes=[bx])
                    else:
                        S.dma("sp", x_[:rows, :], xmid[t0:t0 + rows, :], reads=[B_xmid[n]], writes=[bx])
                    xa, bxa = xar.next()
                    for hf in range(2):
                        ps, bp = PS.next()
                        for kc in range(8):
                            S.op("pe", lambda e: e.matmul(ps[:rows, :], lhsT=mx[:, kc, :rows], rhs=wo[:, kc, hf * 512:(hf + 1) * 512], start=(kc == 0), stop=(kc == 7)),
                                 reads=[bmx, B_w], writes=[bp], inc=(kc == 7))
                        S.op("dve", lambda e: e.scalar_tensor_tensor(out=xa[:rows, hf * 512:(hf + 1) * 512], in0=x_[:rows, hf * 512:(hf + 1) * 512], scalar=DN_ALPHA,
                                                                     in1=ps[:rows, :], op0=ALU.mult, op1=ALU.add), reads=[bx, bp], writes=[bxa])
                    xn, bxn = xnr.next()
                    layer_norm(rows, xa, bxa, xn, bxn, 0)
                    xT_, bxT = xnT.next()
                    for hf in range(2):
                        ps, bp = PS.next()
                        for j in range(4):
                            kc = hf * 4 + j
                            S.op("pe", lambda e: e.transpose(ps[:, j * 128:j * 128 + rows], xn[:rows, kc * 128:(kc + 1) * 128], ident_f[:rows, :rows]),
                                 reads=[bxn, B_id], writes=[bp], inc=(j == 3))
                        evac(hf, xT_[:, hf * 4:hf * 4 + 4, :rows], ps[:].rearrange("p (j c) -> p j c", j=4)[:, :, :rows], [bp], [bxT])
                    hT, bhT = hTr.next()
                    for f0 in range(0, 22, 4):
                        nf = min(4, 22 - f0)
                        ps1, bp1 = PS.next()
                        ps3, bp3 = PS.next()
                        for (ps_, bp_, wt) in ((ps1, bp1, W1), (ps3, bp3, W3)):
                            for f in range(nf):
                                for kc in range(8):
                                    S.op("pe", lambda e: e.matmul(ps_[:, f * 128:f * 128 + rows], lhsT=wt[:, kc, (f0 + f) * 128:(f0 + f + 1) * 128], rhs=xT_[:, kc, :rows],
                                                                  start=(kc == 0), stop=(kc == 7)), reads=[bxT, B_w], writes=[bp_], inc=(kc == 7 and f == nf - 1))
                        sg, bsg = sgr.next()
                        v1 = ps1[:, 0:nf * 128].rearrange("p (f c) -> p f c", f=nf)[:, :, :rows]
                        v3 = ps3[:, 0:nf * 128].rearrange("p (f c) -> p f c", f=nf)[:, :, :rows]
                        sv = sg[:, 0:nf * 128].rearrange("p (f c) -> p f c", f=nf)[:, :, :rows]
                        S.op("act", lambda e: e.activation(out=sv, in_=v1, func=AF.Silu), reads=[bp1], writes=[bsg])
                        S.op("dve", lambda e: e.tensor_tensor(out=hT[:, f0:f0 + nf, :rows], in0=sv, in1=v3, op=ALU.mult), reads=[bsg, bp3], writes=[bhT])
                    xb, bxb = xar.next()
                    for hf in range(2):
                        ps, bp = PS.next()
                        for fc in range(22):
                            S.op("pe", lambda e: e.matmul(ps[:rows, :], lhsT=hT[:, fc, :rows], rhs=W2[:, fc, hf * 512:(hf + 1) * 512], start=(fc == 0), stop=(fc == 21)),
                                 reads=[bhT, B_w], writes=[bp], inc=(fc == 21))
                        S.op("dve", lambda e: e.scalar_tensor_tensor(out=xb[:rows, hf * 512:(hf + 1) * 512], in0=xn[:rows, hf * 512:(hf + 1) * 512], scalar=DN_ALPHA,
                                                                     in1=ps[:rows, :], op0=ALU.mult, op1=ALU.add), reads=[bxn, bp], writes=[bxb])
                    y_, by = xin.next()
                    layer_norm(rows, xb, bxb, y_, by, 2)
                    if l == DEPTH - 1:
                        P.store((y_p[t0:t0 + rows, :] if n < NT else y_s), y_[:rows, :], [by])
                    else:
                        S.dma("sp", xmid[t0:t0 + rows, :], y_[:rows, :], reads=[by], writes=[B_xmid[n]])

        for l in range(DEPTH):
            phase_xT(l)
            S.barrier()
            phase_kv_outputs(l)
            S.barrier()
            phase_conv(l)
            S.barrier()
            phase_nsa(l)
            S.barrier()
            phase_diff(l)
            S.barrier()
            phase_c(l)
            S.barrier()
        S.barrier()
    S.close()
    return P


_PROG = None


def kernel(**inputs):
    global _PROG
    if _PROG is None:
        _PROG = build_program()
    P = _PROG
    f32 = np.float32
    consts = _consts()
    g = lambda k: np.asarray(inputs[k])
    cnames = (("c_cmp_k", "cache_nsa_cmp_k", 128), ("c_cmp_v", "cache_nsa_cmp_v", 128), ("c_sel_k", "cache_nsa_sel_k", 128),
              ("c_sel_v", "cache_nsa_sel_v", 128), ("c_diff_k", "cache_diff_k", 256), ("c_diff_v", "cache_diff_v", 256))
    shared = {}
    if not KDEV:
        for dn, sn, w_ in cnames:
            shared[dn] = g(sn).reshape(DEPTH * NPOOL * 128, w_)
    for nm in P.din:
        if nm in inputs and nm not in shared:
            shared[nm] = g(nm)
    shared.update(consts)
    in_maps = []
    for c in range(8):
        m = dict(shared)
        m["xp"] = g("x_prompt")[c % 4]
        sl = slice(4 * c, 4 * c + 4)
        m["xs"] = g("x_sample")[sl].reshape(NS, D)
        m["st_win_k"] = np.ascontiguousarray(g("state_nsa_win_k")[:, sl].reshape(DEPTH, NSB, 512, 128))
        m["st_win_v"] = np.ascontiguousarray(g("state_nsa_win_v")[:, sl].reshape(DEPTH, NSB, 512, 128))
        m["st_conv"] = np.ascontiguousarray(g("state_conv")[:, sl])
        ptc = np.ascontiguousarray(g("page_table")[sl]).astype(np.int32)
        if KDEV:
            flat = ptc.reshape(-1)
            for dn, sn, w_ in cnames:
                m[dn] = np.ascontiguousarray(g(sn)[:, flat]).reshape(DEPTH * NPOOL * 128, w_)
            ptc = np.arange(NSB * NPG, dtype=np.int32).reshape(NSB, NPG)
        m["pt"] = ptc
        in_maps.append({k: m[k] for k in P.din})
    res = run_bass_kernel_spmd(P.nc, in_maps, core_ids=list(range(8))).results

    def pgather(name, shape):
        return np.stack([res[c][name] for c in range(4)], axis=1).reshape(shape)

    def sgather(name, shape):
        return np.concatenate([res[c][name] for c in range(8)], axis=1).reshape(shape)

    y_prompt = np.stack([res[c]["y_p"] for c in range(4)], axis=0)
    y_sample = np.concatenate([res[c]["y_s"].reshape(NSB, 8, D) for c in range(8)], axis=0)
    outs = [y_prompt, y_sample]
    for nm, shp in (("cmp_k", (DEPTH, 4, T, 2, 64)), ("cmp_v", (DEPTH, 4, T, 2, 64)), ("sel_k", (DEPTH, 4, T, 2, 64)),
                    ("sel_v", (DEPTH, 4, T, 2, 64)), ("diff_k", (DEPTH, 4, T, 4, 64)), ("diff_v", (DEPTH, 4, T, 4, 64)),
                    ("win_k", (DEPTH, 4, 512, 2, 64)), ("win_v", (DEPTH, 4, 512, 2, 64)), ("conv", (DEPTH, 4, 30, 256))):
        outs.append(pgather("p_" + nm, shp))
    for nm, shp in (("cmp_k", (DEPTH, 32, 8, 2, 64)), ("cmp_v", (DEPTH, 32, 8, 2, 64)), ("sel_k", (DEPTH, 32, 8, 2, 64)),
                    ("sel_v", (DEPTH, 32, 8, 2, 64)), ("diff_k", (DEPTH, 32, 8, 4, 64)), ("diff_v", (DEPTH, 32, 8, 4, 64)),
                    ("win_k", (DEPTH, 32, 512, 2, 64)), ("win_v", (DEPTH, 32, 512, 2, 64)), ("conv", (DEPTH, 32, 30, 256))):
        outs.append(sgather("s_" + nm, shp))
    return tuple(np.ascontiguousarray(o, dtype=f32) for o in outs)
```
